# Optimizing a Trainium2 kernel written in Bass

```python
import math
import jax
import jax.numpy as jnp
from jax import lax
import numpy as np

D_MODEL = 4096
BATCH = 2
SEQ = 8192
DEPTH = 4

POOL_WINDOWS = (2, 4, 8, 16)
POOL_GROUPS = 4
POOL_WIDTH = 3 * D_MODEL // 8
POOL_GROUP_DIM = POOL_WIDTH // POOL_GROUPS
SSD_HEAD_DIM = 64
SSD_INNER = D_MODEL // 2
SSD_HEADS = SSD_INNER // SSD_HEAD_DIM
SSD_GROUPS = 4
SSD_STATE = 128
SSD_CONV = 5
SSD_CHUNK = 128
SSD_BC = SSD_GROUPS * SSD_STATE
SSD_XBC = SSD_INNER + 2 * SSD_BC
DILATED_PATTERNS = ((128, 1), (512, 4), (2048, 16))
ATTN_GROUPS = 3
ATTN_HEADS = D_MODEL // 1024
ATTN_HEAD_DIM = 128
ATTN_OUT = ATTN_HEADS * ATTN_HEAD_DIM
ATTN_QKV = 3 * ATTN_GROUPS * ATTN_OUT
T5_BUCKETS = 32
T5_MAX_DISTANCE = 1024
N_BRANCHES = 3
GATE_RANK = D_MODEL // 8
MLP_HIDDEN = 2 * D_MODEL
MLP_CONV = 3
N_MOD = 6
EPS = 1e-6
NEG_INF = -1e30
IN_SPLITS = (POOL_WIDTH, SSD_INNER, SSD_XBC, 2 * SSD_HEADS, ATTN_QKV, GATE_RANK)
N_IN = POOL_WIDTH + SSD_INNER + SSD_XBC + 2 * SSD_HEADS + ATTN_QKV + GATE_RANK

kernel_name = 'hybrid_pool_ssd_dilated_attn_encoder'


def _rmsnorm(t, g):
    t32 = t.astype(jnp.float32)
    t32 = t32 * lax.rsqrt(jnp.mean(t32 * t32, axis=-1, keepdims=True) + EPS)
    return t32.astype(t.dtype) * g


def _dwconv(u, w, bias):
    width = w.shape[0]
    out = lax.conv_general_dilated(u, w[:, None, :], window_strides=(1,), padding=[(width // 2, width // 2)],
                                   dimension_numbers=('NWC', 'WIO', 'NWC'), feature_group_count=u.shape[-1])
    return out + bias


def _pool_mixer(u, pool_w, pool_scale):
    b, l, _ = u.shape
    u32 = u.astype(jnp.float32)
    csum = jnp.concatenate([jnp.zeros((b, 1, POOL_WIDTH), jnp.float32), jnp.cumsum(u32, axis=1)], axis=1)
    pos = np.arange(l)
    groups = []
    for gi, win in enumerate(POOL_WINDOWS):
        lo = np.clip(pos - win // 2, 0, l)
        hi = np.clip(pos + win - win // 2, 0, l)
        sl = slice(gi * POOL_GROUP_DIM, (gi + 1) * POOL_GROUP_DIM)
        cnt = jnp.asarray((hi - lo).astype(np.float32))[None, :, None]
        mean = (csum[:, hi, sl] - csum[:, lo, sl]) / cnt
        groups.append(mean - u32[:, :, sl])
    p = jnp.stack(groups, axis=2).astype(u.dtype)
    y = jnp.einsum('blgc,gcd->blgd', p, pool_w).reshape(b, l, POOL_WIDTH)
    return y * pool_scale


def _ssd_scan(xh, dt, a, bm, cm):
    b, l, nh, hp = xh.shape
    ng, ns = bm.shape[2], bm.shape[3]
    r = nh // ng
    tc = SSD_CHUNK
    nc = l // tc
    xdt = (xh.astype(jnp.float32) * dt[..., None]).reshape(b, nc, tc, ng, r, hp)
    a_cs = jnp.cumsum((dt * a).reshape(b, nc, tc, ng, r).transpose(0, 3, 4, 1, 2), axis=-1)
    bc = bm.astype(jnp.float32).reshape(b, nc, tc, ng, ns)
    cc = cm.astype(jnp.float32).reshape(b, nc, tc, ng, ns)
    lower = np.tril(np.ones((tc, tc), dtype=bool))
    decay = jnp.exp(jnp.where(lower, a_cs[..., :, None] - a_cs[..., None, :], -jnp.inf))
    cb = jnp.einsum('bctgn,bcsgn->bgcts', cc, bc)
    y_diag = jnp.einsum('bgrcts,bcsgrp->bctgrp', cb[:, :, None] * decay, xdt)
    to_end = jnp.exp(a_cs[..., -1:] - a_cs).transpose(0, 3, 4, 1, 2)
    chunk_states = jnp.einsum('bcsgn,bcsgrp->cbgrpn', bc, xdt * to_end[..., None])
    chunk_decay = jnp.exp(a_cs[..., -1]).transpose(3, 0, 1, 2)

    def step(state, inp):
        st, dec = inp
        return state * dec[..., None, None] + st, state

    init = jnp.zeros((b, ng, r, hp, ns), jnp.float32)
    _, prev = lax.scan(step, init, (chunk_states, chunk_decay))
    from_start = jnp.exp(a_cs).transpose(0, 3, 4, 1, 2)
    y_off = jnp.einsum('bctgn,cbgrpn->bctgrp', cc, prev) * from_start[..., None]
    return (y_diag + y_off).reshape(b, l, nh, hp)


def _ssd_mixer(z, xbc, dt_raw, conv_w, conv_b, dt_bias, a_log, d_skip, norm_g):
    b, l, _ = z.shape
    xbc = jax.nn.silu(_dwconv(xbc, conv_w, conv_b))
    xs, bm, cm = jnp.split(xbc, [SSD_INNER, SSD_INNER + SSD_BC], axis=-1)
    xh = xs.reshape(b, l, SSD_HEADS, SSD_HEAD_DIM)
    bm = bm.reshape(b, l, SSD_GROUPS, SSD_STATE)
    cm = cm.reshape(b, l, SSD_GROUPS, SSD_STATE)
    dt = jax.nn.softplus(dt_raw.astype(jnp.float32).reshape(b, l, 2, SSD_HEADS) + dt_bias.astype(jnp.float32))
    a = -jnp.exp(a_log.astype(jnp.float32))
    y_fwd = _ssd_scan(xh, dt[:, :, 0], a[0], bm, cm)
    flip = lambda t: jnp.flip(t, axis=1)
    y_bwd = flip(_ssd_scan(flip(xh), flip(dt[:, :, 1]), a[1], flip(bm), flip(cm)))
    y = y_fwd + y_bwd + xh.astype(jnp.float32) * d_skip.astype(jnp.float32)[:, None]
    y = y.reshape(b, l, SSD_INNER) * jax.nn.silu(z.astype(jnp.float32))
    yg = y.reshape(b, l, SSD_GROUPS, SSD_INNER // SSD_GROUPS)
    yg = yg * lax.rsqrt(jnp.mean(yg * yg, axis=-1, keepdims=True) + EPS)
    return yg.reshape(b, l, SSD_INNER).astype(z.dtype) * norm_g


def _t5_buckets(rel):
    half = T5_BUCKETS // 2
    max_exact = half // 2
    n = np.abs(rel)
    large = max_exact + (np.log(np.maximum(n, max_exact) / max_exact) / np.log(T5_MAX_DISTANCE / max_exact)
                         * (half - max_exact)).astype(np.int32)
    large = np.minimum(large, half - 1)
    return (rel > 0).astype(np.int32) * half + np.where(n < max_exact, n, large).astype(np.int32)


def _dilated_window_attention(q, k, v, bias_table, window, dil):
    b, l, h, e = q.shape
    radius = window // (2 * dil)
    blk = radius
    m = l // dil
    nb = -(-m // blk)
    pad = nb * blk - m

    def sub(t):
        return t.reshape(b, m, dil, h, e).transpose(0, 2, 1, 3, 4)

    qs = jnp.pad(sub(q), ((0, 0), (0, 0), (0, pad), (0, 0), (0, 0))).reshape(b, dil, nb, blk, h, e)

    def key_windows(t):
        tp = jnp.pad(sub(t), ((0, 0), (0, 0), (blk, pad + blk), (0, 0), (0, 0))).reshape(b, dil, nb + 2, blk, h, e)
        return jnp.concatenate([tp[:, :, :-2], tp[:, :, 1:-1], tp[:, :, 2:]], axis=3)

    ks = key_windows(k)
    vs = key_windows(v)
    delta = np.arange(3 * blk)[None, :] - blk - np.arange(blk)[:, None]
    t_key = (np.arange(nb)[:, None] - 1) * blk + np.arange(3 * blk)[None, :]
    mask = (np.abs(delta) <= radius)[None] & ((t_key >= 0) & (t_key < m))[:, None, :]
    bias = jnp.transpose(bias_table[_t5_buckets(delta * dil)], (2, 0, 1)).astype(jnp.float32)
    logits = jnp.einsum('bdnqhe,bdnkhe->bdnhqk', qs, ks).astype(jnp.float32) * (e ** -0.5) + bias
    logits = jnp.where(mask[None, None, :, None], logits, NEG_INF)
    mx = jnp.max(logits, axis=-1, keepdims=True)
    ex = jnp.exp(logits - mx)
    den = jnp.sum(ex, axis=-1, keepdims=True)
    o = jnp.einsum('bdnhqk,bdnkhe->bdnqhe', (ex / den).astype(v.dtype), vs)
    lse = (mx + jnp.log(den))[..., 0]
    o = o.reshape(b, dil, nb * blk, h, e)[:, :, :m].transpose(0, 2, 1, 3, 4).reshape(b, l, h, e)
    lse = lse.transpose(0, 1, 2, 4, 3).reshape(b, dil, nb * blk, h)[:, :, :m].transpose(0, 2, 1, 3).reshape(b, l, h)
    return o, lse


def _dilated_attention_mixer(qkv, q_norm, k_norm, t5_table):
    b, l, _ = qkv.shape
    qkv = qkv.reshape(b, l, 3, ATTN_GROUPS, ATTN_HEADS, ATTN_HEAD_DIM)
    q = _rmsnorm(qkv[:, :, 0], q_norm)
    k = _rmsnorm(qkv[:, :, 1], k_norm)
    v = qkv[:, :, 2]
    outs, lses = [], []
    for gi, (window, dil) in enumerate(DILATED_PATTERNS):
        table = t5_table[:, gi * ATTN_HEADS:(gi + 1) * ATTN_HEADS]
        o, s = _dilated_window_attention(q[:, :, gi], k[:, :, gi], v[:, :, gi], table, window, dil)
        outs.append(o)
        lses.append(s)
    wts = jax.nn.softmax(jnp.stack(lses, axis=0), axis=0)
    out = jnp.einsum('gblh,gblhe->blhe', wts, jnp.stack(outs, axis=0).astype(jnp.float32))
    return out.reshape(b, l, ATTN_OUT).astype(qkv.dtype)


def _mixer_block(h, w_in, pool_w, pool_scale, ssd_conv_w, ssd_conv_b, ssd_dt_bias, ssd_a_log, ssd_d, ssd_norm,
                 q_norm, k_norm, t5_table, proj_a, proj_b, proj_c, gate_up, gate_b, w_out):
    b, l, _ = h.shape
    offsets = [int(o) for o in np.cumsum(IN_SPLITS)[:-1]]
    a_in, z, xbc, dt_raw, qkv, g_low = jnp.split(h @ w_in, offsets, axis=-1)
    y_a = _pool_mixer(a_in, pool_w, pool_scale)
    y_b = _ssd_mixer(z, xbc, dt_raw, ssd_conv_w, ssd_conv_b, ssd_dt_bias, ssd_a_log, ssd_d, ssd_norm)
    y_c = _dilated_attention_mixer(qkv, q_norm, k_norm, t5_table)
    gates = jax.nn.sigmoid(g_low @ gate_up + gate_b).reshape(b, l, N_BRANCHES, D_MODEL)
    merged = gates[:, :, 0] * (y_a @ proj_a) + gates[:, :, 1] * (y_b @ proj_b) + gates[:, :, 2] * (y_c @ proj_c)
    return merged @ w_out


def _conv_ffn(h, w_up, conv_w, conv_b, w_down):
    u, v = jnp.split(_dwconv(h @ w_up, conv_w, conv_b), 2, axis=-1)
    return (jax.nn.silu(u) * v) @ w_down


def setup_inputs(seed: int = 0) -> dict:
    key = jax.random.key(seed)
    ks = jax.random.split(key, 32)
    f32 = jnp.float32

    def normal(k, shape, scale):
        return jax.random.normal(k, shape, f32) * scale

    def gain(k, shape):
        return 1.0 + normal(k, shape, 0.02)

    dt_init = jnp.exp(jax.random.uniform(ks[13], (DEPTH, 2, SSD_HEADS), f32, math.log(1e-3), math.log(1e-1)))
    return {
        'x': normal(ks[0], (BATCH, SEQ, D_MODEL), 1.0),
        'c': normal(ks[1], (BATCH, D_MODEL), 1.0),
        'ada_w': normal(ks[2], (D_MODEL, N_MOD * D_MODEL), 0.5 * D_MODEL ** -0.5),
        'ada_b': normal(ks[3], (N_MOD * D_MODEL,), 0.01),
        'ada_layer': normal(ks[4], (DEPTH, N_MOD * D_MODEL), 0.1),
        't5_table': normal(ks[5], (T5_BUCKETS, ATTN_GROUPS * ATTN_HEADS), 0.5),
        'norm_mix': gain(ks[6], (DEPTH, D_MODEL)),
        'norm_mlp': gain(ks[7], (DEPTH, D_MODEL)),
        'w_in': normal(ks[8], (DEPTH, D_MODEL, N_IN), D_MODEL ** -0.5),
        'pool_w': normal(ks[9], (DEPTH, POOL_GROUPS, POOL_GROUP_DIM, POOL_GROUP_DIM), POOL_GROUP_DIM ** -0.5),
        'pool_scale': gain(ks[10], (DEPTH, POOL_WIDTH)),
        'ssd_conv_w': normal(ks[11], (DEPTH, SSD_CONV, SSD_XBC), SSD_CONV ** -0.5),
        'ssd_conv_b': normal(ks[12], (DEPTH, SSD_XBC), 0.01),
        'ssd_dt_bias': dt_init + jnp.log(-jnp.expm1(-dt_init)),
        'ssd_a_log': jnp.log(jax.random.uniform(ks[14], (DEPTH, 2, SSD_HEADS), f32, 1.0, 16.0)),
        'ssd_d': 1.0 + normal(ks[15], (DEPTH, SSD_HEADS), 0.1),
        'ssd_norm': gain(ks[16], (DEPTH, SSD_INNER)),
        'q_norm': gain(ks[17], (DEPTH, ATTN_HEAD_DIM)),
        'k_norm': gain(ks[18], (DEPTH, ATTN_HEAD_DIM)),
        'proj_a': normal(ks[19], (DEPTH, POOL_WIDTH, D_MODEL), POOL_WIDTH ** -0.5),
        'proj_b': normal(ks[20], (DEPTH, SSD_INNER, D_MODEL), SSD_INNER ** -0.5),
        'proj_c': normal(ks[21], (DEPTH, ATTN_OUT, D_MODEL), ATTN_OUT ** -0.5),
        'gate_up': normal(ks[22], (DEPTH, GATE_RANK, N_BRANCHES * D_MODEL), GATE_RANK ** -0.5),
        'gate_b': normal(ks[23], (DEPTH, N_BRANCHES * D_MODEL), 0.01),
        'w_out': normal(ks[24], (DEPTH, D_MODEL, D_MODEL), D_MODEL ** -0.5),
        'mlp_up': normal(ks[25], (DEPTH, D_MODEL, 2 * MLP_HIDDEN), D_MODEL ** -0.5),
        'mlp_conv_w': normal(ks[26], (DEPTH, MLP_CONV, 2 * MLP_HIDDEN), MLP_CONV ** -0.5),
        'mlp_conv_b': normal(ks[27], (DEPTH, 2 * MLP_HIDDEN), 0.01),
        'mlp_down': normal(ks[28], (DEPTH, MLP_HIDDEN, D_MODEL), MLP_HIDDEN ** -0.5),
    }


def reference(x, c, ada_w, ada_b, ada_layer, t5_table, norm_mix, norm_mlp, w_in, pool_w, pool_scale,
              ssd_conv_w, ssd_conv_b, ssd_dt_bias, ssd_a_log, ssd_d, ssd_norm, q_norm, k_norm,
              proj_a, proj_b, proj_c, gate_up, gate_b, w_out, mlp_up, mlp_conv_w, mlp_conv_b, mlp_down):
    b = x.shape[0]
    mod_shared = jax.nn.silu(c) @ ada_w + ada_b
    for layer in range(DEPTH):
        mod = (mod_shared + ada_layer[layer]).reshape(b, N_MOD, 1, D_MODEL)
        shift_m, scale_m, gate_m = mod[:, 0], mod[:, 1], mod[:, 2]
        shift_f, scale_f, gate_f = mod[:, 3], mod[:, 4], mod[:, 5]
        h = _rmsnorm(x, norm_mix[layer]) * (1 + scale_m) + shift_m
        x = x + gate_m * _mixer_block(h, w_in[layer], pool_w[layer], pool_scale[layer], ssd_conv_w[layer],
                                      ssd_conv_b[layer], ssd_dt_bias[layer], ssd_a_log[layer], ssd_d[layer],
                                      ssd_norm[layer], q_norm[layer], k_norm[layer], t5_table, proj_a[layer],
                                      proj_b[layer], proj_c[layer], gate_up[layer], gate_b[layer], w_out[layer])
        h = _rmsnorm(x, norm_mlp[layer]) * (1 + scale_f) + shift_f
        x = x + gate_f * _conv_ffn(h, mlp_up[layer], mlp_conv_w[layer], mlp_conv_b[layer], mlp_down[layer])
    return x
```

```python
import contextlib
import numpy as np
import concourse.bass as bass
import concourse.mybir as mybir
from concourse.bass_utils import run_bass_kernel_spmd

F32 = mybir.dt.float32
BF16 = mybir.dt.bfloat16
ALU = mybir.AluOpType
AF = mybir.ActivationFunctionType
AX = mybir.AxisListType

N_DMA_SEMS = 6


class Res:
    __slots__ = ("name", "w", "r")

    def __init__(self, name=""):
        self.name = name
        self.w = None
        self.r = []


class Prog:
    ENGS = ("pe", "act", "dve", "pool", "sp")

    def __init__(self, nc):
        self.nc = nc
        self.q = {e: [] for e in self.ENGS}
        self.cnt = {e: 0 for e in self.ENGS}
        self.waited = {e: {} for e in self.ENGS}
        self.dma_n = {e: 0 for e in ("sp", "pool", "act")}
        self.sems = {}
        self.dma_tokens = []

    def _need(self, eng, toks):
        out = []
        wd = self.waited[eng]
        for t in toks:
            if t is None:
                continue
            k, v = t
            if k == eng and eng == "pe":
                continue
            if wd.get(k, 0) >= v:
                continue
            wd[k] = v
            out.append((k, v))
        best = {}
        for k, v in out:
            best[k] = max(best.get(k, 0), v)
        return list(best.items())

    def _deps(self, reads, writes):
        toks = []
        for r in reads:
            toks.append(r.w)
        for w in writes:
            toks.append(w.w)
            toks.extend(w.r)
        return toks

    def _commit(self, tok, reads, writes):
        for r in reads:
            r.r.append(tok)
        for w in writes:
            w.w = tok
            w.r = []

    def op(self, eng, fn, reads=(), writes=()):
        waits = self._need(eng, self._deps(reads, writes))
        self.cnt[eng] += 1
        tok = (eng, self.cnt[eng])
        self.q[eng].append((waits, fn, ("c", eng)))
        self._commit(tok, reads, writes)
        return tok

    def dma(self, queue, fn, reads=(), writes=()):
        j = self.dma_n[queue]
        self.dma_n[queue] += 1
        s = j % N_DMA_SEMS
        semkey = (queue, s)
        val = 16 * (j // N_DMA_SEMS + 1)
        deps = self._deps(reads, writes)
        if j >= N_DMA_SEMS:
            deps.append((semkey, val - 16))
        waits = self._need(queue, deps)
        tok = (semkey, val)
        self.q[queue].append((waits, fn, ("d", semkey)))
        self._commit(tok, reads, writes)
        self.dma_tokens.append(tok)
        return tok

    def barrier(self, final=False):
        last = {}
        for k, v in self.dma_tokens:
            last[k] = max(last.get(k, 0), v)
        toks = list(last.items())
        for e in ("pe", "act", "dve", "pool"):
            if self.cnt[e]:
                toks.append((e, self.cnt[e]))
        for eng in (("sp",) if final else self.ENGS):
            waits = self._need(eng, [t for t in toks if t[0] != eng])
            if waits:
                self.q[eng].append((waits, None, None))

    def alloc_sems(self, st):
        nc = self.nc
        self.semh = {}
        for e in ("pe", "act", "dve", "pool"):
            self.semh[e] = st.enter_context(nc.semaphore("s_" + e))
        for qn in ("sp", "pool", "act"):
            for i in range(N_DMA_SEMS):
                self.semh[(qn, i)] = st.enter_context(nc.semaphore("d_%s%d" % (qn, i)))

    def flush(self, final=False):
        nc = self.nc
        self.barrier(final=False)
        if final:
            self.barrier(final=True)
        semh = self.semh
        q = self.q
        self.q = {e: [] for e in self.ENGS}

        def run(eng, items):
            for waits, fn, inc in items:
                for k, v in waits:
                    eng.wait_ge(semh[k], v)
                if fn is None:
                    continue
                ins = fn(eng)
                if inc[0] == "c":
                    ins.then_inc(semh[inc[1]], 1)
                else:
                    ins.then_inc(semh[inc[1]], 16)

        with nc.Block() as block:
            @block.sync
            def _(e):
                run(e, q["sp"])

            @block.tensor
            def _(e):
                run(e, q["pe"])

            @block.scalar
            def _(e):
                run(e, q["act"])

            @block.vector
            def _(e):
                run(e, q["dve"])

            @block.gpsimd
            def _(e):
                run(e, q["pool"])

D = 4096
KCD = D // 128
DEPTH = 4
SEQ = 8192
POOL_WINDOWS = (2, 4, 8, 16)
POOL_W = 1536
SSD_INNER = 2048
SSD_XBC = 3072
ATT_QKV = 4608
GATE_RANK = 512
N_IN = 11840
A0, Z0, XBC0, DT0, QKV0, GL0 = 0, 1536, 3584, 6656, 6720, 11328
MLP_H = 8192
EPS = 1e-6
W_IN_CHUNKS = [(128 * i, 128) for i in range(52)] + [(DT0, 64)] + [(QKV0 + 128 * i, 128) for i in range(40)]
TT = 2048


def slabs(W, KC):
    K, N = W.shape
    return np.ascontiguousarray(W.reshape(KC, 128, N // 128, 128).transpose(2, 1, 0, 3)).reshape(N // 128, 128, KC * 128)


def cols(v):
    return np.ascontiguousarray(v.reshape(-1, 128).T)


class RowSplit:
    def __init__(self, nc, name, bounds, L, dt):
        self.parts = []
        for i in range(len(bounds) - 1):
            r0, r1 = bounds[i], bounds[i + 1]
            self.parts.append((r0, r1, nc.dram_tensor("%s_%d" % (name, i), [r1 - r0, L], dt).ap()))

    def __getitem__(self, key):
        rs, cs = key
        for r0, r1, ap in self.parts:
            if r0 <= rs.start and rs.stop <= r1:
                return ap[rs.start - r0:rs.stop - r0, cs]
        raise IndexError((rs, cs))


class Builder:
    def __init__(self, L, depth, debug=None):
        self.L = L
        self.depth = depth
        self.debug = debug or ()
        self.nc = bass.Bass("TRN2", target_bir_lowering=False)
        self.P = Prog(self.nc)
        self.st = contextlib.ExitStack()
        self.dram = {}

    def din(self, name, shape, dt=F32):
        t = self.nc.dram_tensor(name, list(shape), dt, kind="ExternalInput").ap()
        self.dram[name] = t
        return t

    def dscr(self, name, shape, dt=F32):
        kind = "ExternalOutput" if name in self.debug else "Internal"
        t = self.nc.dram_tensor(name, list(shape), dt, kind=kind).ap()
        self.dram[name] = t
        return t, Res(name)

    def sb(self, st, name, shape, dt=F32):
        self._uid = getattr(self, "_uid", 0) + 1
        return st.enter_context(self.nc.sbuf_tensor("%s_%d" % (name, self._uid), list(shape), dt))

    def setup(self):
        nc, P, st = self.nc, self.P, self.st
        P.alloc_sems(st)
        self.ps = []
        for i in range(8):
            t = st.enter_context(nc.psum_tensor("ps%d" % i, [128, 512], F32))
            self.ps.append((t, Res("ps%d" % i)))
        self.ones = self.sb(st, "ones", [128, 128], F32)
        self.r_const = Res("const")
        P.op("dve", lambda e: e.memset(self.ones[:], 1.0), writes=[self.r_const])
        self.ident = self.sb(st, "ident", [128, 128], F32)
        self.identb = self.sb(st, "identb", [128, 128], BF16)
        idin = self.din("ident_in", [128, 128])
        P.dma("sp", lambda e: e.dma_start(out=self.ident[:], in_=idin), writes=[self.r_const])
        P.op("dve", lambda e: e.tensor_copy(out=self.identb[:], in_=self.ident[:]), reads=[self.r_const], writes=[self.r_const])
        self.modS = self.sb(st, "modS", [128, 192], F32)
        self.r_modS = Res("modS")
        self.modL = self.sb(st, "modL", [128, 192], F32)
        self.r_modL = Res("modL")
        self.Acol = self.sb(st, "Acol", [128, 64], F32)
        self.r_Acol = Res("Acol")

    def emit_adaln(self):
        nc, P = self.nc, self.P
        ccol = self.din("ccol", [128, KCD])
        adaw = self.din("adaw", [192, 128, D])
        adab = self.din("adab", [128, 192])
        with contextlib.ExitStack() as st:
            sc = self.sb(st, "sc", [128, KCD], F32)
            ab = self.sb(st, "ab", [128, 192], F32)
            r_sc, r_ab = Res(), Res()
            ring = [(self.sb(st, "aw%d" % i, [128, D], F32), Res()) for i in range(3)]
            P.dma("sp", lambda e: e.dma_start(out=sc[:], in_=ccol), writes=[r_sc])
            P.dma("sp", lambda e: e.dma_start(out=ab[:], in_=adab), writes=[r_ab])
            P.op("act", lambda e: e.activation(out=sc[:], in_=sc[:], func=AF.Silu), reads=[r_sc], writes=[r_sc])
            pst, r_ps = self.ps[0]
            for n in range(192):
                wt, r_w = ring[n % 3]
                P.dma("sp" if n % 2 == 0 else "act", lambda e, wt=wt, n=n: e.dma_start(out=wt[:], in_=adaw[n]), writes=[r_w])
                for kc in range(KCD):
                    P.op("pe", lambda e, wt=wt, n=n, kc=kc: e.matmul(pst[:, n:n + 1], lhsT=wt[:, kc * 128:(kc + 1) * 128],
                                                                     rhs=sc[:, kc:kc + 1], start=(kc == 0), stop=(kc == KCD - 1)),
                         reads=[r_w, r_sc], writes=[r_ps])
            P.op("dve", lambda e: e.tensor_tensor(out=self.modS[:], in0=pst[:, 0:192], in1=ab[:], op=ALU.add),
                 reads=[r_ps, r_ab], writes=[self.r_modS])
            P.flush()

    def emit_layer_mod(self, l):
        nc, P = self.nc, self.P
        if l == 0:
            self.adal = self.din("adal", [DEPTH, 128, 192])
            self.gnorm = self.din("gnorm", [DEPTH, 128, 64])
        with contextlib.ExitStack() as st:
            al = self.sb(st, "al", [128, 192], F32)
            gn = self.sb(st, "gn", [128, 64], F32)
            r_al, r_gn = Res(), Res()
            P.dma("sp", lambda e: e.dma_start(out=al[:], in_=self.adal[l]), writes=[r_al])
            P.dma("sp", lambda e: e.dma_start(out=gn[:], in_=self.gnorm[l]), writes=[r_gn])
            P.op("dve", lambda e: e.tensor_tensor(out=self.modL[:], in0=self.modS[:], in1=al[:], op=ALU.add),
                 reads=[self.r_modS, r_al], writes=[self.r_modL])
            for j, sc0 in ((0, 32), (1, 128)):
                P.op("dve", lambda e, j=j, sc0=sc0: e.scalar_tensor_tensor(
                    out=self.Acol[:, j * 32:(j + 1) * 32], in0=self.modL[:, sc0:sc0 + 32], scalar=1.0,
                    in1=gn[:, j * 32:(j + 1) * 32], op0=ALU.add, op1=ALU.mult),
                    reads=[self.r_modL, r_gn], writes=[self.r_Acol])
            P.flush()

    def emit_norm(self, st, xT, r_xT, t0, T, hT, r_h, acol0, shift0):
        nc, P = self.nc, self.P
        xs = [(self.sb(st, "nx%d" % i, [128, 512], F32), Res()) for i in range(4)]
        sq = [(self.sb(st, "nsq%d" % i, [128, 512], F32), Res()) for i in range(2)]
        rstd, r_rstd = self.sb(st, "nrstd", [128, 512], F32), Res()
        tmp = [(self.sb(st, "ntmp%d" % i, [128, 512], F32), Res()) for i in range(2)]
        cnt = 0
        for tb in range(T // 512):
            c0 = t0 + tb * 512
            pst, r_ps = self.ps[tb % 2]
            for kc in range(KCD):
                xt, r_x = xs[cnt % 4]
                sqt, r_sq = sq[cnt % 2]
                cnt += 1
                P.dma("sp", lambda e, xt=xt, kc=kc, c0=c0: e.dma_start(out=xt[:], in_=xT[kc * 128:(kc + 1) * 128, c0:c0 + 512]),
                      writes=[r_x])
                P.op("act", lambda e, xt=xt, sqt=sqt: e.activation(out=sqt[:], in_=xt[:], func=AF.Square), reads=[r_x], writes=[r_sq])
                P.op("pe", lambda e, sqt=sqt, kc=kc, pst=pst: e.matmul(pst[:], lhsT=self.ones[:], rhs=sqt[:], start=(kc == 0), stop=(kc == KCD - 1)),
                     reads=[r_sq, self.r_const], writes=[r_ps])
            P.op("act", lambda e, pst=pst: e.activation(out=rstd[:], in_=pst[:], func=AF.Sqrt, bias=EPS, scale=1.0 / D),
                 reads=[r_ps], writes=[r_rstd])
            P.op("dve", lambda e: e.reciprocal(out=rstd[:], in_=rstd[:]), reads=[r_rstd], writes=[r_rstd])
            for kc in range(KCD):
                xt, r_x = xs[cnt % 4]
                tt_, r_t = tmp[cnt % 2]
                cnt += 1
                P.dma("sp", lambda e, xt=xt, kc=kc, c0=c0: e.dma_start(out=xt[:], in_=xT[kc * 128:(kc + 1) * 128, c0:c0 + 512]),
                      writes=[r_x])
                P.op("dve", lambda e, xt=xt, tt_=tt_, kc=kc: e.scalar_tensor_tensor(
                    out=tt_[:], in0=xt[:], scalar=self.Acol[:, acol0 + kc:acol0 + kc + 1], in1=rstd[:], op0=ALU.mult, op1=ALU.mult),
                    reads=[r_x, r_rstd, self.r_Acol], writes=[r_t])
                P.op("act", lambda e, tt_=tt_, kc=kc, tb=tb: e.activation(
                    out=hT[:, kc, tb * 512:(tb + 1) * 512], in_=tt_[:], func=AF.Identity,
                    bias=self.modL[:, shift0 + kc:shift0 + kc + 1], scale=1.0),
                    reads=[r_t, self.r_modL], writes=[r_h[kc]])

    def emit_linear(self, st, hT, r_h, KC, T, wsl, widths, epi, nring=3, tag="w"):
        nc, P = self.nc, self.P
        ring = [(self.sb(st, "%s%d" % (tag, i), [128, KC * 128], BF16), Res()) for i in range(nring)]
        NTB = T // 512
        nset = 8 // NTB
        for n, width in enumerate(widths):
            wt, r_w = ring[n % nring]
            P.dma("pool", lambda e, wt=wt, n=n: e.dma_start(out=wt[:], in_=wsl[n]), writes=[r_w])
            base = (n % nset) * NTB
            for kc in range(KC):
                for tb in range(NTB):
                    pst, r_ps = self.ps[base + tb]
                    P.op("pe", lambda e, wt=wt, kc=kc, tb=tb, pst=pst, width=width: e.matmul(
                        pst[0:width, :], lhsT=wt[:, kc * 128:kc * 128 + width], rhs=hT[:, kc, tb * 512:(tb + 1) * 512],
                        start=(kc == 0), stop=(kc == KC - 1)),
                        reads=[r_w, r_h[kc]], writes=[r_ps])
            for tb in range(NTB):
                pst, r_ps = self.ps[base + tb]
                epi(n, tb, pst, r_ps, width)

    def emit_inproj(self, l, xT, r_xT, PT, r_PT):
        nc, P = self.nc, self.P
        if l == 0:
            self.w_in = self.din("w_in", [self.depth, len(W_IN_CHUNKS), 128, D])
        for tt in range(self.L // TT):
            t0 = tt * TT
            with contextlib.ExitStack() as st:
                hT = self.sb(st, "hT", [128, KCD, TT], BF16)
                r_h = [Res() for _ in range(KCD)]
                with contextlib.ExitStack() as st2:
                    self.emit_norm(st2, xT, r_xT, t0, TT, hT, r_h, 0, 0)
                    P.flush()
                stg = [(self.sb(st, "stg%d" % i, [128, 512], F32), Res()) for i in range(4)]
                k = [0]

                def epi(n, tb, pst, r_ps, width):
                    s, r_s = stg[k[0] % 4]
                    k[0] += 1
                    r0 = W_IN_CHUNKS[n][0]
                    P.op("act", lambda e: e.activation(out=s[0:width, :], in_=pst[0:width, :], func=AF.Copy), reads=[r_ps], writes=[r_s])
                    P.dma("sp", lambda e: e.dma_start(out=PT[r0:r0 + width, t0 + tb * 512:t0 + (tb + 1) * 512], in_=s[0:width, :]),
                          reads=[r_s])
                self.emit_linear(st, hT, r_h, KCD, TT, self.w_in[l], [w for _, w in W_IN_CHUNKS], epi)
                P.flush()

    def emit_pool(self, l, PT, YT):
        nc, P, L = self.nc, self.P, self.L
        if l == 0:
            self.pool_w = self.din("pool_w", [self.depth, 4, 384, 384])
            self.poolsc = self.din("poolsc", [self.depth, 128, 12])
            self.invcnt = self.din("invcnt", [4, 128, L])
        TS = min(L, 4096)
        HAL = 16
        with contextlib.ExitStack() as st:
            U = self.sb(st, "pU", [128, TS + 32], F32)
            S = [self.sb(st, "pS%d" % i, [128, TS + 32], F32) for i in range(2)]
            IC = self.sb(st, "pIC", [128, TS], F32)
            Pc = [self.sb(st, "pP%d" % i, [128, TS], BF16) for i in range(3)]
            pw = self.sb(st, "pw", [128, 3, 384], BF16)
            psc = self.sb(st, "psc", [128, 12], F32)
            stg = [(self.sb(st, "pstg%d" % i, [128, 512], F32), Res()) for i in range(3)]
            r_U, r_S, r_IC, r_pw, r_psc = Res(), [Res(), Res()], Res(), Res(), Res()
            r_Pc = [Res() for _ in range(3)]
            P.dma("sp", lambda e: e.dma_start(out=psc[:], in_=self.poolsc[l]), writes=[r_psc])
            k = 0
            for g, win in enumerate(POOL_WINDOWS):
                P.dma("pool", lambda e, g=g: e.dma_start(out=pw[:], in_=self.pool_w[l, g].rearrange("(c p) d -> p c d", p=128)), writes=[r_pw])
                for t0 in range(0, L, TS):
                    P.dma("sp", lambda e, g=g, t0=t0: e.dma_start(out=IC[:], in_=self.invcnt[g, :, t0:t0 + TS]), writes=[r_IC])
                    for c in range(3):
                        row0 = A0 + g * 384 + c * 128
                        lo, hi = max(0, t0 - HAL), min(L, t0 + TS + HAL)
                        P.op("pool", lambda e: e.memset(U[:], 0.0), writes=[r_U])
                        P.dma("sp", lambda e, row0=row0, lo=lo, hi=hi, t0=t0: e.dma_start(
                            out=U[:, lo - (t0 - HAL):hi - (t0 - HAL)], in_=PT[row0:row0 + 128, lo:hi]), writes=[r_U])
                        src, r_src = U, r_U
                        kk, wlen, bi = 1, TS + 32, 0
                        while kk < win:
                            wlen -= kk
                            dst, r_dst = S[bi], r_S[bi]
                            P.op("dve", lambda e, src=src, dst=dst, kk=kk, wlen=wlen: e.tensor_tensor(
                                out=dst[:, 0:wlen], in0=src[:, 0:wlen], in1=src[:, kk:kk + wlen], op=ALU.add),
                                reads=[r_src], writes=[r_dst])
                            src, r_src = dst, r_dst
                            kk *= 2
                            bi ^= 1
                        Mb, r_Mb = S[bi], r_S[bi]
                        o = HAL - win // 2
                        P.op("dve", lambda e, src=src, Mb=Mb, o=o: e.tensor_tensor(out=Mb[:, 0:TS], in0=src[:, o:o + TS], in1=IC[:], op=ALU.mult),
                             reads=[r_src, r_IC], writes=[r_Mb])
                        P.op("dve", lambda e, Mb=Mb, c=c: e.tensor_tensor(out=Pc[c][:], in0=Mb[:, 0:TS], in1=U[:, HAL:HAL + TS], op=ALU.subtract),
                             reads=[r_Mb, r_U], writes=[r_Pc[c]])
                    for d in range(3):
                        for tb in range(TS // 512):
                            pst, r_ps = self.ps[k % 8]
                            s, r_s = stg[k % 3]
                            k += 1
                            for c in range(3):
                                P.op("pe", lambda e, pst=pst, c=c, d=d, tb=tb: e.matmul(
                                    pst[:], lhsT=pw[:, c, d * 128:(d + 1) * 128], rhs=Pc[c][:, tb * 512:(tb + 1) * 512],
                                    start=(c == 0), stop=(c == 2)), reads=[r_pw, r_Pc[c]], writes=[r_ps])
                            col = g * 3 + d
                            P.op("act", lambda e, s=s, pst=pst, col=col: e.activation(out=s[:], in_=pst[:], func=AF.Copy, scale=psc[:, col:col + 1]),
                                 reads=[r_ps, r_psc], writes=[r_s])
                            r0 = g * 384 + d * 128
                            P.dma("sp", lambda e, s=s, r0=r0, c0=t0 + tb * 512: e.dma_start(out=YT[r0:r0 + 128, c0:c0 + 512], in_=s[:]), reads=[r_s])
            P.flush()


    def emit_attn(self, l, PT, YT):
        nc, P, L = self.nc, self.P, self.L
        if l == 0:
            self.qkn = self.din("qkn", [self.depth, 128, 2])
            self.bmask = self.din("bmask", [12, 128, 4, 256])
            self.OTd = self.nc.dram_tensor("OTd", [12, 128, L], F32).ap()
            self.LSEd = self.nc.dram_tensor("LSEd", [12, L], F32).ap()
        HALM = 1024
        NB = L // 128
        with contextlib.ExitStack() as st:
            QN = self.sb(st, "aQN", [128, L], BF16)
            KN = self.sb(st, "aKN", [128, L + 2 * HALM], BF16)
            VN = self.sb(st, "aVN", [128, L + 2 * HALM], BF16)
            VT = self.sb(st, "aVT", [128, NB + 16, 128], BF16)
            OT = self.sb(st, "aOT", [128, L], F32)
            LSEc = self.sb(st, "aLSE", [128, NB], F32)
            bm = self.sb(st, "abm", [128, 4, 256], F32)
            gq = self.sb(st, "agq", [128, 2], F32)
            r_QN, r_KN, r_VN, r_VT, r_OT, r_LSE, r_bm, r_gq = [Res() for _ in range(8)]
            ld = [(self.sb(st, "ald%d" % i, [128, 512], F32), Res()) for i in range(3)]
            sq = [(self.sb(st, "asq%d" % i, [128, 512], F32), Res()) for i in range(2)]
            rs = [(self.sb(st, "ars%d" % i, [128, 512], F32), Res()) for i in range(2)]
            S2 = [(self.sb(st, "aS2%d" % i, [128, 256], F32), Res()) for i in range(2)]
            Pm = [(self.sb(st, "aPm%d" % i, [128, 256], BF16), Res()) for i in range(2)]
            PmT = [(self.sb(st, "aPT%d" % i, [128, 256], BF16), Res()) for i in range(2)]
            ob = [(self.sb(st, "aob%d" % i, [128, 128], F32), Res()) for i in range(2)]
            sm = [(self.sb(st, "asm%d" % i, [128, 4], F32), Res()) for i in range(2)]
            P.dma("sp", lambda e: e.dma_start(out=gq[:], in_=self.qkn[l]), writes=[r_gq])
            P.op("dve", lambda e: e.tensor_scalar(out=gq[:, 0:1], in0=gq[:, 0:1], scalar1=float(128 ** -0.5), scalar2=None, op0=ALU.mult),
                 reads=[r_gq], writes=[r_gq])
            for buf, r_b in ((KN, r_KN), (VN, r_VN)):
                P.op("pool", lambda e, buf=buf: e.memset(buf[:, 0:HALM], 0.0), writes=[r_b])
                P.op("pool", lambda e, buf=buf: e.memset(buf[:, HALM + L:HALM + L + HALM], 0.0), writes=[r_b])
            cnt = 0
            for gi, dil in enumerate((1, 4, 16)):
                m = L // dil
                nblk = m // 128
                for hh in range(4):
                    gh = gi * 4 + hh
                    P.dma("sp", lambda e, gh=gh: e.dma_start(out=bm[:], in_=self.bmask[gh]), writes=[r_bm])
                    for which in range(3):
                        row0 = QKV0 + which * 1536 + gi * 512 + hh * 128
                        for tb in range(L // 512):
                            t, r_t = ld[cnt % 3]
                            cnt += 1
                            P.dma("sp", lambda e, t=t, row0=row0, tb=tb: e.dma_start(out=t[:], in_=PT[row0:row0 + 128, tb * 512:(tb + 1) * 512]), writes=[r_t])
                            if which == 2:
                                P.op("act", lambda e, t=t, tb=tb: e.activation(out=VN[:, HALM + tb * 512:HALM + (tb + 1) * 512], in_=t[:], func=AF.Copy),
                                     reads=[r_t], writes=[r_VN])
                                continue
                            sqt, r_sq = sq[cnt % 2]
                            rst, r_rs = rs[cnt % 2]
                            pst, r_ps = self.ps[cnt % 2]
                            P.op("act", lambda e, t=t, sqt=sqt: e.activation(out=sqt[:], in_=t[:], func=AF.Square), reads=[r_t], writes=[r_sq])
                            P.op("pe", lambda e, sqt=sqt, pst=pst: e.matmul(pst[:], lhsT=self.ones[:], rhs=sqt[:], start=True, stop=True),
                                 reads=[r_sq, self.r_const], writes=[r_ps])
                            P.op("act", lambda e, rst=rst, pst=pst: e.activation(out=rst[:], in_=pst[:], func=AF.Sqrt, bias=EPS, scale=1.0 / 128),
                                 reads=[r_ps], writes=[r_rs])
                            P.op("dve", lambda e, rst=rst: e.reciprocal(out=rst[:], in_=rst[:]), reads=[r_rs], writes=[r_rs])
                            dst = QN[:, tb * 512:(tb + 1) * 512] if which == 0 else KN[:, HALM + tb * 512:HALM + (tb + 1) * 512]
                            P.op("dve", lambda e, t=t, rst=rst, dst=dst, which=which: e.scalar_tensor_tensor(
                                out=dst, in0=t[:], scalar=gq[:, which:which + 1], in1=rst[:], op0=ALU.mult, op1=ALU.mult),
                                reads=[r_t, r_rs, r_gq], writes=[r_QN if which == 0 else r_KN])
                    ntile = dil * (nblk + 1)
                    tiles = [(r, n) for r in range(dil) for n in range(nblk + 1)]
                    for t0 in range(0, ntile, 4):
                        pst, r_ps = self.ps[2 + (t0 // 4) % 2]
                        grp = tiles[t0:t0 + 4]
                        for j, (r, n) in enumerate(grp):
                            s0 = HALM + r + dil * (128 * n - 64)
                            P.op("pe", lambda e, pst=pst, j=j, s0=s0, dil=dil: e.matmul(
                                pst[:, j * 128:(j + 1) * 128], lhsT=VN[:, s0:s0 + 127 * dil + 1:dil], rhs=self.identb[:], start=True, stop=True),
                                reads=[r_VN, self.r_const], writes=[r_ps])
                        ng = len(grp)
                        P.op("act", lambda e, pst=pst, t0=t0, ng=ng: e.activation(
                            out=VT[:, t0:t0 + ng, :], in_=pst[:, 0:ng * 128].rearrange("p (a b) -> p a b", b=128), func=AF.Copy),
                            reads=[r_ps], writes=[r_VT])
                    bi = 0
                    for r in range(dil):
                        for n in range(nblk):
                            var = (1 if n == 0 else 0) + (2 if n == nblk - 1 else 0)
                            psS, r_psS = self.ps[bi % 2]
                            psT, r_psT = self.ps[2 + bi % 2]
                            psO, r_psO = self.ps[4 + bi % 2]
                            psX, r_psX = self.ps[6 + bi % 2]
                            s2, r_s2 = S2[bi % 2]
                            pm, r_pm = Pm[bi % 2]
                            pmt, r_pmt = PmT[bi % 2]
                            o_, r_o = ob[bi % 2]
                            smt, r_sm = sm[bi % 2]
                            q0 = r + dil * 128 * n
                            k0 = HALM + r + dil * (128 * n - 64)
                            P.op("pe", lambda e, psS=psS, q0=q0, k0=k0, dil=dil: e.matmul(
                                psS[:, 0:256], lhsT=QN[:, q0:q0 + 127 * dil + 1:dil], rhs=KN[:, k0:k0 + 255 * dil + 1:dil], start=True, stop=True),
                                reads=[r_QN, r_KN], writes=[r_psS])
                            P.op("dve", lambda e, psS=psS, s2=s2, var=var: e.tensor_tensor(out=s2[:], in0=psS[:, 0:256], in1=bm[:, var, :], op=ALU.add),
                                 reads=[r_psS, r_bm], writes=[r_s2])
                            P.op("dve", lambda e, s2=s2, smt=smt: e.reduce_max(out=smt[:, 0:1], in_=s2[:], axis=AX.X), reads=[r_s2], writes=[r_sm])
                            P.op("dve", lambda e, smt=smt: e.tensor_scalar(out=smt[:, 1:2], in0=smt[:, 0:1], scalar1=-1.0, scalar2=None, op0=ALU.mult),
                                 reads=[r_sm], writes=[r_sm])
                            P.op("act", lambda e, s2=s2, pm=pm, smt=smt: e.activation(out=pm[:], in_=s2[:], func=AF.Exp, bias=smt[:, 1:2], scale=1.0,
                                                                                     accum_out=smt[:, 2:3]),
                                 reads=[r_s2, r_sm], writes=[r_pm, r_sm])
                            for kb in range(2):
                                P.op("pe", lambda e, psT=psT, pm=pm, kb=kb: e.matmul(
                                    psT[:, kb * 128:(kb + 1) * 128], lhsT=pm[:, kb * 128:(kb + 1) * 128], rhs=self.identb[:], start=True, stop=True),
                                    reads=[r_pm, self.r_const], writes=[r_psT])
                            P.op("dve", lambda e, psT=psT, pmt=pmt: e.tensor_copy(out=pmt[:], in_=psT[:, 0:256]), reads=[r_psT], writes=[r_pmt])
                            ti = r * (nblk + 1) + n
                            for kb in range(2):
                                P.op("pe", lambda e, psO=psO, pmt=pmt, kb=kb, ti=ti: e.matmul(
                                    psO[:, 0:128], lhsT=pmt[:, kb * 128:(kb + 1) * 128], rhs=VT[:, ti + kb, :], start=(kb == 0), stop=(kb == 1)),
                                    reads=[r_pmt, r_VT], writes=[r_psO])
                            P.op("dve", lambda e, smt=smt: e.reciprocal(out=smt[:, 3:4], in_=smt[:, 2:3]), reads=[r_sm], writes=[r_sm])
                            P.op("act", lambda e, psO=psO, o_=o_, smt=smt: e.activation(out=o_[:], in_=psO[:, 0:128], func=AF.Copy, scale=smt[:, 3:4]),
                                 reads=[r_psO, r_sm], writes=[r_o])
                            P.op("act", lambda e, smt=smt, bi=bi: e.activation(out=LSEc[:, bi:bi + 1], in_=smt[:, 2:3], func=AF.Ln), reads=[r_sm], writes=[r_LSE])
                            P.op("dve", lambda e, smt=smt, bi=bi: e.tensor_tensor(out=LSEc[:, bi:bi + 1], in0=LSEc[:, bi:bi + 1], in1=smt[:, 0:1], op=ALU.add),
                                 reads=[r_sm, r_LSE], writes=[r_LSE])
                            P.op("pe", lambda e, psX=psX, o_=o_: e.matmul(psX[:, 0:128], lhsT=o_[:], rhs=self.ident[:], start=True, stop=True),
                                 reads=[r_o, self.r_const], writes=[r_psX])
                            P.op("act", lambda e, psX=psX, q0=q0, dil=dil: e.activation(out=OT[:, q0:q0 + 127 * dil + 1:dil], in_=psX[:, 0:128], func=AF.Copy),
                                 reads=[r_psX], writes=[r_OT])
                            bi += 1
                    P.dma("sp", lambda e, gh=gh: e.dma_start(out=self.OTd[gh], in_=OT[:]), reads=[r_OT])
                    lse_v = self.LSEd[gh].rearrange("(n i r) -> i r n", i=128, r=dil)
                    for r in range(dil):
                        for n0 in range(0, nblk, 16):
                            n1 = min(nblk, n0 + 16)
                            P.dma("sp", lambda e, lse_v=lse_v, r=r, n0=n0, n1=n1, nblk=nblk: e.dma_start(
                                out=lse_v[:, r, n0:n1], in_=LSEc[:, r * nblk + n0:r * nblk + n1], allow_slow_non_contiguous=True), reads=[r_LSE])
            P.flush()
        with contextlib.ExitStack() as st:
            o3 = [[(self.sb(st, "mo%d%d" % (i, j), [128, 512], F32), Res()) for j in range(3)] for i in range(2)]
            l3 = [[(self.sb(st, "ml%d%d" % (i, j), [128, 512], F32), Res()) for j in range(3)] for i in range(2)]
            mx = [(self.sb(st, "mm%d" % i, [128, 512], F32), Res()) for i in range(2)]
            den = [(self.sb(st, "md%d" % i, [128, 512], F32), Res()) for i in range(2)]
            acc = [(self.sb(st, "ma%d" % i, [128, 512], F32), Res()) for i in range(2)]
            k = 0
            for hh in range(4):
                for tb in range(L // 512):
                    i = k % 2
                    k += 1
                    c0 = tb * 512
                    for gi in range(3):
                        gh = gi * 4 + hh
                        P.dma("sp", lambda e, i=i, gi=gi, gh=gh, c0=c0: e.dma_start(out=o3[i][gi][0][:], in_=self.OTd[gh, :, c0:c0 + 512]), writes=[o3[i][gi][1]])
                        P.dma("act", lambda e, i=i, gi=gi, gh=gh, c0=c0: e.dma_start(
                            out=l3[i][gi][0][:], in_=self.LSEd[gh:gh + 1, c0:c0 + 512].broadcast_to([128, 512])), writes=[l3[i][gi][1]])
                    (m_, r_m), (d_, r_d), (a_, r_a) = mx[i], den[i], acc[i]
                    lt = [l3[i][g][0] for g in range(3)]
                    r_l = [l3[i][g][1] for g in range(3)]
                    ot = [o3[i][g][0] for g in range(3)]
                    r_o3 = [o3[i][g][1] for g in range(3)]
                    P.op("dve", lambda e, m_=m_, lt=lt: e.tensor_tensor(out=m_[:], in0=lt[0][:], in1=lt[1][:], op=ALU.max), reads=[r_l[0], r_l[1]], writes=[r_m])
                    P.op("dve", lambda e, m_=m_, lt=lt: e.tensor_tensor(out=m_[:], in0=m_[:], in1=lt[2][:], op=ALU.max), reads=[r_l[2], r_m], writes=[r_m])
                    for g in range(3):
                        P.op("dve", lambda e, m_=m_, lt=lt, g=g: e.tensor_tensor(out=lt[g][:], in0=lt[g][:], in1=m_[:], op=ALU.subtract), reads=[r_m, r_l[g]], writes=[r_l[g]])
                        P.op("act", lambda e, lt=lt, g=g: e.activation(out=lt[g][:], in_=lt[g][:], func=AF.Exp), reads=[r_l[g]], writes=[r_l[g]])
                        P.op("dve", lambda e, lt=lt, ot=ot, g=g: e.tensor_tensor(out=ot[g][:], in0=ot[g][:], in1=lt[g][:], op=ALU.mult), reads=[r_l[g], r_o3[g]], writes=[r_o3[g]])
                    P.op("dve", lambda e, d_=d_, lt=lt: e.tensor_tensor(out=d_[:], in0=lt[0][:], in1=lt[1][:], op=ALU.add), reads=[r_l[0], r_l[1]], writes=[r_d])
                    P.op("dve", lambda e, d_=d_, lt=lt: e.tensor_tensor(out=d_[:], in0=d_[:], in1=lt[2][:], op=ALU.add), reads=[r_l[2], r_d], writes=[r_d])
                    P.op("dve", lambda e, d_=d_: e.reciprocal(out=d_[:], in_=d_[:]), reads=[r_d], writes=[r_d])
                    P.op("dve", lambda e, a_=a_, ot=ot: e.tensor_tensor(out=a_[:], in0=ot[0][:], in1=ot[1][:], op=ALU.add), reads=[r_o3[0], r_o3[1]], writes=[r_a])
                    P.op("dve", lambda e, a_=a_, ot=ot: e.tensor_tensor(out=a_[:], in0=a_[:], in1=ot[2][:], op=ALU.add), reads=[r_o3[2], r_a], writes=[r_a])
                    P.op("dve", lambda e, a_=a_, d_=d_: e.tensor_tensor(out=a_[:], in0=a_[:], in1=d_[:], op=ALU.mult), reads=[r_d, r_a], writes=[r_a])
                    r0 = 3584 + hh * 128
                    P.dma("sp", lambda e, a_=a_, r0=r0, c0=c0: e.dma_start(out=YT[r0:r0 + 128, c0:c0 + 512], in_=a_[:]), reads=[r_a])
            P.flush()


    def emit_ssd(self, l, PT, YT):
        nc, P, L = self.nc, self.P, self.L
        NC = L // 128
        if l == 0:
            self.ssdv = self.din("ssdv", [self.depth, 4, 16, 2])
            self.convw = self.din("convw", [self.depth, 128, 24, 6])
            self.dskip = self.din("dskip", [self.depth, 4, 128, 8])
            self.ssdn = self.din("ssdn", [self.depth, 128, 16])
            self.ssdc_in = self.din("ssdc", [128, 4, 128])
            self.XCd = self.nc.dram_tensor("XCd", [6, 128, L], F32).ap()
            self.PRVd = self.nc.dram_tensor("PRVd", [NC, 128, 512], BF16).ap()

        def bc3(ap2, n):
            return ap2.rearrange("p (h o) -> p h o", o=1).broadcast_to([128, ap2.shape[1], n])

        for g in range(4):
            with contextlib.ExitStack() as st:
                TS = min(L, 4096)
                U = [(self.sb(st, "sU%d" % i, [128, TS + 4], F32), Res()) for i in range(2)]
                AC = [(self.sb(st, "sA%d" % i, [128, TS], F32), Res()) for i in range(2)]
                cw = self.sb(st, "scw", [128, 24, 6], F32)
                r_cw = Res()
                P.dma("sp", lambda e: e.dma_start(out=cw[:], in_=self.convw[l]), writes=[r_cw])
                k = 0
                for ci, (row0, wc) in enumerate([(XBC0 + g * 512 + j * 128, 4 * g + j) for j in range(4)]
                                                + [(XBC0 + 2048 + g * 128, 16 + g), (XBC0 + 2560 + g * 128, 20 + g)]):
                    for t0 in range(0, L, TS):
                        (u, r_u), (a, r_a) = U[k % 2], AC[k % 2]
                        k += 1
                        lo, hi = max(0, t0 - 2), min(L, t0 + TS + 2)
                        P.op("pool", lambda e, u=u: e.memset(u[:], 0.0), writes=[r_u])
                        P.dma("sp", lambda e, u=u, row0=row0, lo=lo, hi=hi, t0=t0: e.dma_start(
                            out=u[:, lo - (t0 - 2):hi - (t0 - 2)], in_=PT[row0:row0 + 128, lo:hi]), writes=[r_u])
                        P.op("dve", lambda e, u=u, a=a, wc=wc: e.tensor_scalar(out=a[:], in0=u[:, 0:TS], scalar1=cw[:, wc, 0:1], scalar2=cw[:, wc, 5:6],
                                                                             op0=ALU.mult, op1=ALU.add), reads=[r_u, r_cw], writes=[r_a])
                        for tap in range(1, 5):
                            P.op("dve", lambda e, u=u, a=a, wc=wc, tap=tap: e.scalar_tensor_tensor(
                                out=a[:], in0=u[:, tap:tap + TS], scalar=cw[:, wc, tap:tap + 1], in1=a[:], op0=ALU.mult, op1=ALU.add),
                                reads=[r_u, r_cw, r_a], writes=[r_a])
                        P.op("act", lambda e, a=a: e.activation(out=a[:], in_=a[:], func=AF.Silu), reads=[r_a], writes=[r_a])
                        P.dma("sp", lambda e, a=a, ci=ci, t0=t0: e.dma_start(out=self.XCd[ci, :, t0:t0 + TS], in_=a[:]), reads=[r_a])
                P.flush()
            with contextlib.ExitStack() as stg_:
                DT = self.sb(stg_, "sDT", [128, NC, 16], F32)
                ACS = self.sb(stg_, "sACS", [128, NC, 16], F32)
                EIN = self.sb(stg_, "sEIN", [128, NC, 16], F32)
                EOW = self.sb(stg_, "sEOW", [128, NC, 16], F32)
                DEC = self.sb(stg_, "sDEC", [128, NC, 16], F32)
                SC = self.sb(stg_, "sSC", [128, 4, 128], F32)
                NMB = self.sb(stg_, "sNMB", [128, 2, 4, 128], BF16)
                DSK = self.sb(stg_, "sDSK", [128, 8], F32)
                NG = self.sb(stg_, "sNG", [128, 16], F32)
                r_tab = Res()
                r_sc = Res()
                with contextlib.ExitStack() as st:
                    dtT = self.sb(st, "sdtT", [16, L], F32)
                    dtE = self.sb(st, "sdtE", [16, L], F32)
                    dtA = self.sb(st, "sdtA", [16, L], F32)
                    DTA = self.sb(st, "sDTA", [128, NC, 16], F32)
                    TOT = self.sb(st, "sTOT", [128, NC, 16], F32)
                    sv = self.sb(st, "ssv", [16, 4], F32)
                    r_dtT, r_dtE, r_dtA, r_DTA, r_TOT, r_sv = [Res() for _ in range(6)]
                    P.dma("sp", lambda e: e.dma_start(out=SC[:], in_=self.ssdc_in), writes=[r_sc])
                    P.dma("sp", lambda e: e.dma_start(out=DSK[:], in_=self.dskip[l, g]), writes=[r_sc])
                    P.dma("sp", lambda e: e.dma_start(out=NG[:], in_=self.ssdn[l]), writes=[r_sc])
                    for d in range(2):
                        for q in range(4):
                            P.op("dve", lambda e, d=d, q=q: e.tensor_copy(out=NMB[:, d, q, :], in_=SC[:, 2 + d, :]), reads=[r_sc], writes=[r_sc])
                    P.dma("sp", lambda e: e.dma_start(out=sv[:, 0:2], in_=self.ssdv[l, g]), writes=[r_sv])
                    for d in range(2):
                        r0 = DT0 + d * 32 + g * 8
                        P.dma("sp", lambda e, d=d, r0=r0: e.dma_start(out=dtT[d * 8:(d + 1) * 8, :], in_=PT[r0:r0 + 8, :]), writes=[r_dtT])
                    P.op("act", lambda e: e.activation(out=sv[:, 2:3], in_=sv[:, 1:2], func=AF.Exp), reads=[r_sv], writes=[r_sv])
                    P.op("dve", lambda e: e.tensor_scalar(out=sv[:, 3:4], in0=sv[:, 2:3], scalar1=-1.0, scalar2=None, op0=ALU.mult), reads=[r_sv], writes=[r_sv])
                    P.op("act", lambda e: e.activation(out=dtE[:], in_=dtT[:], func=AF.Exp, bias=sv[:, 0:1], scale=1.0), reads=[r_dtT, r_sv], writes=[r_dtE])
                    P.op("act", lambda e: e.activation(out=dtE[:], in_=dtE[:], func=AF.Ln, bias=1.0, scale=1.0), reads=[r_dtE], writes=[r_dtE])
                    P.op("dve", lambda e: e.tensor_scalar(out=dtA[:], in0=dtE[:], scalar1=sv[:, 3:4], scalar2=None, op0=ALU.mult), reads=[r_dtE, r_sv], writes=[r_dtA])
                    for src, r_src, dst in ((dtE, r_dtE, DT), (dtA, r_dtA, DTA)):
                        for c0 in range(0, NC, 32):
                            nn = min(32, NC - c0)
                            pst, r_ps = self.ps[(c0 // 32) % 2]
                            for c in range(nn):
                                P.op("pe", lambda e, pst=pst, src=src, c=c, c0=c0: e.matmul(
                                    pst[:, c * 16:(c + 1) * 16], lhsT=src[0:16, (c0 + c) * 128:(c0 + c + 1) * 128], rhs=self.ident[0:16, 0:16],
                                    start=True, stop=True), reads=[r_src, self.r_const], writes=[r_ps])
                            P.op("act", lambda e, pst=pst, dst=dst, c0=c0, nn=nn: e.activation(
                                out=dst[:, c0:c0 + nn, :], in_=pst[:, 0:nn * 16].rearrange("p (c h) -> p c h", h=16), func=AF.Copy),
                                reads=[r_ps], writes=[r_tab if dst is DT else r_DTA])
                    for dst, r_dst, lh in ((ACS, r_tab, None), (TOT, r_TOT, self.ones)):
                        for d in range(2):
                            for c0 in range(0, NC, 64):
                                nn = min(64, NC - c0)
                                pst, r_ps = self.ps[2 + d]
                                lhs = lh[:] if lh is not None else SC[:, d, :]
                                P.op("pe", lambda e, pst=pst, lhs=lhs, c0=c0, nn=nn, d=d: e.matmul(
                                    pst[:, 0:nn * 8], lhsT=lhs, rhs=DTA[:, c0:c0 + nn, d * 8:(d + 1) * 8], start=True, stop=True),
                                    reads=[r_DTA, r_sc, self.r_const], writes=[r_ps])
                                P.op("act", lambda e, pst=pst, dst=dst, c0=c0, nn=nn, d=d: e.activation(
                                    out=dst[:, c0:c0 + nn, d * 8:(d + 1) * 8], in_=pst[:, 0:nn * 8].rearrange("p (c h) -> p c h", h=8), func=AF.Copy),
                                    reads=[r_ps], writes=[r_dst])
                    P.op("act", lambda e: e.activation(out=EIN[:], in_=ACS[:], func=AF.Exp), reads=[r_tab], writes=[r_tab])
                    P.op("act", lambda e: e.activation(out=DEC[:], in_=TOT[:], func=AF.Exp), reads=[r_TOT], writes=[r_tab])
                    P.op("dve", lambda e: e.tensor_tensor(out=EOW[:], in0=TOT[:], in1=ACS[:], op=ALU.subtract), reads=[r_TOT, r_tab], writes=[r_tab])
                    P.op("act", lambda e: e.activation(out=EOW[:], in_=EOW[:], func=AF.Exp), reads=[r_tab], writes=[r_tab])
                    P.op("dve", lambda e: e.tensor_tensor(out=EOW[:], in0=EOW[:], in1=DT[:], op=ALU.mult), reads=[r_tab], writes=[r_tab])
                    P.flush()

                SB_ = 512

                def load_x(st_, tiles, t0, names):
                    for nm, ci in names:
                        t, r_t = tiles[nm]
                        P.dma("sp", lambda e, t=t, ci=ci, t0=t0: e.dma_start(out=t[:], in_=self.XCd[ci, :, t0:t0 + SB_]), writes=[r_t])

                def xtm(tiles, cc, Xs, r_Xs, Bt, r_Bt):
                    ps0, r_ps0 = self.ps[0]
                    ps1, r_ps1 = self.ps[1]
                    for j in range(4):
                        t, r_t = tiles["x%d" % j]
                        P.op("pe", lambda e, t=t, j=j: e.matmul(ps0[:, j * 128:(j + 1) * 128], lhsT=t[:, cc * 128:(cc + 1) * 128], rhs=self.ident[:],
                                                               start=True, stop=True), reads=[r_t, self.r_const], writes=[r_ps0])
                    P.op("act", lambda e: e.activation(out=Xs[:], in_=ps0[:], func=AF.Copy), reads=[r_ps0], writes=[r_Xs])
                    t, r_t = tiles["B"]
                    P.op("pe", lambda e, t=t: e.matmul(ps1[:, 0:128], lhsT=t[:, cc * 128:(cc + 1) * 128], rhs=self.ident[:], start=True, stop=True),
                         reads=[r_t, self.r_const], writes=[r_ps1])
                    P.op("act", lambda e: e.activation(out=Bt[:], in_=ps1[:, 0:128], func=AF.Copy), reads=[r_ps1], writes=[r_Bt])

                with contextlib.ExitStack() as st:
                    tl = [{nm: (self.sb(st, "s1%s%d" % (nm, i), [128, SB_], F32), Res()) for nm in ("x0", "x1", "x2", "x3", "B")} for i in range(2)]
                    Xs2 = [(self.sb(st, "s1X%d" % i, [128, 512], F32), Res()) for i in range(2)]
                    Bt2 = [(self.sb(st, "s1Bt%d" % i, [128, 128], BF16), Res()) for i in range(2)]
                    xw2 = [(self.sb(st, "s1w%d" % i, [128, 512], BF16), Res()) for i in range(2)]
                    Sb = self.sb(st, "s1S", [128, 512], F32)
                    pv = [(self.sb(st, "s1pv%d" % i, [128, 512], BF16), Res()) for i in range(2)]
                    tmp = self.sb(st, "s1T", [128, 512], F32)
                    r_Sb, r_tmp = Res(), Res()
                    P.op("dve", lambda e: e.memset(Sb[:], 0.0), writes=[r_Sb])
                    k = 0
                    for sbi in reversed(range(L // SB_)):
                        tiles = tl[sbi % 2]
                        load_x(st, tiles, sbi * SB_, [("x0", 0), ("x1", 1), ("x2", 2), ("x3", 3), ("B", 4)])
                        for cc in reversed(range(4)):
                            c = sbi * 4 + cc
                            (Xs, r_Xs), (Bt, r_Bt), (xw, r_xw) = Xs2[k % 2], Bt2[k % 2], xw2[k % 2]
                            k += 1
                            xtm(tiles, cc, Xs, r_Xs, Bt, r_Bt)
                            P.op("dve", lambda e, Xs=Xs, xw=xw, c=c: e.tensor_tensor(
                                out=xw[:].rearrange("p (h q) -> p h q", q=64), in0=Xs[:].rearrange("p (h q) -> p h q", q=64),
                                in1=bc3(EOW[:, c, 8:16], 64), op=ALU.mult), reads=[r_Xs, r_tab], writes=[r_xw])
                            ps2, r_ps2 = self.ps[2 + k % 2]
                            P.op("pe", lambda e, ps2=ps2, Bt=Bt, xw=xw: e.matmul(ps2[:], lhsT=Bt[:], rhs=xw[:], start=True, stop=True),
                                 reads=[r_Bt, r_xw], writes=[r_ps2])
                            pvt, r_pv = pv[k % 2]
                            P.op("act", lambda e, pvt=pvt: e.activation(out=pvt[:], in_=Sb[:], func=AF.Copy), reads=[r_Sb], writes=[r_pv])
                            P.dma("sp", lambda e, pvt=pvt, c=c: e.dma_start(out=self.PRVd[c], in_=pvt[:]), reads=[r_pv])
                            P.op("dve", lambda e, c=c: e.tensor_tensor(
                                out=tmp[:].rearrange("p (h q) -> p h q", q=64), in0=Sb[:].rearrange("p (h q) -> p h q", q=64),
                                in1=bc3(DEC[:, c, 8:16], 64), op=ALU.mult), reads=[r_Sb, r_tab], writes=[r_tmp])
                            P.op("dve", lambda e, ps2=ps2: e.tensor_tensor(out=Sb[:], in0=tmp[:], in1=ps2[:], op=ALU.add),
                                 reads=[r_tmp, r_ps2], writes=[r_Sb])
                    P.flush()

                with contextlib.ExitStack() as st:
                    names = ("x0", "x1", "x2", "x3", "B", "C", "z0", "z1", "z2", "z3")
                    tl = [{nm: (self.sb(st, "s2%s%d" % (nm, i), [128, SB_], F32), Res()) for nm in names} for i in range(2)]
                    Bb = [(self.sb(st, "s2Bb%d" % i, [128, SB_], BF16), Res()) for i in range(2)]
                    Cb = [(self.sb(st, "s2Cb%d" % i, [128, SB_], BF16), Res()) for i in range(2)]
                    Xs2 = [(self.sb(st, "s2X%d" % i, [128, 512], F32), Res()) for i in range(2)]
                    Bt2 = [(self.sb(st, "s2Bt%d" % i, [128, 128], BF16), Res()) for i in range(2)]
                    xd2 = [[(self.sb(st, "s2d%d%d" % (i, d), [128, 512], BF16), Res()) for d in range(2)] for i in range(2)]
                    xw2 = [(self.sb(st, "s2w%d" % i, [128, 512], BF16), Res()) for i in range(2)]
                    CBT = [(self.sb(st, "s2CB%d" % i, [128, 128], F32), Res()) for i in range(2)]
                    RD = [(self.sb(st, "s2RD%d" % i, [128, 16, 128], F32), Res()) for i in range(1)]
                    ARG = [(self.sb(st, "s2AR%d" % i, [128, 512], F32), Res()) for i in range(2)]
                    EX = [(self.sb(st, "s2EX%d" % i, [128, 512], F32), Res()) for i in range(2)]
                    MT = [(self.sb(st, "s2MT%d" % i, [128, 16, 128], BF16), Res()) for i in range(2)]
                    T1 = self.sb(st, "s2T1", [128, 512], F32)
                    T2 = self.sb(st, "s2T2", [128, 512], F32)
                    T3 = self.sb(st, "s2T3", [128, 512], F32)
                    YTM = [(self.sb(st, "s2Y%d" % i, [128, 512], F32), Res()) for i in range(2)]
                    YF = [(self.sb(st, "s2YF%d" % i, [128, 4, 512], F32), Res()) for i in range(1)]
                    Sf = self.sb(st, "s2S", [128, 512], F32)
                    pvl = [(self.sb(st, "s2pv%d" % i, [128, 512], BF16), Res()) for i in range(2)]
                    Sfb = self.sb(st, "s2Sb", [128, 512], BF16)
                    tmp = self.sb(st, "s2T", [128, 512], F32)
                    GZ = [(self.sb(st, "s2GZ%d" % i, [128, 512], F32), Res()) for i in range(2)]
                    YG = [(self.sb(st, "s2YG%d" % i, [128, 512], F32), Res()) for i in range(4)]
                    SQ = [(self.sb(st, "s2SQ%d" % i, [128, 512], F32), Res()) for i in range(2)]
                    RS = self.sb(st, "s2RS", [128, 512], F32)
                    OUTS = [(self.sb(st, "s2O%d" % i, [128, 512], F32), Res()) for i in range(2)]
                    r_T1, r_T2, r_T3, r_Sf, r_Sfb, r_tmp, r_RS = [Res() for _ in range(7)]
                    P.op("dve", lambda e: e.memset(Sf[:], 0.0), writes=[r_Sf])
                    P.op("dve", lambda e: e.memset(Sfb[:], 0.0), writes=[r_Sfb])
                    k = 0
                    for sbi in range(L // SB_):
                        tiles = tl[sbi % 2]
                        load_x(st, tiles, sbi * SB_, [("x0", 0), ("x1", 1), ("x2", 2), ("x3", 3), ("B", 4), ("C", 5)])
                        for j in range(4):
                            t, r_t = tiles["z%d" % j]
                            r0 = Z0 + g * 512 + j * 128
                            P.dma("sp", lambda e, t=t, r0=r0, sbi=sbi: e.dma_start(out=t[:], in_=PT[r0:r0 + 128, sbi * SB_:(sbi + 1) * SB_]), writes=[r_t])
                        (bb, r_bb), (cb, r_cb) = Bb[sbi % 2], Cb[sbi % 2]
                        P.op("act", lambda e, bb=bb, tiles=tiles: e.activation(out=bb[:], in_=tiles["B"][0][:], func=AF.Copy), reads=[tiles["B"][1]], writes=[r_bb])
                        P.op("act", lambda e, cb=cb, tiles=tiles: e.activation(out=cb[:], in_=tiles["C"][0][:], func=AF.Copy), reads=[tiles["C"][1]], writes=[r_cb])
                        yf, r_yf = YF[0]
                        for cc in range(4):
                            c = sbi * 4 + cc
                            i2 = k % 2
                            k += 1
                            (Xs, r_Xs), (Bt, r_Bt), (xw, r_xw) = Xs2[i2], Bt2[i2], xw2[i2]
                            (xdf, r_xdf), (xdb, r_xdb) = xd2[i2]
                            (cbt, r_cbt), (rd, r_rd), (mt, r_mt), (ytm, r_ytm) = CBT[i2], RD[0], MT[i2], YTM[i2]
                            xtm(tiles, cc, Xs, r_Xs, Bt, r_Bt)
                            pvt, r_pv = pvl[i2]
                            P.dma("sp", lambda e, pvt=pvt, c=c: e.dma_start(out=pvt[:], in_=self.PRVd[c]), writes=[r_pv])
                            X3 = Xs[:].rearrange("p (h q) -> p h q", q=64)
                            for dst, r_dst, tabap in ((xdf, r_xdf, DT[:, c, 0:8]), (xdb, r_xdb, DT[:, c, 8:16]), (xw, r_xw, EOW[:, c, 0:8])):
                                P.op("dve", lambda e, dst=dst, X3=X3, tabap=tabap: e.tensor_tensor(
                                    out=dst[:].rearrange("p (h q) -> p h q", q=64), in0=X3, in1=bc3(tabap, 64), op=ALU.mult),
                                    reads=[r_Xs, r_tab], writes=[r_dst])
                            ps1, r_ps1 = self.ps[1]
                            P.op("pe", lambda e, bb=bb, cb=cb, cc=cc: e.matmul(ps1[:, 128:256], lhsT=bb[:, cc * 128:(cc + 1) * 128], rhs=cb[:, cc * 128:(cc + 1) * 128],
                                                                           start=True, stop=True), reads=[r_bb, r_cb], writes=[r_ps1])
                            P.op("act", lambda e, cbt=cbt: e.activation(out=cbt[:], in_=ps1[:, 128:256], func=AF.Copy), reads=[r_ps1], writes=[r_cbt])
                            P.op("dve", lambda e, rd=rd, c=c: e.tensor_tensor(
                                out=rd[:], in0=self.ident[:].rearrange("p (o t) -> p o t", o=1).broadcast_to([128, 16, 128]),
                                in1=bc3(ACS[:, c, :], 128), op=ALU.mult), reads=[self.r_const, r_tab], writes=[r_rd])
                            for q in range(4):
                                d = q // 2
                                psq, r_psq = self.ps[2 + q % 2]
                                (arg, r_arg), (ex, r_ex) = ARG[q % 2], EX[q % 2]
                                P.op("pe", lambda e, psq=psq, rd=rd, q=q: e.matmul(psq[:], lhsT=self.ones[:], rhs=rd[:, 4 * q:4 * q + 4, :], start=True, stop=False),
                                     reads=[r_rd, self.r_const], writes=[r_psq])
                                P.op("pe", lambda e, psq=psq, d=d: e.matmul(psq[:], lhsT=self.identb[:], rhs=NMB[:, d, :, :], start=False, stop=True),
                                     reads=[r_sc, self.r_const], writes=[r_psq])
                                P.op("dve", lambda e, psq=psq, arg=arg, c=c, q=q: e.tensor_tensor(
                                    out=arg[:].rearrange("p (h t) -> p h t", t=128), in0=psq[:].rearrange("p (h t) -> p h t", t=128),
                                    in1=bc3(ACS[:, c, 4 * q:4 * q + 4], 128), op=ALU.subtract), reads=[r_psq, r_tab], writes=[r_arg])
                                P.op("act", lambda e, arg=arg, ex=ex: e.activation(out=ex[:], in_=arg[:], func=AF.Exp), reads=[r_arg], writes=[r_ex])
                                P.op("dve", lambda e, ex=ex, mt=mt, cbt=cbt, q=q: e.tensor_tensor(
                                    out=mt[:, 4 * q:4 * q + 4, :], in0=ex[:].rearrange("p (h t) -> p h t", t=128),
                                    in1=cbt[:].rearrange("p (o t) -> p o t", o=1).broadcast_to([128, 4, 128]), op=ALU.mult),
                                    reads=[r_ex, r_cbt], writes=[r_mt])
                            psY = [self.ps[4], self.ps[5]]
                            for hd in range(16):
                                d, h = hd // 8, hd % 8
                                xd, r_xd = (xdf, r_xdf) if d == 0 else (xdb, r_xdb)
                                P.op("pe", lambda e, d=d, h=h, hd=hd, mt=mt, xd=xd: e.matmul(
                                    psY[d][0][:, h * 64:(h + 1) * 64], lhsT=mt[:, hd, :], rhs=xd[:, h * 64:(h + 1) * 64], start=True, stop=True),
                                    reads=[r_mt, r_xd], writes=[psY[d][1]])
                            psF, r_psF = self.ps[6]
                            psB, r_psB = self.ps[7]
                            P.op("pe", lambda e, cb=cb, cc=cc: e.matmul(psF[:], lhsT=cb[:, cc * 128:(cc + 1) * 128], rhs=Sfb[:], start=True, stop=True),
                                 reads=[r_cb, r_Sfb], writes=[r_psF])
                            P.op("pe", lambda e, cb=cb, cc=cc, pvt=pvt: e.matmul(psB[:], lhsT=cb[:, cc * 128:(cc + 1) * 128], rhs=pvt[:], start=True, stop=True),
                                 reads=[r_cb, r_pv], writes=[r_psB])
                            v3 = lambda ap: ap.rearrange("p (h q) -> p h q", q=64)
                            P.op("dve", lambda e, c=c: e.tensor_tensor(out=v3(T1[:]), in0=v3(psF[:]), in1=bc3(EIN[:, c, 0:8], 64), op=ALU.mult),
                                 reads=[r_psF, r_tab], writes=[r_T1])
                            P.op("dve", lambda e: e.tensor_tensor(out=T1[:], in0=T1[:], in1=psY[0][0][:], op=ALU.add), reads=[psY[0][1], r_T1], writes=[r_T1])
                            P.op("dve", lambda e, c=c: e.tensor_tensor(out=v3(T2[:]), in0=v3(psB[:]), in1=bc3(EIN[:, c, 8:16], 64), op=ALU.mult),
                                 reads=[r_psB, r_tab], writes=[r_T2])
                            P.op("dve", lambda e: e.tensor_tensor(out=T2[:], in0=T2[:], in1=psY[1][0][:], op=ALU.add), reads=[psY[1][1], r_T2], writes=[r_T2])
                            P.op("pool", lambda e, X3=X3: e.tensor_tensor(out=v3(T3[:]), in0=X3, in1=bc3(DSK[:, 0:8], 64), op=ALU.mult),
                                 reads=[r_Xs, r_sc], writes=[r_T3])
                            P.op("pool", lambda e: e.tensor_tensor(out=T3[:], in0=T3[:], in1=T1[:], op=ALU.add), reads=[r_T1, r_T3], writes=[r_T3])
                            P.op("pool", lambda e, ytm=ytm: e.tensor_tensor(out=ytm[:], in0=T3[:], in1=T2[:], op=ALU.add), reads=[r_T2, r_T3], writes=[r_ytm])
                            ps0, r_ps0 = self.ps[0]
                            P.op("pe", lambda e, Bt=Bt, xw=xw: e.matmul(ps0[:], lhsT=Bt[:], rhs=xw[:], start=True, stop=True), reads=[r_Bt, r_xw], writes=[r_ps0])
                            P.op("dve", lambda e, c=c: e.tensor_tensor(out=v3(tmp[:]), in0=v3(Sf[:]), in1=bc3(DEC[:, c, 0:8], 64), op=ALU.mult),
                                 reads=[r_Sf, r_tab], writes=[r_tmp])
                            P.op("dve", lambda e: e.tensor_tensor(out=Sf[:], in0=tmp[:], in1=ps0[:], op=ALU.add), reads=[r_tmp, r_ps0], writes=[r_Sf])
                            P.op("act", lambda e: e.activation(out=Sfb[:], in_=Sf[:], func=AF.Copy), reads=[r_Sf], writes=[r_Sfb])
                            for j in range(4):
                                P.op("pe", lambda e, ytm=ytm, j=j: e.matmul(ps1[:, j * 128:(j + 1) * 128] if False else self.ps[1][0][:, j * 128:(j + 1) * 128],
                                                                           lhsT=ytm[:, j * 128:(j + 1) * 128], rhs=self.ident[:], start=True, stop=True),
                                     reads=[r_ytm, self.r_const], writes=[r_ps1])
                            P.op("act", lambda e, yf=yf, cc=cc: e.activation(out=yf[:, :, cc * 128:(cc + 1) * 128],
                                                                          in_=ps1[:].rearrange("p (j t) -> p j t", t=128), func=AF.Copy),
                                 reads=[r_ps1], writes=[r_yf])
                        ps7, r_ps7 = self.ps[7]
                        for j in range(4):
                            (gz, r_gz), (yg, r_yg), (sq, r_sq) = GZ[j % 2], YG[j], SQ[j % 2]
                            zt, r_zt = tiles["z%d" % j]
                            P.op("act", lambda e, gz=gz, zt=zt: e.activation(out=gz[:], in_=zt[:], func=AF.Silu), reads=[r_zt], writes=[r_gz])
                            P.op("pool", lambda e, yg=yg, gz=gz, yf=yf, j=j: e.tensor_tensor(out=yg[:], in0=yf[:, j, :], in1=gz[:], op=ALU.mult),
                                 reads=[r_yf, r_gz], writes=[r_yg])
                            P.op("act", lambda e, sq=sq, yg=yg: e.activation(out=sq[:], in_=yg[:], func=AF.Square), reads=[r_yg], writes=[r_sq])
                            P.op("pe", lambda e, sq=sq, j=j: e.matmul(ps7[:], lhsT=self.ones[:], rhs=sq[:], start=(j == 0), stop=(j == 3)),
                                 reads=[r_sq, self.r_const], writes=[r_ps7])
                        P.op("act", lambda e: e.activation(out=RS[:], in_=ps7[:], func=AF.Sqrt, bias=EPS, scale=1.0 / 512), reads=[r_ps7], writes=[r_RS])
                        P.op("dve", lambda e: e.reciprocal(out=RS[:], in_=RS[:]), reads=[r_RS], writes=[r_RS])
                        for j in range(4):
                            (o_, r_o), (yg, r_yg) = OUTS[j % 2], YG[j]
                            col = g * 4 + j
                            P.op("dve", lambda e, o_=o_, yg=yg, col=col: e.scalar_tensor_tensor(
                                out=o_[:], in0=yg[:], scalar=NG[:, col:col + 1], in1=RS[:], op0=ALU.mult, op1=ALU.mult),
                                reads=[r_yg, r_RS, r_sc], writes=[r_o])
                            r0 = 1536 + g * 512 + j * 128
                            P.dma("sp", lambda e, o_=o_, r0=r0, sbi=sbi: e.dma_start(out=YT[r0:r0 + 128, sbi * SB_:(sbi + 1) * SB_], in_=o_[:]), reads=[r_o])
                    P.flush()


    def emit_merge(self, l, PT, YT, MT):
        nc, P, L = self.nc, self.P, self.L
        if l == 0:
            self.mergew = self.din("mergew", [self.depth, 32, 128, 44 * 128])
            self.gateb = self.din("gateb", [self.depth, 128, 96])
        KB = (12, 16, 4)
        YOFF = (0, 12, 28)
        WOFF = (12, 24, 40)
        for tt in range(L // TT):
            t0 = tt * TT
            with contextlib.ExitStack() as st:
                yT = self.sb(st, "myT", [128, 32, TT], BF16)
                gT = self.sb(st, "mgT", [128, 4, TT], BF16)
                gb = self.sb(st, "mgb", [128, 96], F32)
                r_y = [Res() for _ in range(32)]
                r_g = [Res() for _ in range(4)]
                r_gb = Res()
                P.dma("sp", lambda e: e.dma_start(out=gb[:], in_=self.gateb[l]), writes=[r_gb])
                for c in range(32):
                    P.dma("pool", lambda e, c=c: e.dma_start(out=yT[:, c, :], in_=YT[c * 128:(c + 1) * 128, t0:t0 + TT]), writes=[r_y[c]])
                for c in range(4):
                    P.dma("pool", lambda e, c=c: e.dma_start(out=gT[:, c, :], in_=PT[GL0 + c * 128:GL0 + (c + 1) * 128, t0:t0 + TT]), writes=[r_g[c]])
                ring = [(self.sb(st, "mw%d" % i, [128, 44 * 128], BF16), Res()) for i in range(2)]
                sig = [(self.sb(st, "msg%d" % i, [128, 512], F32), Res()) for i in range(2)]
                acc = [(self.sb(st, "mac%d" % i, [128, 512], F32), Res()) for i in range(2)]
                mo = [(self.sb(st, "mo%d" % i, [128, 512], BF16), Res()) for i in range(2)]
                k = 0
                kk = 0
                for n in range(32):
                    wt, r_w = ring[n % 2]
                    P.dma("pool", lambda e, wt=wt, n=n: e.dma_start(out=wt[:], in_=self.mergew[l, n]), writes=[r_w])
                    for tb in range(TT // 512):
                        (a_, r_a), (o_, r_o) = acc[kk % 2], mo[kk % 2]
                        kk += 1
                        for br in range(3):
                            psG, r_psG = self.ps[(k % 4) * 2]
                            psY, r_psY = self.ps[(k % 4) * 2 + 1]
                            sg, r_sg = sig[k % 2]
                            k += 1
                            for kc in range(4):
                                P.op("pe", lambda e, psG=psG, wt=wt, br=br, kc=kc, tb=tb: e.matmul(
                                    psG[:], lhsT=wt[:, (br * 4 + kc) * 128:(br * 4 + kc + 1) * 128], rhs=gT[:, kc, tb * 512:(tb + 1) * 512],
                                    start=(kc == 0), stop=(kc == 3)), reads=[r_w, r_g[kc]], writes=[r_psG])
                            for kc in range(KB[br]):
                                P.op("pe", lambda e, psY=psY, wt=wt, br=br, kc=kc, tb=tb: e.matmul(
                                    psY[:], lhsT=wt[:, (WOFF[br] + kc) * 128:(WOFF[br] + kc + 1) * 128], rhs=yT[:, YOFF[br] + kc, tb * 512:(tb + 1) * 512],
                                    start=(kc == 0), stop=(kc == KB[br] - 1)), reads=[r_w, r_y[YOFF[br] + kc]], writes=[r_psY])
                            col = br * 32 + n
                            P.op("act", lambda e, sg=sg, psG=psG, col=col: e.activation(out=sg[:], in_=psG[:], func=AF.Sigmoid, bias=gb[:, col:col + 1], scale=1.0),
                                 reads=[r_psG, r_gb], writes=[r_sg])
                            if br == 0:
                                P.op("dve", lambda e, a_=a_, sg=sg, psY=psY: e.tensor_tensor(out=a_[:], in0=psY[:], in1=sg[:], op=ALU.mult),
                                     reads=[r_psY, r_sg], writes=[r_a])
                            else:
                                P.op("dve", lambda e, sg=sg, psY=psY: e.tensor_tensor(out=sg[:], in0=psY[:], in1=sg[:], op=ALU.mult),
                                     reads=[r_psY, r_sg], writes=[r_sg])
                                dst, r_dst = (a_, r_a) if br == 1 else (o_, r_o)
                                P.op("pool", lambda e, a_=a_, sg=sg, dst=dst: e.tensor_tensor(out=dst[:], in0=a_[:], in1=sg[:], op=ALU.add),
                                     reads=[r_sg, r_a], writes=[r_dst])
                        P.dma("sp", lambda e, o_=o_, n=n, c0=t0 + tb * 512: e.dma_start(out=MT[n * 128:(n + 1) * 128, c0:c0 + 512], in_=o_[:]), reads=[r_o])
                P.flush()

    def emit_res_linear(self, l, HT, KC, T, wsl, xT, gcol0, cast):
        nc, P, L = self.nc, self.P, self.L
        for tt in range(L // T):
            t0 = tt * T
            with contextlib.ExitStack() as st:
                hT = self.sb(st, "rhT", [128, KC, T], BF16)
                r_h = [Res() for _ in range(KC)]
                for c in range(KC):
                    P.dma("pool" if cast else "sp", lambda e, c=c: e.dma_start(out=hT[:, c, :], in_=HT[c * 128:(c + 1) * 128, t0:t0 + T]), writes=[r_h[c]])
                xo = [(self.sb(st, "rxo%d" % i, [128, 512], F32), Res()) for i in range(4)]
                xn = [(self.sb(st, "rxn%d" % i, [128, 512], F32), Res()) for i in range(4)]
                k = [0]

                def epi(n, tb, pst, r_ps, width):
                    (o_, r_o), (n_, r_n) = xo[k[0] % 4], xn[k[0] % 4]
                    k[0] += 1
                    c0 = t0 + tb * 512
                    P.dma("sp", lambda e: e.dma_start(out=o_[:], in_=xT[n * 128:(n + 1) * 128, c0:c0 + 512]), writes=[r_o])
                    P.op("dve", lambda e: e.scalar_tensor_tensor(out=n_[:], in0=pst[:], scalar=self.modL[:, gcol0 + n:gcol0 + n + 1], in1=o_[:],
                                                                op0=ALU.mult, op1=ALU.add), reads=[r_ps, r_o, self.r_modL], writes=[r_n])
                    P.dma("sp", lambda e: e.dma_start(out=xT[n * 128:(n + 1) * 128, c0:c0 + 512], in_=n_[:]), reads=[r_n])
                self.emit_linear(st, hT, r_h, KC, T, wsl, [128] * 32, epi, nring=(3 if KC <= 32 else 2), tag="rw")
                P.flush()

    def emit_ffn(self, l, xT, UT, GT):
        nc, P, L = self.nc, self.P, self.L
        if l == 0:
            self.mlp_up = self.din("mlp_up", [self.depth, 128, 128, D])
            self.mlp_down = self.din("mlp_down", [self.depth, 32, 128, MLP_H])
            self.mconv = self.din("mconv", [self.depth, 128, 128, 4])
        for tt in range(L // TT):
            t0 = tt * TT
            with contextlib.ExitStack() as st:
                hT = self.sb(st, "fhT", [128, KCD, TT], BF16)
                r_h = [Res() for _ in range(KCD)]
                with contextlib.ExitStack() as st2:
                    self.emit_norm(st2, xT, None, t0, TT, hT, r_h, 32, 96)
                    P.flush()
                stg = [(self.sb(st, "fstg%d" % i, [128, 512], F32), Res()) for i in range(4)]
                k = [0]

                def epi(n, tb, pst, r_ps, width):
                    s_, r_s = stg[k[0] % 4]
                    k[0] += 1
                    P.op("act", lambda e: e.activation(out=s_[:], in_=pst[:], func=AF.Copy), reads=[r_ps], writes=[r_s])
                    P.dma("sp", lambda e: e.dma_start(out=UT[n * 128:(n + 1) * 128, t0 + tb * 512:t0 + (tb + 1) * 512], in_=s_[:]), reads=[r_s])
                self.emit_linear(st, hT, r_h, KCD, TT, self.mlp_up[l], [128] * 128, epi, tag="fw")
                P.flush()
        TS = min(L, 4096)
        with contextlib.ExitStack() as st:
            cw = self.sb(st, "fcw", [128, 128, 4], F32)
            r_cw = Res()
            P.dma("sp", lambda e: e.dma_start(out=cw[:], in_=self.mconv[l]), writes=[r_cw])
            Uu = [(self.sb(st, "fU%d" % i, [128, TS + 2], F32), Res()) for i in range(2)]
            Uv = [(self.sb(st, "fV%d" % i, [128, TS + 2], F32), Res()) for i in range(2)]
            Au = [(self.sb(st, "fAu%d" % i, [128, TS], F32), Res()) for i in range(2)]
            Av = [(self.sb(st, "fAv%d" % i, [128, TS], F32), Res()) for i in range(2)]
            Go = [(self.sb(st, "fG%d" % i, [128, TS], BF16), Res()) for i in range(2)]
            Tv, r_Tv = self.sb(st, "fTv", [128, TS], F32), Res()
            k = 0
            for j in range(64):
                for t0 in range(0, L, TS):
                    i2 = k % 2
                    k += 1
                    lo, hi = max(0, t0 - 1), min(L, t0 + TS + 1)
                    for (u, r_u), (a, r_a), ch, eng in ((Uu[i2], Au[i2], j, "dve"), (Uv[i2], Av[i2], 64 + j, "pool")):
                        if lo > t0 - 1:
                            P.op(eng, lambda e, u=u: e.memset(u[:, 0:1], 0.0), writes=[r_u])
                        if hi < t0 + TS + 1:
                            P.op(eng, lambda e, u=u: e.memset(u[:, TS + 1:TS + 2], 0.0), writes=[r_u])
                        P.dma("sp", lambda e, u=u, ch=ch, lo=lo, hi=hi, t0=t0: e.dma_start(
                            out=u[:, lo - (t0 - 1):hi - (t0 - 1)], in_=UT[ch * 128:(ch + 1) * 128, lo:hi]), writes=[r_u])
                        P.op(eng, lambda e, u=u, a=a, ch=ch: e.tensor_scalar(out=a[:], in0=u[:, 0:TS], scalar1=cw[:, ch, 0:1], scalar2=cw[:, ch, 3:4],
                                                                            op0=ALU.mult, op1=ALU.add), reads=[r_u, r_cw], writes=[r_a])
                        for tap in (1, 2):
                            if eng == "dve":
                                P.op(eng, lambda e, u=u, a=a, ch=ch, tap=tap: e.scalar_tensor_tensor(
                                    out=a[:], in0=u[:, tap:tap + TS], scalar=cw[:, ch, tap:tap + 1], in1=a[:], op0=ALU.mult, op1=ALU.add),
                                    reads=[r_u, r_cw, r_a], writes=[r_a])
                            else:
                                P.op(eng, lambda e, u=u, ch=ch, tap=tap: e.tensor_scalar(out=Tv[:], in0=u[:, tap:tap + TS], scalar1=cw[:, ch, tap:tap + 1],
                                                                                        scalar2=None, op0=ALU.mult), reads=[r_u, r_cw], writes=[r_Tv])
                                P.op(eng, lambda e, a=a: e.tensor_tensor(out=a[:], in0=a[:], in1=Tv[:], op=ALU.add), reads=[r_Tv, r_a], writes=[r_a])
                    (au, r_au), (av, r_av), (go, r_go) = Au[i2], Av[i2], Go[i2]
                    P.op("act", lambda e, au=au: e.activation(out=au[:], in_=au[:], func=AF.Silu), reads=[r_au], writes=[r_au])
                    P.op("dve", lambda e, au=au, av=av, go=go: e.tensor_tensor(out=go[:], in0=au[:], in1=av[:], op=ALU.mult), reads=[r_au, r_av], writes=[r_go])
                    P.dma("sp", lambda e, go=go, j=j, t0=t0: e.dma_start(out=GT[j * 128:(j + 1) * 128, t0:t0 + TS], in_=go[:]), reads=[r_go])
            P.flush()
        self.emit_res_linear(l, GT, 64, 1024, self.mlp_down[l], xT, 160, cast=False)

    def emit_model(self, xin, xT):
        nc, P, L = self.nc, self.P, self.L
        for c in range(KCD):
            P.dma("sp", lambda e, c=c: e.dma_start(out=xT[c * 128:(c + 1) * 128, :], in_=xin[c * 128:(c + 1) * 128, :]))
        self.emit_adaln()
        PT = RowSplit(self.nc, "PT", [0, XBC0, QKV0, N_IN], L, F32)
        YT = self.nc.dram_tensor("YT", [D, L], F32).ap()
        MT = self.nc.dram_tensor("MT", [D, L], BF16).ap()
        UT = RowSplit(self.nc, "UT", [0, 4096, 8192, 12288, 16384], L, F32)
        GT = self.nc.dram_tensor("GT", [MLP_H, L], BF16).ap()
        for l in range(self.depth):
            self.emit_layer_mod(l)
            self.emit_inproj(l, xT, None, PT, None)
            self.emit_pool(l, PT, YT)
            self.emit_ssd(l, PT, YT)
            self.emit_attn(l, PT, YT)
            self.emit_merge(l, PT, YT, MT)
            if l == 0:
                self.w_out = self.din("w_out", [self.depth, 32, 128, D])
            self.emit_res_linear(l, MT, KCD, TT, self.w_out[l], xT, 64, cast=False)
            self.emit_ffn(l, xT, UT, GT)
        P.flush(final=True)


def host_pool(inp, L, depth):
    pos = np.arange(L)
    ic = []
    for win in POOL_WINDOWS:
        lo = np.clip(pos - win // 2, 0, L)
        hi = np.clip(pos + win - win // 2, 0, L)
        ic.append(np.broadcast_to((1.0 / (hi - lo).astype(np.float32))[None, :], (128, L)))
    return {
        "pool_w": np.ascontiguousarray(inp["pool_w"][:depth]),
        "poolsc": np.stack([cols(inp["pool_scale"][l]) for l in range(depth)]),
        "invcnt": np.ascontiguousarray(np.stack(ic)).astype(np.float32),
    }


def _t5_buckets(rel):
    half, max_exact = 16, 8
    n = np.abs(rel)
    large = max_exact + (np.log(np.maximum(n, max_exact) / max_exact) / np.log(1024 / max_exact) * (half - max_exact)).astype(np.int32)
    large = np.minimum(large, half - 1)
    return (rel > 0).astype(np.int32) * half + np.where(n < max_exact, n, large).astype(np.int32)


def host_attn(inp, L, depth):
    q = np.arange(128)[:, None]
    j = np.arange(256)[None, :]
    delta = j - 64 - q
    inwin = np.abs(delta) <= 64
    out = np.empty((12, 128, 4, 256), np.float32)
    tab = inp["t5_table"]
    for gi, dil in enumerate((1, 4, 16)):
        bk = _t5_buckets(delta * dil)
        for hh in range(4):
            vals = tab[bk, gi * 4 + hh].astype(np.float32)
            for var in range(4):
                ok = inwin.copy()
                if var & 1:
                    ok &= (j >= 64)
                if var & 2:
                    ok &= (j < 192)
                out[gi * 4 + hh, :, var, :] = np.where(ok, vals, np.float32(-1e30))
    return {
        "bmask": out,
        "qkn": np.stack([np.stack([inp["q_norm"][l], inp["k_norm"][l]], axis=1) for l in range(depth)]).astype(np.float32),
    }


def host_ssd(inp, L, depth):
    k = np.arange(128)[:, None]
    t = np.arange(128)[None, :]
    sc = np.stack([(k <= t).astype(np.float32), (k >= t).astype(np.float32),
                   np.where(k <= t, 0.0, -30000.0).astype(np.float32), np.where(k >= t, 0.0, -30000.0).astype(np.float32)], axis=1)
    ssdv = np.empty((depth, 4, 16, 2), np.float32)
    convw = np.empty((depth, 128, 24, 6), np.float32)
    dskip = np.empty((depth, 4, 128, 8), np.float32)
    for l in range(depth):
        for g in range(4):
            for d in range(2):
                ssdv[l, g, d * 8:(d + 1) * 8, 0] = inp["ssd_dt_bias"][l, d, g * 8:(g + 1) * 8]
                ssdv[l, g, d * 8:(d + 1) * 8, 1] = inp["ssd_a_log"][l, d, g * 8:(g + 1) * 8]
            dskip[l, g] = np.broadcast_to(inp["ssd_d"][l, g * 8:(g + 1) * 8][None, :], (128, 8))
        for tap in range(5):
            convw[l, :, :, tap] = cols(inp["ssd_conv_w"][l, tap])
        convw[l, :, :, 5] = cols(inp["ssd_conv_b"][l])
    return {"ssdc": np.ascontiguousarray(sc), "ssdv": ssdv, "convw": convw, "dskip": dskip,
            "ssdn": np.stack([cols(inp["ssd_norm"][l]) for l in range(depth)])}


def host_dense(inp, L, depth):
    out = {}
    w_in = inp["w_in"][:depth]
    wpad = np.concatenate([w_in[:, :, :DT0 + 64], np.zeros((depth, D, 64), np.float32), w_in[:, :, DT0 + 64:]], axis=2)
    out["w_in"] = np.stack([slabs(wpad[l], KCD) for l in range(depth)])
    mw = np.empty((depth, 32, 128, 44 * 128), np.float32)
    for l in range(depth):
        gu = inp["gate_up"][l]
        parts = [slabs(gu[:, br * D:(br + 1) * D], 4) for br in range(3)]
        parts += [slabs(inp["proj_a"][l], 12), slabs(inp["proj_b"][l], 16), slabs(inp["proj_c"][l], 4)]
        mw[l] = np.concatenate(parts, axis=2)
    out["mergew"] = mw
    out["gateb"] = np.stack([cols(inp["gate_b"][l]) for l in range(depth)])
    out["w_out"] = np.stack([slabs(inp["w_out"][l], KCD) for l in range(depth)])
    out["mlp_up"] = np.stack([slabs(inp["mlp_up"][l], KCD) for l in range(depth)])
    out["mlp_down"] = np.stack([slabs(inp["mlp_down"][l], 64) for l in range(depth)])
    mc = np.empty((depth, 128, 128, 4), np.float32)
    for l in range(depth):
        for tap in range(3):
            mc[l, :, :, tap] = cols(inp["mlp_conv_w"][l, tap])
        mc[l, :, :, 3] = cols(inp["mlp_conv_b"][l])
    out["mconv"] = mc
    return out


def host_common(inp, b, depth):
    return {
        "ident_in": np.eye(128, dtype=np.float32),
        "xin": np.ascontiguousarray(inp["x"][b].T),
        "ccol": cols(inp["c"][b]),
        "adaw": slabs(inp["ada_w"], KCD),
        "adab": cols(inp["ada_b"]),
        "adal": np.stack([cols(inp["ada_layer"][l]) for l in range(DEPTH)]),
        "gnorm": np.stack([np.concatenate([cols(inp["norm_mix"][l]), cols(inp["norm_mlp"][l])], axis=1) for l in range(DEPTH)]),
    }


_CACHE = {}


def build(L, depth):
    key = (L, depth)
    if key not in _CACHE:
        b = Builder(L, depth)
        b.setup()
        xin = b.din("xin", [D, L])
        xT = b.nc.dram_tensor("xT", [D, L], F32, kind="ExternalOutput").ap()
        b.emit_model(xin, xT)
        _CACHE[key] = b
    return _CACHE[key]


def run_model(inp, L, depth, trace=False):
    B = inp["x"].shape[0]
    b = build(L, depth)
    shared = {}
    shared.update(host_dense(inp, L, depth))
    shared.update(host_pool(inp, L, depth))
    shared.update(host_ssd(inp, L, depth))
    shared.update(host_attn(inp, L, depth))
    in_maps = []
    for bi in range(B):
        m = dict(shared)
        m.update(host_common(inp, bi, depth))
        in_maps.append(m)
    res = run_bass_kernel_spmd(b.nc, in_maps, core_ids=list(range(B)), trace=trace)
    out = np.stack([np.ascontiguousarray(res.results[bi]["xT"].T) for bi in range(B)])
    return out, res


def kernel(**inputs):
    inp = {k: np.asarray(v) for k, v in inputs.items()}
    out, _ = run_model(inp, SEQ, DEPTH)
    return out.astype(np.float32)
```

```python
import contextlib
import numpy as np
import concourse.bass as bass
import concourse.mybir as mybir
from concourse.bass_utils import run_bass_kernel_spmd

F32 = mybir.dt.float32
BF16 = mybir.dt.bfloat16
ALU = mybir.AluOpType
AF = mybir.ActivationFunctionType
AX = mybir.AxisListType

N_DMA_SEMS = 6


class Res:
    __slots__ = ("name", "w", "r")

    def __init__(self, name=""):
        self.name = name
        self.w = None
        self.r = []


class Prog:
    ENGS = ("pe", "act", "dve", "pool", "sp")

    def __init__(self, nc):
        self.nc = nc
        self.q = {e: [] for e in self.ENGS}
        self.cnt = {e: 0 for e in self.ENGS}
        self.waited = {e: {} for e in self.ENGS}
        self.dma_n = {e: 0 for e in ("sp", "pool", "act")}
        self.sems = {}
        self.dma_tokens = []

    def _need(self, eng, toks):
        out = []
        wd = self.waited[eng]
        for t in toks:
            if t is None:
                continue
            k, v = t
            if k == eng and eng == "pe":
                continue
            if wd.get(k, 0) >= v:
                continue
            wd[k] = v
            out.append((k, v))
        best = {}
        for k, v in out:
            best[k] = max(best.get(k, 0), v)
        return list(best.items())

    def _deps(self, reads, writes):
        toks = []
        for r in reads:
            toks.append(r.w)
        for w in writes:
            toks.append(w.w)
            toks.extend(w.r)
        return toks

    def _commit(self, tok, reads, writes):
        for r in reads:
            r.r.append(tok)
        for w in writes:
            w.w = tok
            w.r = []

    def op(self, eng, fn, reads=(), writes=()):
        waits = self._need(eng, self._deps(reads, writes))
        self.cnt[eng] += 1
        tok = (eng, self.cnt[eng])
        self.q[eng].append((waits, fn, ("c", eng)))
        self._commit(tok, reads, writes)
        return tok

    def dma(self, queue, fn, reads=(), writes=()):
        j = self.dma_n[queue]
        self.dma_n[queue] += 1
        s = j % N_DMA_SEMS
        semkey = (queue, s)
        val = 16 * (j // N_DMA_SEMS + 1)
        deps = self._deps(reads, writes)
        if j >= N_DMA_SEMS:
            deps.append((semkey, val - 16))
        waits = self._need(queue, deps)
        tok = (semkey, val)
        self.q[queue].append((waits, fn, ("d", semkey)))
        self._commit(tok, reads, writes)
        self.dma_tokens.append(tok)
        return tok

    def barrier(self, final=False):
        last = {}
        for k, v in self.dma_tokens:
            last[k] = max(last.get(k, 0), v)
        toks = list(last.items())
        for e in ("pe", "act", "dve", "pool"):
            if self.cnt[e]:
                toks.append((e, self.cnt[e]))
        for eng in (("sp",) if final else self.ENGS):
            waits = self._need(eng, [t for t in toks if t[0] != eng])
            if waits:
                self.q[eng].append((waits, None, None))

    def alloc_sems(self, st):
        nc = self.nc
        self.semh = {}
        for e in ("pe", "act", "dve", "pool"):
            self.semh[e] = st.enter_context(nc.semaphore("s_" + e))
        for qn in ("sp", "pool", "act"):
            for i in range(N_DMA_SEMS):
                self.semh[(qn, i)] = st.enter_context(nc.semaphore("d_%s%d" % (qn, i)))

    def flush(self, final=False):
        nc = self.nc
        self.barrier(final=False)
        if final:
            self.barrier(final=True)
        semh = self.semh
        q = self.q
        self.q = {e: [] for e in self.ENGS}

        def run(eng, items):
            for waits, fn, inc in items:
                for k, v in waits:
                    eng.wait_ge(semh[k], v)
                if fn is None:
                    continue
                ins = fn(eng)
                if inc[0] == "c":
                    ins.then_inc(semh[inc[1]], 1)
                else:
                    ins.then_inc(semh[inc[1]], 16)

        with nc.Block() as block:
            @block.sync
            def _(e):
                run(e, q["sp"])

            @block.tensor
            def _(e):
                run(e, q["pe"])

            @block.scalar
            def _(e):
                run(e, q["act"])

            @block.vector
            def _(e):
                run(e, q["dve"])

            @block.gpsimd
            def _(e):
                run(e, q["pool"])

D = 4096
KCD = D // 128
DEPTH = 4
SEQ = 8192
POOL_WINDOWS = (2, 4, 8, 16)
POOL_W = 1536
SSD_INNER = 2048
SSD_XBC = 3072
ATT_QKV = 4608
GATE_RANK = 512
N_IN = 11840
A0, Z0, XBC0, DT0, QKV0, GL0 = 0, 1536, 3584, 6656, 6720, 11328
MLP_H = 8192
EPS = 1e-6
W_IN_CHUNKS = [(128 * i, 128) for i in range(52)] + [(DT0, 64)] + [(QKV0 + 128 * i, 128) for i in range(40)]
TT = 2048


def slabs(W, KC):
    K, N = W.shape
    return np.ascontiguousarray(W.reshape(KC, 128, N // 128, 128).transpose(2, 1, 0, 3)).reshape(N // 128, 128, KC * 128)


def cols(v):
    return np.ascontiguousarray(v.reshape(-1, 128).T)


class RowSplit:
    def __init__(self, nc, name, bounds, L, dt):
        self.parts = []
        for i in range(len(bounds) - 1):
            r0, r1 = bounds[i], bounds[i + 1]
            self.parts.append((r0, r1, nc.dram_tensor("%s_%d" % (name, i), [r1 - r0, L], dt).ap()))

    def __getitem__(self, key):
        rs, cs = key
        for r0, r1, ap in self.parts:
            if r0 <= rs.start and rs.stop <= r1:
                return ap[rs.start - r0:rs.stop - r0, cs]
        raise IndexError((rs, cs))


class Builder:
    def __init__(self, L, depth, debug=None):
        self.L = L
        self.depth = depth
        self.debug = debug or ()
        self.nc = bass.Bass("TRN2", target_bir_lowering=False)
        self.P = Prog(self.nc)
        self.st = contextlib.ExitStack()
        self.dram = {}

    def din(self, name, shape, dt=F32):
        t = self.nc.dram_tensor(name, list(shape), dt, kind="ExternalInput").ap()
        self.dram[name] = t
        return t

    def dscr(self, name, shape, dt=F32):
        kind = "ExternalOutput" if name in self.debug else "Internal"
        t = self.nc.dram_tensor(name, list(shape), dt, kind=kind).ap()
        self.dram[name] = t
        return t, Res(name)

    def sb(self, st, name, shape, dt=F32):
        self._uid = getattr(self, "_uid", 0) + 1
        return st.enter_context(self.nc.sbuf_tensor("%s_%d" % (name, self._uid), list(shape), dt))

    def setup(self):
        nc, P, st = self.nc, self.P, self.st
        P.alloc_sems(st)
        self.ps = []
        for i in range(8):
            t = st.enter_context(nc.psum_tensor("ps%d" % i, [128, 512], F32))
            self.ps.append((t, Res("ps%d" % i)))
        self.ones = self.sb(st, "ones", [128, 128], F32)
        self.r_const = Res("const")
        P.op("dve", lambda e: e.memset(self.ones[:], 1.0), writes=[self.r_const])
        self.ident = self.sb(st, "ident", [128, 128], F32)
        self.identb = self.sb(st, "identb", [128, 128], BF16)
        idin = self.din("ident_in", [128, 128])
        P.dma("sp", lambda e: e.dma_start(out=self.ident[:], in_=idin), writes=[self.r_const])
        P.op("dve", lambda e: e.tensor_copy(out=self.identb[:], in_=self.ident[:]), reads=[self.r_const], writes=[self.r_const])
        self.modS = self.sb(st, "modS", [128, 192], F32)
        self.r_modS = Res("modS")
        self.modL = self.sb(st, "modL", [128, 192], F32)
        self.r_modL = Res("modL")
        self.Acol = self.sb(st, "Acol", [128, 64], F32)
        self.r_Acol = Res("Acol")

    def emit_adaln(self):
        nc, P = self.nc, self.P
        ccol = self.din("ccol", [128, KCD])
        adaw = self.din("adaw", [192, 128, D])
        adab = self.din("adab", [128, 192])
        with contextlib.ExitStack() as st:
            sc = self.sb(st, "sc", [128, KCD], F32)
            ab = self.sb(st, "ab", [128, 192], F32)
            r_sc, r_ab = Res(), Res()
            ring = [(self.sb(st, "aw%d" % i, [128, D], F32), Res()) for i in range(3)]
            P.dma("sp", lambda e: e.dma_start(out=sc[:], in_=ccol), writes=[r_sc])
            P.dma("sp", lambda e: e.dma_start(out=ab[:], in_=adab), writes=[r_ab])
            P.op("act", lambda e: e.activation(out=sc[:], in_=sc[:], func=AF.Silu), reads=[r_sc], writes=[r_sc])
            pst, r_ps = self.ps[0]
            for n in range(192):
                wt, r_w = ring[n % 3]
                P.dma("sp" if n % 2 == 0 else "act", lambda e, wt=wt, n=n: e.dma_start(out=wt[:], in_=adaw[n]), writes=[r_w])
                for kc in range(KCD):
                    P.op("pe", lambda e, wt=wt, n=n, kc=kc: e.matmul(pst[:, n:n + 1], lhsT=wt[:, kc * 128:(kc + 1) * 128],
                                                                     rhs=sc[:, kc:kc + 1], start=(kc == 0), stop=(kc == KCD - 1)),
                         reads=[r_w, r_sc], writes=[r_ps])
            P.op("dve", lambda e: e.tensor_tensor(out=self.modS[:], in0=pst[:, 0:192], in1=ab[:], op=ALU.add),
                 reads=[r_ps, r_ab], writes=[self.r_modS])
            P.flush()

    def emit_layer_mod(self, l):
        nc, P = self.nc, self.P
        if l == 0:
            self.adal = self.din("adal", [DEPTH, 128, 192])
            self.gnorm = self.din("gnorm", [DEPTH, 128, 64])
        with contextlib.ExitStack() as st:
            al = self.sb(st, "al", [128, 192], F32)
            gn = self.sb(st, "gn", [128, 64], F32)
            r_al, r_gn = Res(), Res()
            P.dma("sp", lambda e: e.dma_start(out=al[:], in_=self.adal[l]), writes=[r_al])
            P.dma("sp", lambda e: e.dma_start(out=gn[:], in_=self.gnorm[l]), writes=[r_gn])
            P.op("dve", lambda e: e.tensor_tensor(out=self.modL[:], in0=self.modS[:], in1=al[:], op=ALU.add),
                 reads=[self.r_modS, r_al], writes=[self.r_modL])
            for j, sc0 in ((0, 32), (1, 128)):
                P.op("dve", lambda e, j=j, sc0=sc0: e.scalar_tensor_tensor(
                    out=self.Acol[:, j * 32:(j + 1) * 32], in0=self.modL[:, sc0:sc0 + 32], scalar=1.0,
                    in1=gn[:, j * 32:(j + 1) * 32], op0=ALU.add, op1=ALU.mult),
                    reads=[self.r_modL, r_gn], writes=[self.r_Acol])
            P.flush()

    def emit_norm(self, st, xT, r_xT, t0, T, hT, r_h, acol0, shift0):
        nc, P = self.nc, self.P
        xs = [(self.sb(st, "nx%d" % i, [128, 512], F32), Res()) for i in range(4)]
        sq = [(self.sb(st, "nsq%d" % i, [128, 512], F32), Res()) for i in range(2)]
        rstd, r_rstd = self.sb(st, "nrstd", [128, 512], F32), Res()
        tmp = [(self.sb(st, "ntmp%d" % i, [128, 512], F32), Res()) for i in range(2)]
        cnt = 0
        for tb in range(T // 512):
            c0 = t0 + tb * 512
            pst, r_ps = self.ps[tb % 2]
            for kc in range(KCD):
                xt, r_x = xs[cnt % 4]
                sqt, r_sq = sq[cnt % 2]
                cnt += 1
                P.dma("sp", lambda e, xt=xt, kc=kc, c0=c0: e.dma_start(out=xt[:], in_=xT[kc * 128:(kc + 1) * 128, c0:c0 + 512]),
                      writes=[r_x])
                P.op("act", lambda e, xt=xt, sqt=sqt: e.activation(out=sqt[:], in_=xt[:], func=AF.Square), reads=[r_x], writes=[r_sq])
                P.op("pe", lambda e, sqt=sqt, kc=kc, pst=pst: e.matmul(pst[:], lhsT=self.ones[:], rhs=sqt[:], start=(kc == 0), stop=(kc == KCD - 1)),
                     reads=[r_sq, self.r_const], writes=[r_ps])
            P.op("act", lambda e, pst=pst: e.activation(out=rstd[:], in_=pst[:], func=AF.Sqrt, bias=EPS, scale=1.0 / D),
                 reads=[r_ps], writes=[r_rstd])
            P.op("dve", lambda e: e.reciprocal(out=rstd[:], in_=rstd[:]), reads=[r_rstd], writes=[r_rstd])
            for kc in range(KCD):
                xt, r_x = xs[cnt % 4]
                tt_, r_t = tmp[cnt % 2]
                cnt += 1
                P.dma("sp", lambda e, xt=xt, kc=kc, c0=c0: e.dma_start(out=xt[:], in_=xT[kc * 128:(kc + 1) * 128, c0:c0 + 512]),
                      writes=[r_x])
                P.op("dve", lambda e, xt=xt, tt_=tt_, kc=kc: e.scalar_tensor_tensor(
                    out=tt_[:], in0=xt[:], scalar=self.Acol[:, acol0 + kc:acol0 + kc + 1], in1=rstd[:], op0=ALU.mult, op1=ALU.mult),
                    reads=[r_x, r_rstd, self.r_Acol], writes=[r_t])
                P.op("act", lambda e, tt_=tt_, kc=kc, tb=tb: e.activation(
                    out=hT[:, kc, tb * 512:(tb + 1) * 512], in_=tt_[:], func=AF.Identity,
                    bias=self.modL[:, shift0 + kc:shift0 + kc + 1], scale=1.0),
                    reads=[r_t, self.r_modL], writes=[r_h[kc]])

    def emit_linear(self, st, hT, r_h, KC, T, wsl, widths, epi, nring=3, tag="w"):
        nc, P = self.nc, self.P
        ring = [(self.sb(st, "%s%d" % (tag, i), [128, KC * 128], BF16), Res()) for i in range(nring)]
        NTB = T // 512
        nset = 8 // NTB
        for n, width in enumerate(widths):
            wt, r_w = ring[n % nring]
            P.dma("pool", lambda e, wt=wt, n=n: e.dma_start(out=wt[:], in_=wsl[n]), writes=[r_w])
            base = (n % nset) * NTB
            for kc in range(KC):
                for tb in range(NTB):
                    pst, r_ps = self.ps[base + tb]
                    P.op("pe", lambda e, wt=wt, kc=kc, tb=tb, pst=pst, width=width: e.matmul(
                        pst[0:width, :], lhsT=wt[:, kc * 128:kc * 128 + width], rhs=hT[:, kc, tb * 512:(tb + 1) * 512],
                        start=(kc == 0), stop=(kc == KC - 1)),
                        reads=[r_w, r_h[kc]], writes=[r_ps])
            for tb in range(NTB):
                pst, r_ps = self.ps[base + tb]
                epi(n, tb, pst, r_ps, width)

    def emit_inproj(self, l, xT, r_xT, PT, r_PT):
        nc, P = self.nc, self.P
        if l == 0:
            self.w_in = self.din("w_in", [self.depth, len(W_IN_CHUNKS), 128, D])
        for tt in range(self.L // TT):
            t0 = tt * TT
            with contextlib.ExitStack() as st:
                hT = self.sb(st, "hT", [128, KCD, TT], BF16)
                r_h = [Res() for _ in range(KCD)]
                with contextlib.ExitStack() as st2:
                    self.emit_norm(st2, xT, r_xT, t0, TT, hT, r_h, 0, 0)
                    P.flush()
                stg = [(self.sb(st, "stg%d" % i, [128, 512], F32), Res()) for i in range(4)]
                k = [0]

                def epi(n, tb, pst, r_ps, width):
                    s, r_s = stg[k[0] % 4]
                    k[0] += 1
                    r0 = W_IN_CHUNKS[n][0]
                    P.op("act", lambda e: e.activation(out=s[0:width, :], in_=pst[0:width, :], func=AF.Copy), reads=[r_ps], writes=[r_s])
                    P.dma("sp", lambda e: e.dma_start(out=PT[r0:r0 + width, t0 + tb * 512:t0 + (tb + 1) * 512], in_=s[0:width, :]),
                          reads=[r_s])
                self.emit_linear(st, hT, r_h, KCD, TT, self.w_in[l], [w for _, w in W_IN_CHUNKS], epi)
                P.flush()

    def emit_pool(self, l, PT, YT):
        nc, P, L = self.nc, self.P, self.L
        if l == 0:
            self.pool_w = self.din("pool_w", [self.depth, 4, 384, 384])
            self.poolsc = self.din("poolsc", [self.depth, 128, 12])
            self.invcnt = self.din("invcnt", [4, 128, L])
        TS = min(L, 4096)
        HAL = 16
        with contextlib.ExitStack() as st:
            U = self.sb(st, "pU", [128, TS + 32], F32)
            S = [self.sb(st, "pS%d" % i, [128, TS + 32], F32) for i in range(2)]
            IC = self.sb(st, "pIC", [128, TS], F32)
            Pc = [self.sb(st, "pP%d" % i, [128, TS], BF16) for i in range(3)]
            pw = self.sb(st, "pw", [128, 3, 384], BF16)
            psc = self.sb(st, "psc", [128, 12], F32)
            stg = [(self.sb(st, "pstg%d" % i, [128, 512], F32), Res()) for i in range(3)]
            r_U, r_S, r_IC, r_pw, r_psc = Res(), [Res(), Res()], Res(), Res(), Res()
            r_Pc = [Res() for _ in range(3)]
            P.dma("sp", lambda e: e.dma_start(out=psc[:], in_=self.poolsc[l]), writes=[r_psc])
            k = 0
            for g, win in enumerate(POOL_WINDOWS):
                P.dma("pool", lambda e, g=g: e.dma_start(out=pw[:], in_=self.pool_w[l, g].rearrange("(c p) d -> p c d", p=128)), writes=[r_pw])
                for t0 in range(0, L, TS):
                    P.dma("sp", lambda e, g=g, t0=t0: e.dma_start(out=IC[:], in_=self.invcnt[g, :, t0:t0 + TS]), writes=[r_IC])
                    for c in range(3):
                        row0 = A0 + g * 384 + c * 128
                        lo, hi = max(0, t0 - HAL), min(L, t0 + TS + HAL)
                        P.op("pool", lambda e: e.memset(U[:], 0.0), writes=[r_U])
                        P.dma("sp", lambda e, row0=row0, lo=lo, hi=hi, t0=t0: e.dma_start(
                            out=U[:, lo - (t0 - HAL):hi - (t0 - HAL)], in_=PT[row0:row0 + 128, lo:hi]), writes=[r_U])
                        src, r_src = U, r_U
                        kk, wlen, bi = 1, TS + 32, 0
                        while kk < win:
                            wlen -= kk
                            dst, r_dst = S[bi], r_S[bi]
                            P.op("dve", lambda e, src=src, dst=dst, kk=kk, wlen=wlen: e.tensor_tensor(
                                out=dst[:, 0:wlen], in0=src[:, 0:wlen], in1=src[:, kk:kk + wlen], op=ALU.add),
                                reads=[r_src], writes=[r_dst])
                            src, r_src = dst, r_dst
                            kk *= 2
                            bi ^= 1
                        Mb, r_Mb = S[bi], r_S[bi]
                        o = HAL - win // 2
                        P.op("dve", lambda e, src=src, Mb=Mb, o=o: e.tensor_tensor(out=Mb[:, 0:TS], in0=src[:, o:o + TS], in1=IC[:], op=ALU.mult),
                             reads=[r_src, r_IC], writes=[r_Mb])
                        P.op("dve", lambda e, Mb=Mb, c=c: e.tensor_tensor(out=Pc[c][:], in0=Mb[:, 0:TS], in1=U[:, HAL:HAL + TS], op=ALU.subtract),
                             reads=[r_Mb, r_U], writes=[r_Pc[c]])
                    for d in range(3):
                        for tb in range(TS // 512):
                            pst, r_ps = self.ps[k % 8]
                            s, r_s = stg[k % 3]
                            k += 1
                            for c in range(3):
                                P.op("pe", lambda e, pst=pst, c=c, d=d, tb=tb: e.matmul(
                                    pst[:], lhsT=pw[:, c, d * 128:(d + 1) * 128], rhs=Pc[c][:, tb * 512:(tb + 1) * 512],
                                    start=(c == 0), stop=(c == 2)), reads=[r_pw, r_Pc[c]], writes=[r_ps])
                            col = g * 3 + d
                            P.op("act", lambda e, s=s, pst=pst, col=col: e.activation(out=s[:], in_=pst[:], func=AF.Copy, scale=psc[:, col:col + 1]),
                                 reads=[r_ps, r_psc], writes=[r_s])
                            r0 = g * 384 + d * 128
                            P.dma("sp", lambda e, s=s, r0=r0, c0=t0 + tb * 512: e.dma_start(out=YT[r0:r0 + 128, c0:c0 + 512], in_=s[:]), reads=[r_s])
            P.flush()


    def emit_attn(self, l, PT, YT):
        nc, P, L = self.nc, self.P, self.L
        if l == 0:
            self.qkn = self.din("qkn", [self.depth, 128, 2])
            self.bmask = self.din("bmask", [12, 128, 4, 256])
            self.OTd = self.nc.dram_tensor("OTd", [12, 128, L], F32).ap()
            self.LSEd = self.nc.dram_tensor("LSEd", [12, L], F32).ap()
        HALM = 1024
        NB = L // 128
        with contextlib.ExitStack() as st:
            QN = self.sb(st, "aQN", [128, L], BF16)
            KN = self.sb(st, "aKN", [128, L + 2 * HALM], BF16)
            VN = self.sb(st, "aVN", [128, L + 2 * HALM], BF16)
            VT = self.sb(st, "aVT", [128, NB + 16, 128], BF16)
            OT = self.sb(st, "aOT", [128, L], F32)
            LSEc = self.sb(st, "aLSE", [128, NB], F32)
            bm = self.sb(st, "abm", [128, 4, 256], F32)
            gq = self.sb(st, "agq", [128, 2], F32)
            r_QN, r_KN, r_VN, r_VT, r_OT, r_LSE, r_bm, r_gq = [Res() for _ in range(8)]
            ld = [(self.sb(st, "ald%d" % i, [128, 512], F32), Res()) for i in range(3)]
            sq = [(self.sb(st, "asq%d" % i, [128, 512], F32), Res()) for i in range(2)]
            rs = [(self.sb(st, "ars%d" % i, [128, 512], F32), Res()) for i in range(2)]
            S2 = [(self.sb(st, "aS2%d" % i, [128, 256], F32), Res()) for i in range(2)]
            Pm = [(self.sb(st, "aPm%d" % i, [128, 256], BF16), Res()) for i in range(2)]
            PmT = [(self.sb(st, "aPT%d" % i, [128, 256], BF16), Res()) for i in range(2)]
            ob = [(self.sb(st, "aob%d" % i, [128, 128], F32), Res()) for i in range(2)]
            sm = [(self.sb(st, "asm%d" % i, [128, 4], F32), Res()) for i in range(2)]
            P.dma("sp", lambda e: e.dma_start(out=gq[:], in_=self.qkn[l]), writes=[r_gq])
            P.op("dve", lambda e: e.tensor_scalar(out=gq[:, 0:1], in0=gq[:, 0:1], scalar1=float(128 ** -0.5), scalar2=None, op0=ALU.mult),
                 reads=[r_gq], writes=[r_gq])
            for buf, r_b in ((KN, r_KN), (VN, r_VN)):
                P.op("pool", lambda e, buf=buf: e.memset(buf[:, 0:HALM], 0.0), writes=[r_b])
                P.op("pool", lambda e, buf=buf: e.memset(buf[:, HALM + L:HALM + L + HALM], 0.0), writes=[r_b])
            cnt = 0
            for gi, dil in enumerate((1, 4, 16)):
                m = L // dil
                nblk = m // 128
                for hh in range(4):
                    gh = gi * 4 + hh
                    P.dma("sp", lambda e, gh=gh: e.dma_start(out=bm[:], in_=self.bmask[gh]), writes=[r_bm])
                    for which in range(3):
                        row0 = QKV0 + which * 1536 + gi * 512 + hh * 128
                        for tb in range(L // 512):
                            t, r_t = ld[cnt % 3]
                            cnt += 1
                            P.dma("sp", lambda e, t=t, row0=row0, tb=tb: e.dma_start(out=t[:], in_=PT[row0:row0 + 128, tb * 512:(tb + 1) * 512]), writes=[r_t])
                            if which == 2:
                                P.op("act", lambda e, t=t, tb=tb: e.activation(out=VN[:, HALM + tb * 512:HALM + (tb + 1) * 512], in_=t[:], func=AF.Copy),
                                     reads=[r_t], writes=[r_VN])
                                continue
                            sqt, r_sq = sq[cnt % 2]
                            rst, r_rs = rs[cnt % 2]
                            pst, r_ps = self.ps[cnt % 2]
                            P.op("act", lambda e, t=t, sqt=sqt: e.activation(out=sqt[:], in_=t[:], func=AF.Square), reads=[r_t], writes=[r_sq])
                            P.op("pe", lambda e, sqt=sqt, pst=pst: e.matmul(pst[:], lhsT=self.ones[:], rhs=sqt[:], start=True, stop=True),
                                 reads=[r_sq, self.r_const], writes=[r_ps])
                            P.op("act", lambda e, rst=rst, pst=pst: e.activation(out=rst[:], in_=pst[:], func=AF.Sqrt, bias=EPS, scale=1.0 / 128),
                                 reads=[r_ps], writes=[r_rs])
                            P.op("dve", lambda e, rst=rst: e.reciprocal(out=rst[:], in_=rst[:]), reads=[r_rs], writes=[r_rs])
                            dst = QN[:, tb * 512:(tb + 1) * 512] if which == 0 else KN[:, HALM + tb * 512:HALM + (tb + 1) * 512]
                            P.op("dve", lambda e, t=t, rst=rst, dst=dst, which=which: e.scalar_tensor_tensor(
                                out=dst, in0=t[:], scalar=gq[:, which:which + 1], in1=rst[:], op0=ALU.mult, op1=ALU.mult),
                                reads=[r_t, r_rs, r_gq], writes=[r_QN if which == 0 else r_KN])
                    ntile = dil * (nblk + 1)
                    tiles = [(r, n) for r in range(dil) for n in range(nblk + 1)]
                    for t0 in range(0, ntile, 4):
                        pst, r_ps = self.ps[2 + (t0 // 4) % 2]
                        grp = tiles[t0:t0 + 4]
                        for j, (r, n) in enumerate(grp):
                            s0 = HALM + r + dil * (128 * n - 64)
                            P.op("pe", lambda e, pst=pst, j=j, s0=s0, dil=dil: e.matmul(
                                pst[:, j * 128:(j + 1) * 128], lhsT=VN[:, s0:s0 + 127 * dil + 1:dil], rhs=self.identb[:], start=True, stop=True),
                                reads=[r_VN, self.r_const], writes=[r_ps])
                        ng = len(grp)
                        P.op("act", lambda e, pst=pst, t0=t0, ng=ng: e.activation(
                            out=VT[:, t0:t0 + ng, :], in_=pst[:, 0:ng * 128].rearrange("p (a b) -> p a b", b=128), func=AF.Copy),
                            reads=[r_ps], writes=[r_VT])
                    bi = 0
                    for r in range(dil):
                        for n in range(nblk):
                            var = (1 if n == 0 else 0) + (2 if n == nblk - 1 else 0)
                            psS, r_psS = self.ps[bi % 2]
                            psT, r_psT = self.ps[2 + bi % 2]
                            psO, r_psO = self.ps[4 + bi % 2]
                            psX, r_psX = self.ps[6 + bi % 2]
                            s2, r_s2 = S2[bi % 2]
                            pm, r_pm = Pm[bi % 2]
                            pmt, r_pmt = PmT[bi % 2]
                            o_, r_o = ob[bi % 2]
                            smt, r_sm = sm[bi % 2]
                            q0 = r + dil * 128 * n
                            k0 = HALM + r + dil * (128 * n - 64)
                            P.op("pe", lambda e, psS=psS, q0=q0, k0=k0, dil=dil: e.matmul(
                                psS[:, 0:256], lhsT=QN[:, q0:q0 + 127 * dil + 1:dil], rhs=KN[:, k0:k0 + 255 * dil + 1:dil], start=True, stop=True),
                                reads=[r_QN, r_KN], writes=[r_psS])
                            P.op("dve", lambda e, psS=psS, s2=s2, var=var: e.tensor_tensor(out=s2[:], in0=psS[:, 0:256], in1=bm[:, var, :], op=ALU.add),
                                 reads=[r_psS, r_bm], writes=[r_s2])
                            P.op("dve", lambda e, s2=s2, smt=smt: e.reduce_max(out=smt[:, 0:1], in_=s2[:], axis=AX.X), reads=[r_s2], writes=[r_sm])
                            P.op("dve", lambda e, smt=smt: e.tensor_scalar(out=smt[:, 1:2], in0=smt[:, 0:1], scalar1=-1.0, scalar2=None, op0=ALU.mult),
                                 reads=[r_sm], writes=[r_sm])
                            P.op("act", lambda e, s2=s2, pm=pm, smt=smt: e.activation(out=pm[:], in_=s2[:], func=AF.Exp, bias=smt[:, 1:2], scale=1.0,
                                                                                     accum_out=smt[:, 2:3]),
                                 reads=[r_s2, r_sm], writes=[r_pm, r_sm])
                            for kb in range(2):
                                P.op("pe", lambda e, psT=psT, pm=pm, kb=kb: e.matmul(
                                    psT[:, kb * 128:(kb + 1) * 128], lhsT=pm[:, kb * 128:(kb + 1) * 128], rhs=self.identb[:], start=True, stop=True),
                                    reads=[r_pm, self.r_const], writes=[r_psT])
                            P.op("dve", lambda e, psT=psT, pmt=pmt: e.tensor_copy(out=pmt[:], in_=psT[:, 0:256]), reads=[r_psT], writes=[r_pmt])
                            ti = r * (nblk + 1) + n
                            for kb in range(2):
                                P.op("pe", lambda e, psO=psO, pmt=pmt, kb=kb, ti=ti: e.matmul(
                                    psO[:, 0:128], lhsT=pmt[:, kb * 128:(kb + 1) * 128], rhs=VT[:, ti + kb, :], start=(kb == 0), stop=(kb == 1)),
                                    reads=[r_pmt, r_VT], writes=[r_psO])
                            P.op("dve", lambda e, smt=smt: e.reciprocal(out=smt[:, 3:4], in_=smt[:, 2:3]), reads=[r_sm], writes=[r_sm])
                            P.op("act", lambda e, psO=psO, o_=o_, smt=smt: e.activation(out=o_[:], in_=psO[:, 0:128], func=AF.Copy, scale=smt[:, 3:4]),
                                 reads=[r_psO, r_sm], writes=[r_o])
                            P.op("act", lambda e, smt=smt, bi=bi: e.activation(out=LSEc[:, bi:bi + 1], in_=smt[:, 2:3], func=AF.Ln), reads=[r_sm], writes=[r_LSE])
                            P.op("dve", lambda e, smt=smt, bi=bi: e.tensor_tensor(out=LSEc[:, bi:bi + 1], in0=LSEc[:, bi:bi + 1], in1=smt[:, 0:1], op=ALU.add),
                                 reads=[r_sm, r_LSE], writes=[r_LSE])
                            P.op("pe", lambda e, psX=psX, o_=o_: e.matmul(psX[:, 0:128], lhsT=o_[:], rhs=self.ident[:], start=True, stop=True),
                                 reads=[r_o, self.r_const], writes=[r_psX])
                            P.op("act", lambda e, psX=psX, q0=q0, dil=dil: e.activation(out=OT[:, q0:q0 + 127 * dil + 1:dil], in_=psX[:, 0:128], func=AF.Copy),
                                 reads=[r_psX], writes=[r_OT])
                            bi += 1
                    P.dma("sp", lambda e, gh=gh: e.dma_start(out=self.OTd[gh], in_=OT[:]), reads=[r_OT])
                    lse_v = self.LSEd[gh].rearrange("(n i r) -> i r n", i=128, r=dil)
                    for r in range(dil):
                        for n0 in range(0, nblk, 16):
                            n1 = min(nblk, n0 + 16)
                            P.dma("sp", lambda e, lse_v=lse_v, r=r, n0=n0, n1=n1, nblk=nblk: e.dma_start(
                                out=lse_v[:, r, n0:n1], in_=LSEc[:, r * nblk + n0:r * nblk + n1], allow_slow_non_contiguous=True), reads=[r_LSE])
            P.flush()
        with contextlib.ExitStack() as st:
            o3 = [[(self.sb(st, "mo%d%d" % (i, j), [128, 512], F32), Res()) for j in range(3)] for i in range(2)]
            l3 = [[(self.sb(st, "ml%d%d" % (i, j), [128, 512], F32), Res()) for j in range(3)] for i in range(2)]
            mx = [(self.sb(st, "mm%d" % i, [128, 512], F32), Res()) for i in range(2)]
            den = [(self.sb(st, "md%d" % i, [128, 512], F32), Res()) for i in range(2)]
            acc = [(self.sb(st, "ma%d" % i, [128, 512], F32), Res()) for i in range(2)]
            k = 0
            for hh in range(4):
                for tb in range(L // 512):
                    i = k % 2
                    k += 1
                    c0 = tb * 512
                    for gi in range(3):
                        gh = gi * 4 + hh
                        P.dma("sp", lambda e, i=i, gi=gi, gh=gh, c0=c0: e.dma_start(out=o3[i][gi][0][:], in_=self.OTd[gh, :, c0:c0 + 512]), writes=[o3[i][gi][1]])
                        P.dma("act", lambda e, i=i, gi=gi, gh=gh, c0=c0: e.dma_start(
                            out=l3[i][gi][0][:], in_=self.LSEd[gh:gh + 1, c0:c0 + 512].broadcast_to([128, 512])), writes=[l3[i][gi][1]])
                    (m_, r_m), (d_, r_d), (a_, r_a) = mx[i], den[i], acc[i]
                    lt = [l3[i][g][0] for g in range(3)]
                    r_l = [l3[i][g][1] for g in range(3)]
                    ot = [o3[i][g][0] for g in range(3)]
                    r_o3 = [o3[i][g][1] for g in range(3)]
                    P.op("dve", lambda e, m_=m_, lt=lt: e.tensor_tensor(out=m_[:], in0=lt[0][:], in1=lt[1][:], op=ALU.max), reads=[r_l[0], r_l[1]], writes=[r_m])
                    P.op("dve", lambda e, m_=m_, lt=lt: e.tensor_tensor(out=m_[:], in0=m_[:], in1=lt[2][:], op=ALU.max), reads=[r_l[2], r_m], writes=[r_m])
                    for g in range(3):
                        P.op("dve", lambda e, m_=m_, lt=lt, g=g: e.tensor_tensor(out=lt[g][:], in0=lt[g][:], in1=m_[:], op=ALU.subtract), reads=[r_m, r_l[g]], writes=[r_l[g]])
                        P.op("act", lambda e, lt=lt, g=g: e.activation(out=lt[g][:], in_=lt[g][:], func=AF.Exp), reads=[r_l[g]], writes=[r_l[g]])
                        P.op("dve", lambda e, lt=lt, ot=ot, g=g: e.tensor_tensor(out=ot[g][:], in0=ot[g][:], in1=lt[g][:], op=ALU.mult), reads=[r_l[g], r_o3[g]], writes=[r_o3[g]])
                    P.op("dve", lambda e, d_=d_, lt=lt: e.tensor_tensor(out=d_[:], in0=lt[0][:], in1=lt[1][:], op=ALU.add), reads=[r_l[0], r_l[1]], writes=[r_d])
                    P.op("dve", lambda e, d_=d_, lt=lt: e.tensor_tensor(out=d_[:], in0=d_[:], in1=lt[2][:], op=ALU.add), reads=[r_l[2], r_d], writes=[r_d])
                    P.op("dve", lambda e, d_=d_: e.reciprocal(out=d_[:], in_=d_[:]), reads=[r_d], writes=[r_d])
                    P.op("dve", lambda e, a_=a_, ot=ot: e.tensor_tensor(out=a_[:], in0=ot[0][:], in1=ot[1][:], op=ALU.add), reads=[r_o3[0], r_o3[1]], writes=[r_a])
                    P.op("dve", lambda e, a_=a_, ot=ot: e.tensor_tensor(out=a_[:], in0=a_[:], in1=ot[2][:], op=ALU.add), reads=[r_o3[2], r_a], writes=[r_a])
                    P.op("dve", lambda e, a_=a_, d_=d_: e.tensor_tensor(out=a_[:], in0=a_[:], in1=d_[:], op=ALU.mult), reads=[r_d, r_a], writes=[r_a])
                    r0 = 3584 + hh * 128
                    P.dma("sp", lambda e, a_=a_, r0=r0, c0=c0: e.dma_start(out=YT[r0:r0 + 128, c0:c0 + 512], in_=a_[:]), reads=[r_a])
            P.flush()


    def emit_ssd(self, l, PT, YT):
        nc, P, L = self.nc, self.P, self.L
        NC = L // 128
        if l == 0:
            self.ssdv = self.din("ssdv", [self.depth, 4, 16, 2])
            self.convw = self.din("convw", [self.depth, 128, 24, 6])
            self.dskip = self.din("dskip", [self.depth, 4, 128, 8])
            self.ssdn = self.din("ssdn", [self.depth, 128, 16])
            self.ssdc_in = self.din("ssdc", [128, 4, 128])
            self.XCd = self.nc.dram_tensor("XCd", [6, 128, L], F32).ap()
            self.PRVd = self.nc.dram_tensor("PRVd", [NC, 128, 512], BF16).ap()

        def bc3(ap2, n):
            return ap2.rearrange("p (h o) -> p h o", o=1).broadcast_to([128, ap2.shape[1], n])

        for g in range(4):
            with contextlib.ExitStack() as st:
                TS = min(L, 4096)
                U = [(self.sb(st, "sU%d" % i, [128, TS + 4], F32), Res()) for i in range(2)]
                AC = [(self.sb(st, "sA%d" % i, [128, TS], F32), Res()) for i in range(2)]
                cw = self.sb(st, "scw", [128, 24, 6], F32)
                r_cw = Res()
                P.dma("sp", lambda e: e.dma_start(out=cw[:], in_=self.convw[l]), writes=[r_cw])
                k = 0
                for ci, (row0, wc) in enumerate([(XBC0 + g * 512 + j * 128, 4 * g + j) for j in range(4)]
                                                + [(XBC0 + 2048 + g * 128, 16 + g), (XBC0 + 2560 + g * 128, 20 + g)]):
                    for t0 in range(0, L, TS):
                        (u, r_u), (a, r_a) = U[k % 2], AC[k % 2]
                        k += 1
                        lo, hi = max(0, t0 - 2), min(L, t0 + TS + 2)
                        P.op("pool", lambda e, u=u: e.memset(u[:], 0.0), writes=[r_u])
                        P.dma("sp", lambda e, u=u, row0=row0, lo=lo, hi=hi, t0=t0: e.dma_start(
                            out=u[:, lo - (t0 - 2):hi - (t0 - 2)], in_=PT[row0:row0 + 128, lo:hi]), writes=[r_u])
                        P.op("dve", lambda e, u=u, a=a, wc=wc: e.tensor_scalar(out=a[:], in0=u[:, 0:TS], scalar1=cw[:, wc, 0:1], scalar2=cw[:, wc, 5:6],
                                                                             op0=ALU.mult, op1=ALU.add), reads=[r_u, r_cw], writes=[r_a])
                        for tap in range(1, 5):
                            P.op("dve", lambda e, u=u, a=a, wc=wc, tap=tap: e.scalar_tensor_tensor(
                                out=a[:], in0=u[:, tap:tap + TS], scalar=cw[:, wc, tap:tap + 1], in1=a[:], op0=ALU.mult, op1=ALU.add),
                                reads=[r_u, r_cw, r_a], writes=[r_a])
                        P.op("act", lambda e, a=a: e.activation(out=a[:], in_=a[:], func=AF.Silu), reads=[r_a], writes=[r_a])
                        P.dma("sp", lambda e, a=a, ci=ci, t0=t0: e.dma_start(out=self.XCd[ci, :, t0:t0 + TS], in_=a[:]), reads=[r_a])
                P.flush()
            with contextlib.ExitStack() as stg_:
                DT = self.sb(stg_, "sDT", [128, NC, 16], F32)
                ACS = self.sb(stg_, "sACS", [128, NC, 16], F32)
                EIN = self.sb(stg_, "sEIN", [128, NC, 16], F32)
                EOW = self.sb(stg_, "sEOW", [128, NC, 16], F32)
                DEC = self.sb(stg_, "sDEC", [128, NC, 16], F32)
                SC = self.sb(stg_, "sSC", [128, 4, 128], F32)
                NMB = self.sb(stg_, "sNMB", [128, 2, 4, 128], BF16)
                DSK = self.sb(stg_, "sDSK", [128, 8], F32)
                NG = self.sb(stg_, "sNG", [128, 16], F32)
                r_tab = Res()
                r_sc = Res()
                with contextlib.ExitStack() as st:
                    dtT = self.sb(st, "sdtT", [16, L], F32)
                    dtE = self.sb(st, "sdtE", [16, L], F32)
                    dtA = self.sb(st, "sdtA", [16, L], F32)
                    DTA = self.sb(st, "sDTA", [128, NC, 16], F32)
                    TOT = self.sb(st, "sTOT", [128, NC, 16], F32)
                    sv = self.sb(st, "ssv", [16, 4], F32)
                    r_dtT, r_dtE, r_dtA, r_DTA, r_TOT, r_sv = [Res() for _ in range(6)]
                    P.dma("sp", lambda e: e.dma_start(out=SC[:], in_=self.ssdc_in), writes=[r_sc])
                    P.dma("sp", lambda e: e.dma_start(out=DSK[:], in_=self.dskip[l, g]), writes=[r_sc])
                    P.dma("sp", lambda e: e.dma_start(out=NG[:], in_=self.ssdn[l]), writes=[r_sc])
                    for d in range(2):
                        for q in range(4):
                            P.op("dve", lambda e, d=d, q=q: e.tensor_copy(out=NMB[:, d, q, :], in_=SC[:, 2 + d, :]), reads=[r_sc], writes=[r_sc])
                    P.dma("sp", lambda e: e.dma_start(out=sv[:, 0:2], in_=self.ssdv[l, g]), writes=[r_sv])
                    for d in range(2):
                        r0 = DT0 + d * 32 + g * 8
                        P.dma("sp", lambda e, d=d, r0=r0: e.dma_start(out=dtT[d * 8:(d + 1) * 8, :], in_=PT[r0:r0 + 8, :]), writes=[r_dtT])
                    P.op("act", lambda e: e.activation(out=sv[:, 2:3], in_=sv[:, 1:2], func=AF.Exp), reads=[r_sv], writes=[r_sv])
                    P.op("dve", lambda e: e.tensor_scalar(out=sv[:, 3:4], in0=sv[:, 2:3], scalar1=-1.0, scalar2=None, op0=ALU.mult), reads=[r_sv], writes=[r_sv])
                    P.op("act", lambda e: e.activation(out=dtE[:], in_=dtT[:], func=AF.Exp, bias=sv[:, 0:1], scale=1.0), reads=[r_dtT, r_sv], writes=[r_dtE])
                    P.op("act", lambda e: e.activation(out=dtE[:], in_=dtE[:], func=AF.Ln, bias=1.0, scale=1.0), reads=[r_dtE], writes=[r_dtE])
                    P.op("dve", lambda e: e.tensor_scalar(out=dtA[:], in0=dtE[:], scalar1=sv[:, 3:4], scalar2=None, op0=ALU.mult), reads=[r_dtE, r_sv], writes=[r_dtA])
                    for src, r_src, dst in ((dtE, r_dtE, DT), (dtA, r_dtA, DTA)):
                        for c0 in range(0, NC, 32):
                            nn = min(32, NC - c0)
                            pst, r_ps = self.ps[(c0 // 32) % 2]
                            for c in range(nn):
                                P.op("pe", lambda e, pst=pst, src=src, c=c, c0=c0: e.matmul(
                                    pst[:, c * 16:(c + 1) * 16], lhsT=src[0:16, (c0 + c) * 128:(c0 + c + 1) * 128], rhs=self.ident[0:16, 0:16],
                                    start=True, stop=True), reads=[r_src, self.r_const], writes=[r_ps])
                            P.op("act", lambda e, pst=pst, dst=dst, c0=c0, nn=nn: e.activation(
                                out=dst[:, c0:c0 + nn, :], in_=pst[:, 0:nn * 16].rearrange("p (c h) -> p c h", h=16), func=AF.Copy),
                                reads=[r_ps], writes=[r_tab if dst is DT else r_DTA])
                    for dst, r_dst, lh in ((ACS, r_tab, None), (TOT, r_TOT, self.ones)):
                        for d in range(2):
                            for c0 in range(0, NC, 64):
                                nn = min(64, NC - c0)
                                pst, r_ps = self.ps[2 + d]
                                lhs = lh[:] if lh is not None else SC[:, d, :]
                                P.op("pe", lambda e, pst=pst, lhs=lhs, c0=c0, nn=nn, d=d: e.matmul(
                                    pst[:, 0:nn * 8], lhsT=lhs, rhs=DTA[:, c0:c0 + nn, d * 8:(d + 1) * 8], start=True, stop=True),
                                    reads=[r_DTA, r_sc, self.r_const], writes=[r_ps])
                                P.op("act", lambda e, pst=pst, dst=dst, c0=c0, nn=nn, d=d: e.activation(
                                    out=dst[:, c0:c0 + nn, d * 8:(d + 1) * 8], in_=pst[:, 0:nn * 8].rearrange("p (c h) -> p c h", h=8), func=AF.Copy),
                                    reads=[r_ps], writes=[r_dst])
                    P.op("act", lambda e: e.activation(out=EIN[:], in_=ACS[:], func=AF.Exp), reads=[r_tab], writes=[r_tab])
                    P.op("act", lambda e: e.activation(out=DEC[:], in_=TOT[:], func=AF.Exp), reads=[r_TOT], writes=[r_tab])
                    P.op("dve", lambda e: e.tensor_tensor(out=EOW[:], in0=TOT[:], in1=ACS[:], op=ALU.subtract), reads=[r_TOT, r_tab], writes=[r_tab])
                    P.op("act", lambda e: e.activation(out=EOW[:], in_=EOW[:], func=AF.Exp), reads=[r_tab], writes=[r_tab])
                    P.op("dve", lambda e: e.tensor_tensor(out=EOW[:], in0=EOW[:], in1=DT[:], op=ALU.mult), reads=[r_tab], writes=[r_tab])
                    P.flush()

                SB_ = 512

                def load_x(st_, tiles, t0, names):
                    for nm, ci in names:
                        t, r_t = tiles[nm]
                        P.dma("sp", lambda e, t=t, ci=ci, t0=t0: e.dma_start(out=t[:], in_=self.XCd[ci, :, t0:t0 + SB_]), writes=[r_t])

                def xtm(tiles, cc, Xs, r_Xs, Bt, r_Bt):
                    ps0, r_ps0 = self.ps[0]
                    ps1, r_ps1 = self.ps[1]
                    for j in range(4):
                        t, r_t = tiles["x%d" % j]
                        P.op("pe", lambda e, t=t, j=j: e.matmul(ps0[:, j * 128:(j + 1) * 128], lhsT=t[:, cc * 128:(cc + 1) * 128], rhs=self.ident[:],
                                                               start=True, stop=True), reads=[r_t, self.r_const], writes=[r_ps0])
                    P.op("act", lambda e: e.activation(out=Xs[:], in_=ps0[:], func=AF.Copy), reads=[r_ps0], writes=[r_Xs])
                    t, r_t = tiles["B"]
                    P.op("pe", lambda e, t=t: e.matmul(ps1[:, 0:128], lhsT=t[:, cc * 128:(cc + 1) * 128], rhs=self.ident[:], start=True, stop=True),
                         reads=[r_t, self.r_const], writes=[r_ps1])
                    P.op("act", lambda e: e.activation(out=Bt[:], in_=ps1[:, 0:128], func=AF.Copy), reads=[r_ps1], writes=[r_Bt])

                with contextlib.ExitStack() as st:
                    tl = [{nm: (self.sb(st, "s1%s%d" % (nm, i), [128, SB_], F32), Res()) for nm in ("x0", "x1", "x2", "x3", "B")} for i in range(2)]
                    Xs2 = [(self.sb(st, "s1X%d" % i, [128, 512], F32), Res()) for i in range(2)]
                    Bt2 = [(self.sb(st, "s1Bt%d" % i, [128, 128], BF16), Res()) for i in range(2)]
                    xw2 = [(self.sb(st, "s1w%d" % i, [128, 512], BF16), Res()) for i in range(2)]
                    Sb = self.sb(st, "s1S", [128, 512], F32)
                    pv = [(self.sb(st, "s1pv%d" % i, [128, 512], BF16), Res()) for i in range(2)]
                    tmp = self.sb(st, "s1T", [128, 512], F32)
                    r_Sb, r_tmp = Res(), Res()
                    P.op("dve", lambda e: e.memset(Sb[:], 0.0), writes=[r_Sb])
                    k = 0
                    for sbi in reversed(range(L // SB_)):
                        tiles = tl[sbi % 2]
                        load_x(st, tiles, sbi * SB_, [("x0", 0), ("x1", 1), ("x2", 2), ("x3", 3), ("B", 4)])
                        for cc in reversed(range(4)):
                            c = sbi * 4 + cc
                            (Xs, r_Xs), (Bt, r_Bt), (xw, r_xw) = Xs2[k % 2], Bt2[k % 2], xw2[k % 2]
                            k += 1
                            xtm(tiles, cc, Xs, r_Xs, Bt, r_Bt)
                            P.op("dve", lambda e, Xs=Xs, xw=xw, c=c: e.tensor_tensor(
                                out=xw[:].rearrange("p (h q) -> p h q", q=64), in0=Xs[:].rearrange("p (h q) -> p h q", q=64),
                                in1=bc3(EOW[:, c, 8:16], 64), op=ALU.mult), reads=[r_Xs, r_tab], writes=[r_xw])
                            ps2, r_ps2 = self.ps[2 + k % 2]
                            P.op("pe", lambda e, ps2=ps2, Bt=Bt, xw=xw: e.matmul(ps2[:], lhsT=Bt[:], rhs=xw[:], start=True, stop=True),
                                 reads=[r_Bt, r_xw], writes=[r_ps2])
                            pvt, r_pv = pv[k % 2]
                            P.op("act", lambda e, pvt=pvt: e.activation(out=pvt[:], in_=Sb[:], func=AF.Copy), reads=[r_Sb], writes=[r_pv])
                            P.dma("sp", lambda e, pvt=pvt, c=c: e.dma_start(out=self.PRVd[c], in_=pvt[:]), reads=[r_pv])
                            P.op("dve", lambda e, c=c: e.tensor_tensor(
                                out=tmp[:].rearrange("p (h q) -> p h q", q=64), in0=Sb[:].rearrange("p (h q) -> p h q", q=64),
                                in1=bc3(DEC[:, c, 8:16], 64), op=ALU.mult), reads=[r_Sb, r_tab], writes=[r_tmp])
                            P.op("dve", lambda e, ps2=ps2: e.tensor_tensor(out=Sb[:], in0=tmp[:], in1=ps2[:], op=ALU.add),
                                 reads=[r_tmp, r_ps2], writes=[r_Sb])
                    P.flush()

                with contextlib.ExitStack() as st:
                    names = ("x0", "x1", "x2", "x3", "B", "C", "z0", "z1", "z2", "z3")
                    tl = [{nm: (self.sb(st, "s2%s%d" % (nm, i), [128, SB_], F32), Res()) for nm in names} for i in range(2)]
                    Bb = [(self.sb(st, "s2Bb%d" % i, [128, SB_], BF16), Res()) for i in range(2)]
                    Cb = [(self.sb(st, "s2Cb%d" % i, [128, SB_], BF16), Res()) for i in range(2)]
                    Xs2 = [(self.sb(st, "s2X%d" % i, [128, 512], F32), Res()) for i in range(2)]
                    Bt2 = [(self.sb(st, "s2Bt%d" % i, [128, 128], BF16), Res()) for i in range(2)]
                    xd2 = [[(self.sb(st, "s2d%d%d" % (i, d), [128, 512], BF16), Res()) for d in range(2)] for i in range(2)]
                    xw2 = [(self.sb(st, "s2w%d" % i, [128, 512], BF16), Res()) for i in range(2)]
                    CBT = [(self.sb(st, "s2CB%d" % i, [128, 128], F32), Res()) for i in range(2)]
                    RD = [(self.sb(st, "s2RD%d" % i, [128, 16, 128], F32), Res()) for i in range(1)]
                    ARG = [(self.sb(st, "s2AR%d" % i, [128, 512], F32), Res()) for i in range(2)]
                    EX = [(self.sb(st, "s2EX%d" % i, [128, 512], F32), Res()) for i in range(2)]
                    MT = [(self.sb(st, "s2MT%d" % i, [128, 16, 128], BF16), Res()) for i in range(2)]
                    T1 = self.sb(st, "s2T1", [128, 512], F32)
                    T2 = self.sb(st, "s2T2", [128, 512], F32)
                    T3 = self.sb(st, "s2T3", [128, 512], F32)
                    YTM = [(self.sb(st, "s2Y%d" % i, [128, 512], F32), Res()) for i in range(2)]
                    YF = [(self.sb(st, "s2YF%d" % i, [128, 4, 512], F32), Res()) for i in range(1)]
                    Sf = self.sb(st, "s2S", [128, 512], F32)
                    pvl = [(self.sb(st, "s2pv%d" % i, [128, 512], BF16), Res()) for i in range(2)]
                    Sfb = self.sb(st, "s2Sb", [128, 512], BF16)
                    tmp = self.sb(st, "s2T", [128, 512], F32)
                    GZ = [(self.sb(st, "s2GZ%d" % i, [128, 512], F32), Res()) for i in range(2)]
                    YG = [(self.sb(st, "s2YG%d" % i, [128, 512], F32), Res()) for i in range(4)]
                    SQ = [(self.sb(st, "s2SQ%d" % i, [128, 512], F32), Res()) for i in range(2)]
                    RS = self.sb(st, "s2RS", [128, 512], F32)
                    OUTS = [(self.sb(st, "s2O%d" % i, [128, 512], F32), Res()) for i in range(2)]
                    r_T1, r_T2, r_T3, r_Sf, r_Sfb, r_tmp, r_RS = [Res() for _ in range(7)]
                    P.op("dve", lambda e: e.memset(Sf[:], 0.0), writes=[r_Sf])
                    P.op("dve", lambda e: e.memset(Sfb[:], 0.0), writes=[r_Sfb])
                    k = 0
                    for sbi in range(L // SB_):
                        tiles = tl[sbi % 2]
                        load_x(st, tiles, sbi * SB_, [("x0", 0), ("x1", 1), ("x2", 2), ("x3", 3), ("B", 4), ("C", 5)])
                        for j in range(4):
                            t, r_t = tiles["z%d" % j]
                            r0 = Z0 + g * 512 + j * 128
                            P.dma("sp", lambda e, t=t, r0=r0, sbi=sbi: e.dma_start(out=t[:], in_=PT[r0:r0 + 128, sbi * SB_:(sbi + 1) * SB_]), writes=[r_t])
                        (bb, r_bb), (cb, r_cb) = Bb[sbi % 2], Cb[sbi % 2]
                        P.op("act", lambda e, bb=bb, tiles=tiles: e.activation(out=bb[:], in_=tiles["B"][0][:], func=AF.Copy), reads=[tiles["B"][1]], writes=[r_bb])
                        P.op("act", lambda e, cb=cb, tiles=tiles: e.activation(out=cb[:], in_=tiles["C"][0][:], func=AF.Copy), reads=[tiles["C"][1]], writes=[r_cb])
                        yf, r_yf = YF[0]
                        for cc in range(4):
                            c = sbi * 4 + cc
                            i2 = k % 2
                            k += 1
                            (Xs, r_Xs), (Bt, r_Bt), (xw, r_xw) = Xs2[i2], Bt2[i2], xw2[i2]
                            (xdf, r_xdf), (xdb, r_xdb) = xd2[i2]
                            (cbt, r_cbt), (rd, r_rd), (mt, r_mt), (ytm, r_ytm) = CBT[i2], RD[0], MT[i2], YTM[i2]
                            xtm(tiles, cc, Xs, r_Xs, Bt, r_Bt)
                            pvt, r_pv = pvl[i2]
                            P.dma("sp", lambda e, pvt=pvt, c=c: e.dma_start(out=pvt[:], in_=self.PRVd[c]), writes=[r_pv])
                            X3 = Xs[:].rearrange("p (h q) -> p h q", q=64)
                            for dst, r_dst, tabap in ((xdf, r_xdf, DT[:, c, 0:8]), (xdb, r_xdb, DT[:, c, 8:16]), (xw, r_xw, EOW[:, c, 0:8])):
                                P.op("dve", lambda e, dst=dst, X3=X3, tabap=tabap: e.tensor_tensor(
                                    out=dst[:].rearrange("p (h q) -> p h q", q=64), in0=X3, in1=bc3(tabap, 64), op=ALU.mult),
                                    reads=[r_Xs, r_tab], writes=[r_dst])
                            ps1, r_ps1 = self.ps[1]
                            P.op("pe", lambda e, bb=bb, cb=cb, cc=cc: e.matmul(ps1[:, 128:256], lhsT=bb[:, cc * 128:(cc + 1) * 128], rhs=cb[:, cc * 128:(cc + 1) * 128],
                                                                           start=True, stop=True), reads=[r_bb, r_cb], writes=[r_ps1])
                            P.op("act", lambda e, cbt=cbt: e.activation(out=cbt[:], in_=ps1[:, 128:256], func=AF.Copy), reads=[r_ps1], writes=[r_cbt])
                            P.op("dve", lambda e, rd=rd, c=c: e.tensor_tensor(
                                out=rd[:], in0=self.ident[:].rearrange("p (o t) -> p o t", o=1).broadcast_to([128, 16, 128]),
                                in1=bc3(ACS[:, c, :], 128), op=ALU.mult), reads=[self.r_const, r_tab], writes=[r_rd])
                            for q in range(4):
                                d = q // 2
                                psq, r_psq = self.ps[2 + q % 2]
                                (arg, r_arg), (ex, r_ex) = ARG[q % 2], EX[q % 2]
                                P.op("pe", lambda e, psq=psq, rd=rd, q=q: e.matmul(psq[:], lhsT=self.ones[:], rhs=rd[:, 4 * q:4 * q + 4, :], start=True, stop=False),
                                     reads=[r_rd, self.r_const], writes=[r_psq])
                                P.op("pe", lambda e, psq=psq, d=d: e.matmul(psq[:], lhsT=self.identb[:], rhs=NMB[:, d, :, :], start=False, stop=True),
                                     reads=[r_sc, self.r_const], writes=[r_psq])
                                P.op("dve", lambda e, psq=psq, arg=arg, c=c, q=q: e.tensor_tensor(
                                    out=arg[:].rearrange("p (h t) -> p h t", t=128), in0=psq[:].rearrange("p (h t) -> p h t", t=128),
                                    in1=bc3(ACS[:, c, 4 * q:4 * q + 4], 128), op=ALU.subtract), reads=[r_psq, r_tab], writes=[r_arg])
                                P.op("act", lambda e, arg=arg, ex=ex: e.activation(out=ex[:], in_=arg[:], func=AF.Exp), reads=[r_arg], writes=[r_ex])
                                P.op("dve", lambda e, ex=ex, mt=mt, cbt=cbt, q=q: e.tensor_tensor(
                                    out=mt[:, 4 * q:4 * q + 4, :], in0=ex[:].rearrange("p (h t) -> p h t", t=128),
                                    in1=cbt[:].rearrange("p (o t) -> p o t", o=1).broadcast_to([128, 4, 128]), op=ALU.mult),
                                    reads=[r_ex, r_cbt], writes=[r_mt])
                            psY = [self.ps[4], self.ps[5]]
                            for hd in range(16):
                                d, h = hd // 8, hd % 8
                                xd, r_xd = (xdf, r_xdf) if d == 0 else (xdb, r_xdb)
                                P.op("pe", lambda e, d=d, h=h, hd=hd, mt=mt, xd=xd: e.matmul(
                                    psY[d][0][:, h * 64:(h + 1) * 64], lhsT=mt[:, hd, :], rhs=xd[:, h * 64:(h + 1) * 64], start=True, stop=True),
                                    reads=[r_mt, r_xd], writes=[psY[d][1]])
                            psF, r_psF = self.ps[6]
                            psB, r_psB = self.ps[7]
                            P.op("pe", lambda e, cb=cb, cc=cc: e.matmul(psF[:], lhsT=cb[:, cc * 128:(cc + 1) * 128], rhs=Sfb[:], start=True, stop=True),
                                 reads=[r_cb, r_Sfb], writes=[r_psF])
                            P.op("pe", lambda e, cb=cb, cc=cc, pvt=pvt: e.matmul(psB[:], lhsT=cb[:, cc * 128:(cc + 1) * 128], rhs=pvt[:], start=True, stop=True),
                                 reads=[r_cb, r_pv], writes=[r_psB])
                            v3 = lambda ap: ap.rearrange("p (h q) -> p h q", q=64)
                            P.op("dve", lambda e, c=c: e.tensor_tensor(out=v3(T1[:]), in0=v3(psF[:]), in1=bc3(EIN[:, c, 0:8], 64), op=ALU.mult),
                                 reads=[r_psF, r_tab], writes=[r_T1])
                            P.op("dve", lambda e: e.tensor_tensor(out=T1[:], in0=T1[:], in1=psY[0][0][:], op=ALU.add), reads=[psY[0][1], r_T1], writes=[r_T1])
                            P.op("dve", lambda e, c=c: e.tensor_tensor(out=v3(T2[:]), in0=v3(psB[:]), in1=bc3(EIN[:, c, 8:16], 64), op=ALU.mult),
                                 reads=[r_psB, r_tab], writes=[r_T2])
                            P.op("dve", lambda e: e.tensor_tensor(out=T2[:], in0=T2[:], in1=psY[1][0][:], op=ALU.add), reads=[psY[1][1], r_T2], writes=[r_T2])
                            P.op("pool", lambda e, X3=X3: e.tensor_tensor(out=v3(T3[:]), in0=X3, in1=bc3(DSK[:, 0:8], 64), op=ALU.mult),
                                 reads=[r_Xs, r_sc], writes=[r_T3])
                            P.op("pool", lambda e: e.tensor_tensor(out=T3[:], in0=T3[:], in1=T1[:], op=ALU.add), reads=[r_T1, r_T3], writes=[r_T3])
                            P.op("pool", lambda e, ytm=ytm: e.tensor_tensor(out=ytm[:], in0=T3[:], in1=T2[:], op=ALU.add), reads=[r_T2, r_T3], writes=[r_ytm])
                            ps0, r_ps0 = self.ps[0]
                            P.op("pe", lambda e, Bt=Bt, xw=xw: e.matmul(ps0[:], lhsT=Bt[:], rhs=xw[:], start=True, stop=True), reads=[r_Bt, r_xw], writes=[r_ps0])
                            P.op("dve", lambda e, c=c: e.tensor_tensor(out=v3(tmp[:]), in0=v3(Sf[:]), in1=bc3(DEC[:, c, 0:8], 64), op=ALU.mult),
                                 reads=[r_Sf, r_tab], writes=[r_tmp])
                            P.op("dve", lambda e: e.tensor_tensor(out=Sf[:], in0=tmp[:], in1=ps0[:], op=ALU.add), reads=[r_tmp, r_ps0], writes=[r_Sf])
                            P.op("act", lambda e: e.activation(out=Sfb[:], in_=Sf[:], func=AF.Copy), reads=[r_Sf], writes=[r_Sfb])
                            for j in range(4):
                                P.op("pe", lambda e, ytm=ytm, j=j: e.matmul(ps1[:, j * 128:(j + 1) * 128] if False else self.ps[1][0][:, j * 128:(j + 1) * 128],
                                                                           lhsT=ytm[:, j * 128:(j + 1) * 128], rhs=self.ident[:], start=True, stop=True),
                                     reads=[r_ytm, self.r_const], writes=[r_ps1])
                            P.op("act", lambda e, yf=yf, cc=cc: e.activation(out=yf[:, :, cc * 128:(cc + 1) * 128],
                                                                          in_=ps1[:].rearrange("p (j t) -> p j t", t=128), func=AF.Copy),
                                 reads=[r_ps1], writes=[r_yf])
                        ps7, r_ps7 = self.ps[7]
                        for j in range(4):
                            (gz, r_gz), (yg, r_yg), (sq, r_sq) = GZ[j % 2], YG[j], SQ[j % 2]
                            zt, r_zt = tiles["z%d" % j]
                            P.op("act", lambda e, gz=gz, zt=zt: e.activation(out=gz[:], in_=zt[:], func=AF.Silu), reads=[r_zt], writes=[r_gz])
                            P.op("pool", lambda e, yg=yg, gz=gz, yf=yf, j=j: e.tensor_tensor(out=yg[:], in0=yf[:, j, :], in1=gz[:], op=ALU.mult),
                                 reads=[r_yf, r_gz], writes=[r_yg])
                            P.op("act", lambda e, sq=sq, yg=yg: e.activation(out=sq[:], in_=yg[:], func=AF.Square), reads=[r_yg], writes=[r_sq])
                            P.op("pe", lambda e, sq=sq, j=j: e.matmul(ps7[:], lhsT=self.ones[:], rhs=sq[:], start=(j == 0), stop=(j == 3)),
                                 reads=[r_sq, self.r_const], writes=[r_ps7])
                        P.op("act", lambda e: e.activation(out=RS[:], in_=ps7[:], func=AF.Sqrt, bias=EPS, scale=1.0 / 512), reads=[r_ps7], writes=[r_RS])
                        P.op("dve", lambda e: e.reciprocal(out=RS[:], in_=RS[:]), reads=[r_RS], writes=[r_RS])
                        for j in range(4):
                            (o_, r_o), (yg, r_yg) = OUTS[j % 2], YG[j]
                            col = g * 4 + j
                            P.op("dve", lambda e, o_=o_, yg=yg, col=col: e.scalar_tensor_tensor(
                                out=o_[:], in0=yg[:], scalar=NG[:, col:col + 1], in1=RS[:], op0=ALU.mult, op1=ALU.mult),
                                reads=[r_yg, r_RS, r_sc], writes=[r_o])
                            r0 = 1536 + g * 512 + j * 128
                            P.dma("sp", lambda e, o_=o_, r0=r0, sbi=sbi: e.dma_start(out=YT[r0:r0 + 128, sbi * SB_:(sbi + 1) * SB_], in_=o_[:]), reads=[r_o])
                    P.flush()


    def emit_merge(self, l, PT, YT, MT):
        nc, P, L = self.nc, self.P, self.L
        if l == 0:
            self.mergew = self.din("mergew", [self.depth, 32, 128, 44 * 128])
            self.gateb = self.din("gateb", [self.depth, 128, 96])
        KB = (12, 16, 4)
        YOFF = (0, 12, 28)
        WOFF = (12, 24, 40)
        for tt in range(L // TT):
            t0 = tt * TT
            with contextlib.ExitStack() as st:
                yT = self.sb(st, "myT", [128, 32, TT], BF16)
                gT = self.sb(st, "mgT", [128, 4, TT], BF16)
                gb = self.sb(st, "mgb", [128, 96], F32)
                r_y = [Res() for _ in range(32)]
                r_g = [Res() for _ in range(4)]
                r_gb = Res()
                P.dma("sp", lambda e: e.dma_start(out=gb[:], in_=self.gateb[l]), writes=[r_gb])
                for c in range(32):
                    P.dma("pool", lambda e, c=c: e.dma_start(out=yT[:, c, :], in_=YT[c * 128:(c + 1) * 128, t0:t0 + TT]), writes=[r_y[c]])
                for c in range(4):
                    P.dma("pool", lambda e, c=c: e.dma_start(out=gT[:, c, :], in_=PT[GL0 + c * 128:GL0 + (c + 1) * 128, t0:t0 + TT]), writes=[r_g[c]])
                ring = [(self.sb(st, "mw%d" % i, [128, 44 * 128], BF16), Res()) for i in range(2)]
                sig = [(self.sb(st, "msg%d" % i, [128, 512], F32), Res()) for i in range(2)]
                acc = [(self.sb(st, "mac%d" % i, [128, 512], F32), Res()) for i in range(2)]
                mo = [(self.sb(st, "mo%d" % i, [128, 512], BF16), Res()) for i in range(2)]
                k = 0
                kk = 0
                for n in range(32):
                    wt, r_w = ring[n % 2]
                    P.dma("pool", lambda e, wt=wt, n=n: e.dma_start(out=wt[:], in_=self.mergew[l, n]), writes=[r_w])
                    for tb in range(TT // 512):
                        (a_, r_a), (o_, r_o) = acc[kk % 2], mo[kk % 2]
                        kk += 1
                        for br in range(3):
                            psG, r_psG = self.ps[(k % 4) * 2]
                            psY, r_psY = self.ps[(k % 4) * 2 + 1]
                            sg, r_sg = sig[k % 2]
                            k += 1
                            for kc in range(4):
                                P.op("pe", lambda e, psG=psG, wt=wt, br=br, kc=kc, tb=tb: e.matmul(
                                    psG[:], lhsT=wt[:, (br * 4 + kc) * 128:(br * 4 + kc + 1) * 128], rhs=gT[:, kc, tb * 512:(tb + 1) * 512],
                                    start=(kc == 0), stop=(kc == 3)), reads=[r_w, r_g[kc]], writes=[r_psG])
                            for kc in range(KB[br]):
                                P.op("pe", lambda e, psY=psY, wt=wt, br=br, kc=kc, tb=tb: e.matmul(
                                    psY[:], lhsT=wt[:, (WOFF[br] + kc) * 128:(WOFF[br] + kc + 1) * 128], rhs=yT[:, YOFF[br] + kc, tb * 512:(tb + 1) * 512],
                                    start=(kc == 0), stop=(kc == KB[br] - 1)), reads=[r_w, r_y[YOFF[br] + kc]], writes=[r_psY])
                            col = br * 32 + n
                            P.op("act", lambda e, sg=sg, psG=psG, col=col: e.activation(out=sg[:], in_=psG[:], func=AF.Sigmoid, bias=gb[:, col:col + 1], scale=1.0),
                                 reads=[r_psG, r_gb], writes=[r_sg])
                            if br == 0:
                                P.op("dve", lambda e, a_=a_, sg=sg, psY=psY: e.tensor_tensor(out=a_[:], in0=psY[:], in1=sg[:], op=ALU.mult),
                                     reads=[r_psY, r_sg], writes=[r_a])
                            else:
                                P.op("dve", lambda e, sg=sg, psY=psY: e.tensor_tensor(out=sg[:], in0=psY[:], in1=sg[:], op=ALU.mult),
                                     reads=[r_psY, r_sg], writes=[r_sg])
                                dst, r_dst = (a_, r_a) if br == 1 else (o_, r_o)
                                P.op("pool", lambda e, a_=a_, sg=sg, dst=dst: e.tensor_tensor(out=dst[:], in0=a_[:], in1=sg[:], op=ALU.add),
                                     reads=[r_sg, r_a], writes=[r_dst])
                        P.dma("sp", lambda e, o_=o_, n=n, c0=t0 + tb * 512: e.dma_start(out=MT[n * 128:(n + 1) * 128, c0:c0 + 512], in_=o_[:]), reads=[r_o])
                P.flush()

    def emit_res_linear(self, l, HT, KC, T, wsl, xT, gcol0, cast):
        nc, P, L = self.nc, self.P, self.L
        for tt in range(L // T):
            t0 = tt * T
            with contextlib.ExitStack() as st:
                hT = self.sb(st, "rhT", [128, KC, T], BF16)
                r_h = [Res() for _ in range(KC)]
                for c in range(KC):
                    P.dma("pool" if cast else "sp", lambda e, c=c: e.dma_start(out=hT[:, c, :], in_=HT[c * 128:(c + 1) * 128, t0:t0 + T]), writes=[r_h[c]])
                xo = [(self.sb(st, "rxo%d" % i, [128, 512], F32), Res()) for i in range(4)]
                xn = [(self.sb(st, "rxn%d" % i, [128, 512], F32), Res()) for i in range(4)]
                k = [0]

                def epi(n, tb, pst, r_ps, width):
                    (o_, r_o), (n_, r_n) = xo[k[0] % 4], xn[k[0] % 4]
                    k[0] += 1
                    c0 = t0 + tb * 512
                    P.dma("sp", lambda e: e.dma_start(out=o_[:], in_=xT[n * 128:(n + 1) * 128, c0:c0 + 512]), writes=[r_o])
                    P.op("dve", lambda e: e.scalar_tensor_tensor(out=n_[:], in0=pst[:], scalar=self.modL[:, gcol0 + n:gcol0 + n + 1], in1=o_[:],
                                                                op0=ALU.mult, op1=ALU.add), reads=[r_ps, r_o, self.r_modL], writes=[r_n])
                    P.dma("sp", lambda e: e.dma_start(out=xT[n * 128:(n + 1) * 128, c0:c0 + 512], in_=n_[:]), reads=[r_n])
                self.emit_linear(st, hT, r_h, KC, T, wsl, [128] * 32, epi, nring=(3 if KC <= 32 else 2), tag="rw")
                P.flush()

    def emit_ffn(self, l, xT, UT, GT, parts=("up", "conv", "down")):
        nc, P, L = self.nc, self.P, self.L
        if l == 0:
            self.mlp_up = self.din("mlp_up", [self.depth, 128, 128, D])
            self.mlp_down = self.din("mlp_down", [self.depth, 32, 128, MLP_H])
            self.mconv = self.din("mconv", [self.depth, 128, 128, 4])
        for tt in (range(L // TT) if "up" in parts else ()):
            t0 = tt * TT
            with contextlib.ExitStack() as st:
                hT = self.sb(st, "fhT", [128, KCD, TT], BF16)
                r_h = [Res() for _ in range(KCD)]
                with contextlib.ExitStack() as st2:
                    self.emit_norm(st2, xT, None, t0, TT, hT, r_h, 32, 96)
                    P.flush()
                stg = [(self.sb(st, "fstg%d" % i, [128, 512], F32), Res()) for i in range(4)]
                k = [0]

                def epi(n, tb, pst, r_ps, width):
                    s_, r_s = stg[k[0] % 4]
                    k[0] += 1
                    P.op("act", lambda e: e.activation(out=s_[:], in_=pst[:], func=AF.Copy), reads=[r_ps], writes=[r_s])
                    P.dma("sp", lambda e: e.dma_start(out=UT[n * 128:(n + 1) * 128, t0 + tb * 512:t0 + (tb + 1) * 512], in_=s_[:]), reads=[r_s])
                self.emit_linear(st, hT, r_h, KCD, TT, self.mlp_up[l], [128] * 128, epi, tag="fw")
                P.flush()
        TS = min(L, 4096)
        with contextlib.ExitStack() as st:
            cw = self.sb(st, "fcw", [128, 128, 4], F32)
            r_cw = Res()
            P.dma("sp", lambda e: e.dma_start(out=cw[:], in_=self.mconv[l]), writes=[r_cw])
            Uu = [(self.sb(st, "fU%d" % i, [128, TS + 2], F32), Res()) for i in range(2)]
            Uv = [(self.sb(st, "fV%d" % i, [128, TS + 2], F32), Res()) for i in range(2)]
            Au = [(self.sb(st, "fAu%d" % i, [128, TS], F32), Res()) for i in range(2)]
            Av = [(self.sb(st, "fAv%d" % i, [128, TS], F32), Res()) for i in range(2)]
            Go = [(self.sb(st, "fG%d" % i, [128, TS], BF16), Res()) for i in range(2)]
            k = 0
            for j in (range(64) if "conv" in parts else ()):
                for t0 in range(0, L, TS):
                    i2 = k % 2
                    k += 1
                    lo, hi = max(0, t0 - 1), min(L, t0 + TS + 1)
                    for (u, r_u), (a, r_a), ch in ((Uu[i2], Au[i2], j), (Uv[i2], Av[i2], 64 + j)):
                        if lo > t0 - 1:
                            P.op("pool", lambda e, u=u: e.memset(u[:, 0:1], 0.0), writes=[r_u])
                        if hi < t0 + TS + 1:
                            P.op("pool", lambda e, u=u: e.memset(u[:, TS + 1:TS + 2], 0.0), writes=[r_u])
                        P.dma("sp", lambda e, u=u, ch=ch, lo=lo, hi=hi, t0=t0: e.dma_start(
                            out=u[:, lo - (t0 - 1):hi - (t0 - 1)], in_=UT[ch * 128:(ch + 1) * 128, lo:hi]), writes=[r_u])
                        P.op("act", lambda e, u=u, a=a, ch=ch: e.activation(out=a[:], in_=u[:, 0:TS], func=AF.Identity,
                                                                           bias=cw[:, ch, 3:4], scale=cw[:, ch, 0:1]), reads=[r_u, r_cw], writes=[r_a])
                        for tap in (1, 2):
                            P.op("dve", lambda e, u=u, a=a, ch=ch, tap=tap: e.scalar_tensor_tensor(
                                out=a[:], in0=u[:, tap:tap + TS], scalar=cw[:, ch, tap:tap + 1], in1=a[:], op0=ALU.mult, op1=ALU.add),
                                reads=[r_u, r_cw, r_a], writes=[r_a])
                    (au, r_au), (av, r_av), (go, r_go) = Au[i2], Av[i2], Go[i2]
                    P.op("act", lambda e, au=au: e.activation(out=au[:], in_=au[:], func=AF.Silu), reads=[r_au], writes=[r_au])
                    P.op("pool", lambda e, au=au, av=av, go=go: e.tensor_tensor(out=go[:], in0=au[:], in1=av[:], op=ALU.mult), reads=[r_au, r_av], writes=[r_go])
                    P.dma("sp", lambda e, go=go, j=j, t0=t0: e.dma_start(out=GT[j * 128:(j + 1) * 128, t0:t0 + TS], in_=go[:]), reads=[r_go])
            P.flush()
        if "down" in parts:
            self.emit_res_linear(l, GT, 64, 1024, self.mlp_down[l], xT, 160, cast=False)

    def emit_model(self, xin, xT):
        nc, P, L = self.nc, self.P, self.L
        for c in range(KCD):
            P.dma("sp", lambda e, c=c: e.dma_start(out=xT[c * 128:(c + 1) * 128, :], in_=xin[c * 128:(c + 1) * 128, :]))
        self.emit_adaln()
        PT = RowSplit(self.nc, "PT", [0, XBC0, QKV0, N_IN], L, F32)
        YT = self.nc.dram_tensor("YT", [D, L], F32).ap()
        MT = self.nc.dram_tensor("MT", [D, L], BF16).ap()
        UT = RowSplit(self.nc, "UT", [0, 4096, 8192, 12288, 16384], L, F32)
        GT = self.nc.dram_tensor("GT", [MLP_H, L], BF16).ap()
        for l in range(self.depth):
            self.emit_layer_mod(l)
            self.emit_inproj(l, xT, None, PT, None)
            self.emit_pool(l, PT, YT)
            self.emit_ssd(l, PT, YT)
            self.emit_attn(l, PT, YT)
            self.emit_merge(l, PT, YT, MT)
            if l == 0:
                self.w_out = self.din("w_out", [self.depth, 32, 128, D])
            self.emit_res_linear(l, MT, KCD, TT, self.w_out[l], xT, 64, cast=False)
            self.emit_ffn(l, xT, UT, GT)
        P.flush(final=True)


def host_pool(inp, L, depth):
    pos = np.arange(L)
    ic = []
    for win in POOL_WINDOWS:
        lo = np.clip(pos - win // 2, 0, L)
        hi = np.clip(pos + win - win // 2, 0, L)
        ic.append(np.broadcast_to((1.0 / (hi - lo).astype(np.float32))[None, :], (128, L)))
    return {
        "pool_w": np.ascontiguousarray(inp["pool_w"][:depth]),
        "poolsc": np.stack([cols(inp["pool_scale"][l]) for l in range(depth)]),
        "invcnt": np.ascontiguousarray(np.stack(ic)).astype(np.float32),
    }


def _t5_buckets(rel):
    half, max_exact = 16, 8
    n = np.abs(rel)
    large = max_exact + (np.log(np.maximum(n, max_exact) / max_exact) / np.log(1024 / max_exact) * (half - max_exact)).astype(np.int32)
    large = np.minimum(large, half - 1)
    return (rel > 0).astype(np.int32) * half + np.where(n < max_exact, n, large).astype(np.int32)


def host_attn(inp, L, depth):
    q = np.arange(128)[:, None]
    j = np.arange(256)[None, :]
    delta = j - 64 - q
    inwin = np.abs(delta) <= 64
    out = np.empty((12, 128, 4, 256), np.float32)
    tab = inp["t5_table"]
    for gi, dil in enumerate((1, 4, 16)):
        bk = _t5_buckets(delta * dil)
        for hh in range(4):
            vals = tab[bk, gi * 4 + hh].astype(np.float32)
            for var in range(4):
                ok = inwin.copy()
                if var & 1:
                    ok &= (j >= 64)
                if var & 2:
                    ok &= (j < 192)
                out[gi * 4 + hh, :, var, :] = np.where(ok, vals, np.float32(-1e30))
    return {
        "bmask": out,
        "qkn": np.stack([np.stack([inp["q_norm"][l], inp["k_norm"][l]], axis=1) for l in range(depth)]).astype(np.float32),
    }


def host_ssd(inp, L, depth):
    k = np.arange(128)[:, None]
    t = np.arange(128)[None, :]
    sc = np.stack([(k <= t).astype(np.float32), (k >= t).astype(np.float32),
                   np.where(k <= t, 0.0, -30000.0).astype(np.float32), np.where(k >= t, 0.0, -30000.0).astype(np.float32)], axis=1)
    ssdv = np.empty((depth, 4, 16, 2), np.float32)
    convw = np.empty((depth, 128, 24, 6), np.float32)
    dskip = np.empty((depth, 4, 128, 8), np.float32)
    for l in range(depth):
        for g in range(4):
            for d in range(2):
                ssdv[l, g, d * 8:(d + 1) * 8, 0] = inp["ssd_dt_bias"][l, d, g * 8:(g + 1) * 8]
                ssdv[l, g, d * 8:(d + 1) * 8, 1] = inp["ssd_a_log"][l, d, g * 8:(g + 1) * 8]
            dskip[l, g] = np.broadcast_to(inp["ssd_d"][l, g * 8:(g + 1) * 8][None, :], (128, 8))
        for tap in range(5):
            convw[l, :, :, tap] = cols(inp["ssd_conv_w"][l, tap])
        convw[l, :, :, 5] = cols(inp["ssd_conv_b"][l])
    return {"ssdc": np.ascontiguousarray(sc), "ssdv": ssdv, "convw": convw, "dskip": dskip,
            "ssdn": np.stack([cols(inp["ssd_norm"][l]) for l in range(depth)])}


def host_dense(inp, L, depth):
    out = {}
    w_in = inp["w_in"][:depth]
    wpad = np.concatenate([w_in[:, :, :DT0 + 64], np.zeros((depth, D, 64), np.float32), w_in[:, :, DT0 + 64:]], axis=2)
    out["w_in"] = np.stack([slabs(wpad[l], KCD) for l in range(depth)])
    mw = np.empty((depth, 32, 128, 44 * 128), np.float32)
    for l in range(depth):
        gu = inp["gate_up"][l]
        parts = [slabs(gu[:, br * D:(br + 1) * D], 4) for br in range(3)]
        parts += [slabs(inp["proj_a"][l], 12), slabs(inp["proj_b"][l], 16), slabs(inp["proj_c"][l], 4)]
        mw[l] = np.concatenate(parts, axis=2)
    out["mergew"] = mw
    out["gateb"] = np.stack([cols(inp["gate_b"][l]) for l in range(depth)])
    out["w_out"] = np.stack([slabs(inp["w_out"][l], KCD) for l in range(depth)])
    out["mlp_up"] = np.stack([slabs(inp["mlp_up"][l], KCD) for l in range(depth)])
    out["mlp_down"] = np.stack([slabs(inp["mlp_down"][l], 64) for l in range(depth)])
    mc = np.empty((depth, 128, 128, 4), np.float32)
    for l in range(depth):
        for tap in range(3):
            mc[l, :, :, tap] = cols(inp["mlp_conv_w"][l, tap])
        mc[l, :, :, 3] = cols(inp["mlp_conv_b"][l])
    out["mconv"] = mc
    return out


def host_common(inp, b, depth):
    return {
        "ident_in": np.eye(128, dtype=np.float32),
        "xin": np.ascontiguousarray(inp["x"][b].T),
        "ccol": cols(inp["c"][b]),
        "adaw": slabs(inp["ada_w"], KCD),
        "adab": cols(inp["ada_b"]),
        "adal": np.stack([cols(inp["ada_layer"][l]) for l in range(DEPTH)]),
        "gnorm": np.stack([np.concatenate([cols(inp["norm_mix"][l]), cols(inp["norm_mlp"][l])], axis=1) for l in range(DEPTH)]),
    }


_CACHE = {}


def build(L, depth):
    key = (L, depth)
    if key not in _CACHE:
        b = Builder(L, depth)
        b.setup()
        xin = b.din("xin", [D, L])
        xT = b.nc.dram_tensor("xT", [D, L], F32, kind="ExternalOutput").ap()
        b.emit_model(xin, xT)
        _CACHE[key] = b
    return _CACHE[key]


def run_model(inp, L, depth, trace=False):
    B = inp["x"].shape[0]
    b = build(L, depth)
    shared = {}
    shared.update(host_dense(inp, L, depth))
    shared.update(host_pool(inp, L, depth))
    shared.update(host_ssd(inp, L, depth))
    shared.update(host_attn(inp, L, depth))
    in_maps = []
    for bi in range(B):
        m = dict(shared)
        m.update(host_common(inp, bi, depth))
        in_maps.append(m)
    res = run_bass_kernel_spmd(b.nc, in_maps, core_ids=list(range(B)), trace=trace)
    out = np.stack([np.ascontiguousarray(res.results[bi]["xT"].T) for bi in range(B)])
    return out, res


def kernel(**inputs):
    inp = {k: np.asarray(v) for k, v in inputs.items()}
    out, _ = run_model(inp, SEQ, DEPTH)
    return out.astype(np.float32)
```

```python
import contextlib
import numpy as np
import concourse.bass as bass
import concourse.mybir as mybir
from concourse.bass_utils import run_bass_kernel_spmd

F32 = mybir.dt.float32
BF16 = mybir.dt.bfloat16
ALU = mybir.AluOpType
AF = mybir.ActivationFunctionType
AX = mybir.AxisListType

N_DMA_SEMS = 6


class Res:
    __slots__ = ("name", "w", "r")

    def __init__(self, name=""):
        self.name = name
        self.w = None
        self.r = []


class Prog:
    ENGS = ("pe", "act", "dve", "pool", "sp")

    def __init__(self, nc):
        self.nc = nc
        self.q = {e: [] for e in self.ENGS}
        self.cnt = {e: 0 for e in self.ENGS}
        self.waited = {e: {} for e in self.ENGS}
        self.dma_n = {e: 0 for e in ("sp", "pool", "act")}
        self.sems = {}
        self.dma_tokens = []

    def _need(self, eng, toks):
        out = []
        wd = self.waited[eng]
        for t in toks:
            if t is None:
                continue
            k, v = t
            if k == eng and eng == "pe":
                continue
            if wd.get(k, 0) >= v:
                continue
            wd[k] = v
            out.append((k, v))
        best = {}
        for k, v in out:
            best[k] = max(best.get(k, 0), v)
        return list(best.items())

    def _deps(self, reads, writes):
        toks = []
        for r in reads:
            toks.append(r.w)
        for w in writes:
            toks.append(w.w)
            toks.extend(w.r)
        return toks

    def _commit(self, tok, reads, writes):
        for r in reads:
            r.r.append(tok)
        for w in writes:
            w.w = tok
            w.r = []

    def op(self, eng, fn, reads=(), writes=()):
        waits = self._need(eng, self._deps(reads, writes))
        self.cnt[eng] += 1
        tok = (eng, self.cnt[eng])
        self.q[eng].append((waits, fn, ("c", eng)))
        self._commit(tok, reads, writes)
        return tok

    def dma(self, queue, fn, reads=(), writes=()):
        j = self.dma_n[queue]
        self.dma_n[queue] += 1
        s = j % N_DMA_SEMS
        semkey = (queue, s)
        val = 16 * (j // N_DMA_SEMS + 1)
        deps = self._deps(reads, writes)
        if j >= N_DMA_SEMS:
            deps.append((semkey, val - 16))
        waits = self._need(queue, deps)
        tok = (semkey, val)
        self.q[queue].append((waits, fn, ("d", semkey)))
        self._commit(tok, reads, writes)
        self.dma_tokens.append(tok)
        return tok

    def barrier(self, final=False):
        last = {}
        for k, v in self.dma_tokens:
            last[k] = max(last.get(k, 0), v)
        toks = list(last.items())
        for e in ("pe", "act", "dve", "pool"):
            if self.cnt[e]:
                toks.append((e, self.cnt[e]))
        for eng in (("sp",) if final else self.ENGS):
            waits = self._need(eng, [t for t in toks if t[0] != eng])
            if waits:
                self.q[eng].append((waits, None, None))

    def alloc_sems(self, st):
        nc = self.nc
        self.semh = {}
        for e in ("pe", "act", "dve", "pool"):
            self.semh[e] = st.enter_context(nc.semaphore("s_" + e))
        for qn in ("sp", "pool", "act"):
            for i in range(N_DMA_SEMS):
                self.semh[(qn, i)] = st.enter_context(nc.semaphore("d_%s%d" % (qn, i)))

    def flush(self, final=False):
        nc = self.nc
        self.barrier(final=False)
        if final:
            self.barrier(final=True)
        semh = self.semh
        q = self.q
        self.q = {e: [] for e in self.ENGS}

        def run(eng, items):
            for waits, fn, inc in items:
                for k, v in waits:
                    eng.wait_ge(semh[k], v)
                if fn is None:
                    continue
                ins = fn(eng)
                if inc[0] == "c":
                    ins.then_inc(semh[inc[1]], 1)
                else:
                    ins.then_inc(semh[inc[1]], 16)

        with nc.Block() as block:
            @block.sync
            def _(e):
                run(e, q["sp"])

            @block.tensor
            def _(e):
                run(e, q["pe"])

            @block.scalar
            def _(e):
                run(e, q["act"])

            @block.vector
            def _(e):
                run(e, q["dve"])

            @block.gpsimd
            def _(e):
                run(e, q["pool"])

D = 4096
KCD = D // 128
DEPTH = 4
SEQ = 8192
POOL_WINDOWS = (2, 4, 8, 16)
POOL_W = 1536
SSD_INNER = 2048
SSD_XBC = 3072
ATT_QKV = 4608
GATE_RANK = 512
N_IN = 11840
A0, Z0, XBC0, DT0, QKV0, GL0 = 0, 1536, 3584, 6656, 6720, 11328
MLP_H = 8192
EPS = 1e-6
W_IN_CHUNKS = [(128 * i, 128) for i in range(52)] + [(DT0, 64)] + [(QKV0 + 128 * i, 128) for i in range(40)]
TT = 2048


def slabs(W, KC):
    K, N = W.shape
    return np.ascontiguousarray(W.reshape(KC, 128, N // 128, 128).transpose(2, 1, 0, 3)).reshape(N // 128, 128, KC * 128)


def cols(v):
    return np.ascontiguousarray(v.reshape(-1, 128).T)


class RowSplit:
    def __init__(self, nc, name, bounds, L, dt):
        self.parts = []
        for i in range(len(bounds) - 1):
            r0, r1 = bounds[i], bounds[i + 1]
            self.parts.append((r0, r1, nc.dram_tensor("%s_%d" % (name, i), [r1 - r0, L], dt).ap()))

    def __getitem__(self, key):
        rs, cs = key
        for r0, r1, ap in self.parts:
            if r0 <= rs.start and rs.stop <= r1:
                return ap[rs.start - r0:rs.stop - r0, cs]
        raise IndexError((rs, cs))


class Builder:
    def __init__(self, L, depth, debug=None):
        self.L = L
        self.depth = depth
        self.debug = debug or ()
        self.nc = bass.Bass("TRN2", target_bir_lowering=False)
        self.P = Prog(self.nc)
        self.st = contextlib.ExitStack()
        self.dram = {}

    def din(self, name, shape, dt=F32):
        t = self.nc.dram_tensor(name, list(shape), dt, kind="ExternalInput").ap()
        self.dram[name] = t
        return t

    def dscr(self, name, shape, dt=F32):
        kind = "ExternalOutput" if name in self.debug else "Internal"
        t = self.nc.dram_tensor(name, list(shape), dt, kind=kind).ap()
        self.dram[name] = t
        return t, Res(name)

    def sb(self, st, name, shape, dt=F32):
        self._uid = getattr(self, "_uid", 0) + 1
        return st.enter_context(self.nc.sbuf_tensor("%s_%d" % (name, self._uid), list(shape), dt))

    def setup(self):
        nc, P, st = self.nc, self.P, self.st
        P.alloc_sems(st)
        self.ps = []
        for i in range(8):
            t = st.enter_context(nc.psum_tensor("ps%d" % i, [128, 512], F32))
            self.ps.append((t, Res("ps%d" % i)))
        self.ones = self.sb(st, "ones", [128, 128], F32)
        self.r_const = Res("const")
        P.op("dve", lambda e: e.memset(self.ones[:], 1.0), writes=[self.r_const])
        self.ident = self.sb(st, "ident", [128, 128], F32)
        self.identb = self.sb(st, "identb", [128, 128], BF16)
        idin = self.din("ident_in", [128, 128])
        P.dma("sp", lambda e: e.dma_start(out=self.ident[:], in_=idin), writes=[self.r_const])
        P.op("dve", lambda e: e.tensor_copy(out=self.identb[:], in_=self.ident[:]), reads=[self.r_const], writes=[self.r_const])
        self.modS = self.sb(st, "modS", [128, 192], F32)
        self.r_modS = Res("modS")
        self.modL = self.sb(st, "modL", [128, 192], F32)
        self.r_modL = Res("modL")
        self.Acol = self.sb(st, "Acol", [128, 64], F32)
        self.r_Acol = Res("Acol")

    def emit_adaln(self):
        nc, P = self.nc, self.P
        ccol = self.din("ccol", [128, KCD])
        adaw = self.din("adaw", [192, 128, D])
        adab = self.din("adab", [128, 192])
        with contextlib.ExitStack() as st:
            sc = self.sb(st, "sc", [128, KCD], F32)
            ab = self.sb(st, "ab", [128, 192], F32)
            r_sc, r_ab = Res(), Res()
            ring = [(self.sb(st, "aw%d" % i, [128, D], F32), Res()) for i in range(3)]
            P.dma("sp", lambda e: e.dma_start(out=sc[:], in_=ccol), writes=[r_sc])
            P.dma("sp", lambda e: e.dma_start(out=ab[:], in_=adab), writes=[r_ab])
            P.op("act", lambda e: e.activation(out=sc[:], in_=sc[:], func=AF.Silu), reads=[r_sc], writes=[r_sc])
            pst, r_ps = self.ps[0]
            for n in range(192):
                wt, r_w = ring[n % 3]
                P.dma("sp" if n % 2 == 0 else "act", lambda e, wt=wt, n=n: e.dma_start(out=wt[:], in_=adaw[n]), writes=[r_w])
                for kc in range(KCD):
                    P.op("pe", lambda e, wt=wt, n=n, kc=kc: e.matmul(pst[:, n:n + 1], lhsT=wt[:, kc * 128:(kc + 1) * 128],
                                                                     rhs=sc[:, kc:kc + 1], start=(kc == 0), stop=(kc == KCD - 1)),
                         reads=[r_w, r_sc], writes=[r_ps])
            P.op("dve", lambda e: e.tensor_tensor(out=self.modS[:], in0=pst[:, 0:192], in1=ab[:], op=ALU.add),
                 reads=[r_ps, r_ab], writes=[self.r_modS])
            P.flush()

    def emit_layer_mod(self, l):
        nc, P = self.nc, self.P
        if l == 0:
            self.adal = self.din("adal", [DEPTH, 128, 192])
            self.gnorm = self.din("gnorm", [DEPTH, 128, 64])
        with contextlib.ExitStack() as st:
            al = self.sb(st, "al", [128, 192], F32)
            gn = self.sb(st, "gn", [128, 64], F32)
            r_al, r_gn = Res(), Res()
            P.dma("sp", lambda e: e.dma_start(out=al[:], in_=self.adal[l]), writes=[r_al])
            P.dma("sp", lambda e: e.dma_start(out=gn[:], in_=self.gnorm[l]), writes=[r_gn])
            P.op("dve", lambda e: e.tensor_tensor(out=self.modL[:], in0=self.modS[:], in1=al[:], op=ALU.add),
                 reads=[self.r_modS, r_al], writes=[self.r_modL])
            for j, sc0 in ((0, 32), (1, 128)):
                P.op("dve", lambda e, j=j, sc0=sc0: e.scalar_tensor_tensor(
                    out=self.Acol[:, j * 32:(j + 1) * 32], in0=self.modL[:, sc0:sc0 + 32], scalar=1.0,
                    in1=gn[:, j * 32:(j + 1) * 32], op0=ALU.add, op1=ALU.mult),
                    reads=[self.r_modL, r_gn], writes=[self.r_Acol])
            P.flush()

    def emit_norm(self, st, xT, r_xT, t0, T, hT, r_h, acol0, shift0):
        nc, P = self.nc, self.P
        xs = [(self.sb(st, "nx%d" % i, [128, 512], F32), Res()) for i in range(4)]
        sq = [(self.sb(st, "nsq%d" % i, [128, 512], F32), Res()) for i in range(2)]
        rstd, r_rstd = self.sb(st, "nrstd", [128, 512], F32), Res()
        tmp = [(self.sb(st, "ntmp%d" % i, [128, 512], F32), Res()) for i in range(2)]
        cnt = 0
        for tb in range(T // 512):
            c0 = t0 + tb * 512
            pst, r_ps = self.ps[tb % 2]
            for kc in range(KCD):
                xt, r_x = xs[cnt % 4]
                sqt, r_sq = sq[cnt % 2]
                cnt += 1
                P.dma("sp", lambda e, xt=xt, kc=kc, c0=c0: e.dma_start(out=xt[:], in_=xT[kc * 128:(kc + 1) * 128, c0:c0 + 512]),
                      writes=[r_x])
                P.op("act", lambda e, xt=xt, sqt=sqt: e.activation(out=sqt[:], in_=xt[:], func=AF.Square), reads=[r_x], writes=[r_sq])
                P.op("pe", lambda e, sqt=sqt, kc=kc, pst=pst: e.matmul(pst[:], lhsT=self.ones[:], rhs=sqt[:], start=(kc == 0), stop=(kc == KCD - 1)),
                     reads=[r_sq, self.r_const], writes=[r_ps])
            P.op("act", lambda e, pst=pst: e.activation(out=rstd[:], in_=pst[:], func=AF.Sqrt, bias=EPS, scale=1.0 / D),
                 reads=[r_ps], writes=[r_rstd])
            P.op("dve", lambda e: e.reciprocal(out=rstd[:], in_=rstd[:]), reads=[r_rstd], writes=[r_rstd])
            for kc in range(KCD):
                xt, r_x = xs[cnt % 4]
                tt_, r_t = tmp[cnt % 2]
                cnt += 1
                P.dma("sp", lambda e, xt=xt, kc=kc, c0=c0: e.dma_start(out=xt[:], in_=xT[kc * 128:(kc + 1) * 128, c0:c0 + 512]),
                      writes=[r_x])
                P.op("dve", lambda e, xt=xt, tt_=tt_, kc=kc: e.scalar_tensor_tensor(
                    out=tt_[:], in0=xt[:], scalar=self.Acol[:, acol0 + kc:acol0 + kc + 1], in1=rstd[:], op0=ALU.mult, op1=ALU.mult),
                    reads=[r_x, r_rstd, self.r_Acol], writes=[r_t])
                P.op("act", lambda e, tt_=tt_, kc=kc, tb=tb: e.activation(
                    out=hT[:, kc, tb * 512:(tb + 1) * 512], in_=tt_[:], func=AF.Identity,
                    bias=self.modL[:, shift0 + kc:shift0 + kc + 1], scale=1.0),
                    reads=[r_t, self.r_modL], writes=[r_h[kc]])

    def emit_linear(self, st, hT, r_h, KC, T, wsl, widths, epi, nring=3, tag="w"):
        nc, P = self.nc, self.P
        ring = [(self.sb(st, "%s%d" % (tag, i), [128, KC * 128], BF16), Res()) for i in range(nring)]
        NTB = T // 512
        nset = 8 // NTB
        for n, width in enumerate(widths):
            wt, r_w = ring[n % nring]
            P.dma("pool", lambda e, wt=wt, n=n: e.dma_start(out=wt[:], in_=wsl[n]), writes=[r_w])
            base = (n % nset) * NTB
            for kc in range(KC):
                for tb in range(NTB):
                    pst, r_ps = self.ps[base + tb]
                    P.op("pe", lambda e, wt=wt, kc=kc, tb=tb, pst=pst, width=width: e.matmul(
                        pst[0:width, :], lhsT=wt[:, kc * 128:kc * 128 + width], rhs=hT[:, kc, tb * 512:(tb + 1) * 512],
                        start=(kc == 0), stop=(kc == KC - 1)),
                        reads=[r_w, r_h[kc]], writes=[r_ps])
            for tb in range(NTB):
                pst, r_ps = self.ps[base + tb]
                epi(n, tb, pst, r_ps, width)

    def emit_inproj(self, l, xT, r_xT, PT, r_PT):
        nc, P = self.nc, self.P
        if l == 0:
            self.w_in = self.din("w_in", [self.depth, len(W_IN_CHUNKS), 128, D])
        for tt in range(self.L // TT):
            t0 = tt * TT
            with contextlib.ExitStack() as st:
                hT = self.sb(st, "hT", [128, KCD, TT], BF16)
                r_h = [Res() for _ in range(KCD)]
                with contextlib.ExitStack() as st2:
                    self.emit_norm(st2, xT, r_xT, t0, TT, hT, r_h, 0, 0)
                    P.flush()
                stg = [(self.sb(st, "stg%d" % i, [128, 512], F32), Res()) for i in range(4)]
                k = [0]

                def epi(n, tb, pst, r_ps, width):
                    s, r_s = stg[k[0] % 4]
                    k[0] += 1
                    r0 = W_IN_CHUNKS[n][0]
                    P.op("act", lambda e: e.activation(out=s[0:width, :], in_=pst[0:width, :], func=AF.Copy), reads=[r_ps], writes=[r_s])
                    P.dma("sp", lambda e: e.dma_start(out=PT[r0:r0 + width, t0 + tb * 512:t0 + (tb + 1) * 512], in_=s[0:width, :]),
                          reads=[r_s])
                self.emit_linear(st, hT, r_h, KCD, TT, self.w_in[l], [w for _, w in W_IN_CHUNKS], epi)
                P.flush()

    def emit_pool(self, l, PT, YT):
        nc, P, L = self.nc, self.P, self.L
        if l == 0:
            self.pool_w = self.din("pool_w", [self.depth, 4, 384, 384])
            self.poolsc = self.din("poolsc", [self.depth, 128, 12])
            self.invcnt = self.din("invcnt", [4, 128, L])
        TS = min(L, 4096)
        HAL = 16
        with contextlib.ExitStack() as st:
            U = self.sb(st, "pU", [128, TS + 32], F32)
            S = [self.sb(st, "pS%d" % i, [128, TS + 32], F32) for i in range(2)]
            IC = self.sb(st, "pIC", [128, TS], F32)
            Pc = [self.sb(st, "pP%d" % i, [128, TS], BF16) for i in range(3)]
            pw = self.sb(st, "pw", [128, 3, 384], BF16)
            psc = self.sb(st, "psc", [128, 12], F32)
            stg = [(self.sb(st, "pstg%d" % i, [128, 512], F32), Res()) for i in range(3)]
            r_U, r_S, r_IC, r_pw, r_psc = Res(), [Res(), Res()], Res(), Res(), Res()
            r_Pc = [Res() for _ in range(3)]
            P.dma("sp", lambda e: e.dma_start(out=psc[:], in_=self.poolsc[l]), writes=[r_psc])
            k = 0
            for g, win in enumerate(POOL_WINDOWS):
                P.dma("pool", lambda e, g=g: e.dma_start(out=pw[:], in_=self.pool_w[l, g].rearrange("(c p) d -> p c d", p=128)), writes=[r_pw])
                for t0 in range(0, L, TS):
                    P.dma("sp", lambda e, g=g, t0=t0: e.dma_start(out=IC[:], in_=self.invcnt[g, :, t0:t0 + TS]), writes=[r_IC])
                    for c in range(3):
                        row0 = A0 + g * 384 + c * 128
                        lo, hi = max(0, t0 - HAL), min(L, t0 + TS + HAL)
                        P.op("pool", lambda e: e.memset(U[:], 0.0), writes=[r_U])
                        P.dma("sp", lambda e, row0=row0, lo=lo, hi=hi, t0=t0: e.dma_start(
                            out=U[:, lo - (t0 - HAL):hi - (t0 - HAL)], in_=PT[row0:row0 + 128, lo:hi]), writes=[r_U])
                        src, r_src = U, r_U
                        kk, wlen, bi = 1, TS + 32, 0
                        while kk < win:
                            wlen -= kk
                            dst, r_dst = S[bi], r_S[bi]
                            P.op("dve", lambda e, src=src, dst=dst, kk=kk, wlen=wlen: e.tensor_tensor(
                                out=dst[:, 0:wlen], in0=src[:, 0:wlen], in1=src[:, kk:kk + wlen], op=ALU.add),
                                reads=[r_src], writes=[r_dst])
                            src, r_src = dst, r_dst
                            kk *= 2
                            bi ^= 1
                        Mb, r_Mb = S[bi], r_S[bi]
                        o = HAL - win // 2
                        P.op("dve", lambda e, src=src, Mb=Mb, o=o: e.tensor_tensor(out=Mb[:, 0:TS], in0=src[:, o:o + TS], in1=IC[:], op=ALU.mult),
                             reads=[r_src, r_IC], writes=[r_Mb])
                        P.op("dve", lambda e, Mb=Mb, c=c: e.tensor_tensor(out=Pc[c][:], in0=Mb[:, 0:TS], in1=U[:, HAL:HAL + TS], op=ALU.subtract),
                             reads=[r_Mb, r_U], writes=[r_Pc[c]])
                    for d in range(3):
                        for tb in range(TS // 512):
                            pst, r_ps = self.ps[k % 8]
                            s, r_s = stg[k % 3]
                            k += 1
                            for c in range(3):
                                P.op("pe", lambda e, pst=pst, c=c, d=d, tb=tb: e.matmul(
                                    pst[:], lhsT=pw[:, c, d * 128:(d + 1) * 128], rhs=Pc[c][:, tb * 512:(tb + 1) * 512],
                                    start=(c == 0), stop=(c == 2)), reads=[r_pw, r_Pc[c]], writes=[r_ps])
                            col = g * 3 + d
                            P.op("act", lambda e, s=s, pst=pst, col=col: e.activation(out=s[:], in_=pst[:], func=AF.Copy, scale=psc[:, col:col + 1]),
                                 reads=[r_ps, r_psc], writes=[r_s])
                            r0 = g * 384 + d * 128
                            P.dma("sp", lambda e, s=s, r0=r0, c0=t0 + tb * 512: e.dma_start(out=YT[r0:r0 + 128, c0:c0 + 512], in_=s[:]), reads=[r_s])
            P.flush()


    def emit_attn(self, l, PT, YT):
        nc, P, L = self.nc, self.P, self.L
        if l == 0:
            self.qkn = self.din("qkn", [self.depth, 128, 2])
            self.bmask = self.din("bmask", [12, 128, 4, 256])
            self.OTd = self.nc.dram_tensor("OTd", [12, 128, L], F32).ap()
            self.LSEd = self.nc.dram_tensor("LSEd", [12, L], F32).ap()
        HALM = 1024
        NB = L // 128
        with contextlib.ExitStack() as st:
            QN = self.sb(st, "aQN", [128, L], BF16)
            KN = self.sb(st, "aKN", [128, L + 2 * HALM], BF16)
            VN = self.sb(st, "aVN", [128, L + 2 * HALM], BF16)
            VT = self.sb(st, "aVT", [128, NB + 16, 128], BF16)
            OT = self.sb(st, "aOT", [128, L], F32)
            LSEc = self.sb(st, "aLSE", [128, NB], F32)
            bm = self.sb(st, "abm", [128, 4, 256], F32)
            gq = self.sb(st, "agq", [128, 2], F32)
            r_QN, r_KN, r_VN, r_VT, r_OT, r_LSE, r_bm, r_gq = [Res() for _ in range(8)]
            ld = [(self.sb(st, "ald%d" % i, [128, 512], F32), Res()) for i in range(3)]
            sq = [(self.sb(st, "asq%d" % i, [128, 512], F32), Res()) for i in range(2)]
            rs = [(self.sb(st, "ars%d" % i, [128, 512], F32), Res()) for i in range(2)]
            S2 = [(self.sb(st, "aS2%d" % i, [128, 256], F32), Res()) for i in range(2)]
            Pm = [(self.sb(st, "aPm%d" % i, [128, 256], BF16), Res()) for i in range(2)]
            PmT = [(self.sb(st, "aPT%d" % i, [128, 256], BF16), Res()) for i in range(2)]
            ob = [(self.sb(st, "aob%d" % i, [128, 128], F32), Res()) for i in range(2)]
            sm = [(self.sb(st, "asm%d" % i, [128, 4], F32), Res()) for i in range(4)]
            P.dma("sp", lambda e: e.dma_start(out=gq[:], in_=self.qkn[l]), writes=[r_gq])
            P.op("dve", lambda e: e.tensor_scalar(out=gq[:, 0:1], in0=gq[:, 0:1], scalar1=float(128 ** -0.5), scalar2=None, op0=ALU.mult),
                 reads=[r_gq], writes=[r_gq])
            for buf, r_b in ((KN, r_KN), (VN, r_VN)):
                P.op("pool", lambda e, buf=buf: e.memset(buf[:, 0:HALM], 0.0), writes=[r_b])
                P.op("pool", lambda e, buf=buf: e.memset(buf[:, HALM + L:HALM + L + HALM], 0.0), writes=[r_b])
            cnt = 0
            for gi, dil in enumerate((1, 4, 16)):
                m = L // dil
                nblk = m // 128
                for hh in range(4):
                    gh = gi * 4 + hh
                    P.dma("sp", lambda e, gh=gh: e.dma_start(out=bm[:], in_=self.bmask[gh]), writes=[r_bm])
                    for which in range(3):
                        row0 = QKV0 + which * 1536 + gi * 512 + hh * 128
                        for tb in range(L // 512):
                            t, r_t = ld[cnt % 3]
                            cnt += 1
                            P.dma("sp", lambda e, t=t, row0=row0, tb=tb: e.dma_start(out=t[:], in_=PT[row0:row0 + 128, tb * 512:(tb + 1) * 512]), writes=[r_t])
                            if which == 2:
                                P.op("act", lambda e, t=t, tb=tb: e.activation(out=VN[:, HALM + tb * 512:HALM + (tb + 1) * 512], in_=t[:], func=AF.Copy),
                                     reads=[r_t], writes=[r_VN])
                                continue
                            sqt, r_sq = sq[cnt % 2]
                            rst, r_rs = rs[cnt % 2]
                            pst, r_ps = self.ps[cnt % 2]
                            P.op("act", lambda e, t=t, sqt=sqt: e.activation(out=sqt[:], in_=t[:], func=AF.Square), reads=[r_t], writes=[r_sq])
                            P.op("pe", lambda e, sqt=sqt, pst=pst: e.matmul(pst[:], lhsT=self.ones[:], rhs=sqt[:], start=True, stop=True),
                                 reads=[r_sq, self.r_const], writes=[r_ps])
                            P.op("act", lambda e, rst=rst, pst=pst: e.activation(out=rst[:], in_=pst[:], func=AF.Sqrt, bias=EPS, scale=1.0 / 128),
                                 reads=[r_ps], writes=[r_rs])
                            P.op("dve", lambda e, rst=rst: e.reciprocal(out=rst[:], in_=rst[:]), reads=[r_rs], writes=[r_rs])
                            dst = QN[:, tb * 512:(tb + 1) * 512] if which == 0 else KN[:, HALM + tb * 512:HALM + (tb + 1) * 512]
                            P.op("dve", lambda e, t=t, rst=rst, dst=dst, which=which: e.scalar_tensor_tensor(
                                out=dst, in0=t[:], scalar=gq[:, which:which + 1], in1=rst[:], op0=ALU.mult, op1=ALU.mult),
                                reads=[r_t, r_rs, r_gq], writes=[r_QN if which == 0 else r_KN])
                    ntile = dil * (nblk + 1)
                    tiles = [(r, n) for r in range(dil) for n in range(nblk + 1)]
                    for t0 in range(0, ntile, 4):
                        pst, r_ps = self.ps[2 + (t0 // 4) % 2]
                        grp = tiles[t0:t0 + 4]
                        for j, (r, n) in enumerate(grp):
                            s0 = HALM + r + dil * (128 * n - 64)
                            P.op("pe", lambda e, pst=pst, j=j, s0=s0, dil=dil: e.matmul(
                                pst[:, j * 128:(j + 1) * 128], lhsT=VN[:, s0:s0 + 127 * dil + 1:dil], rhs=self.identb[:], start=True, stop=True),
                                reads=[r_VN, self.r_const], writes=[r_ps])
                        ng = len(grp)
                        P.op("act", lambda e, pst=pst, t0=t0, ng=ng: e.activation(
                            out=VT[:, t0:t0 + ng, :], in_=pst[:, 0:ng * 128].rearrange("p (a b) -> p a b", b=128), func=AF.Copy),
                            reads=[r_ps], writes=[r_VT])
                    blks = []
                    for r in range(dil):
                        for n in range(nblk):
                            blks.append(dict(var=(1 if n == 0 else 0) + (2 if n == nblk - 1 else 0), q0=r + dil * 128 * n,
                                             k0=HALM + r + dil * (128 * n - 64), ti=r * (nblk + 1) + n))

                    def stA(bi, dil=dil, blks=blks):
                        b_ = blks[bi]
                        psS, r_psS = self.ps[bi % 2]
                        s2, r_s2 = S2[bi % 2]
                        pm, r_pm = Pm[bi % 2]
                        smt, r_sm = sm[bi % 4]
                        q0, k0, var = b_["q0"], b_["k0"], b_["var"]
                        P.op("pe", lambda e: e.matmul(psS[:, 0:256], lhsT=QN[:, q0:q0 + 127 * dil + 1:dil], rhs=KN[:, k0:k0 + 255 * dil + 1:dil],
                                                     start=True, stop=True), reads=[r_QN, r_KN], writes=[r_psS])
                        P.op("dve", lambda e: e.tensor_tensor(out=s2[:], in0=psS[:, 0:256], in1=bm[:, var, :], op=ALU.add), reads=[r_psS, r_bm], writes=[r_s2])
                        P.op("dve", lambda e: e.reduce_max(out=smt[:, 0:1], in_=s2[:], axis=AX.X), reads=[r_s2], writes=[r_sm])
                        P.op("dve", lambda e: e.tensor_scalar(out=smt[:, 1:2], in0=smt[:, 0:1], scalar1=-1.0, scalar2=None, op0=ALU.mult), reads=[r_sm], writes=[r_sm])
                        P.op("act", lambda e: e.activation(out=pm[:], in_=s2[:], func=AF.Exp, bias=smt[:, 1:2], scale=1.0, accum_out=smt[:, 2:3]),
                             reads=[r_s2, r_sm], writes=[r_pm, r_sm])

                    def stB1(bi, dil=dil, blks=blks):
                        pm, r_pm = Pm[bi % 2]
                        psT, r_psT = self.ps[2 + bi % 2]
                        pmt, r_pmt = PmT[bi % 2]
                        for kb in range(2):
                            P.op("pe", lambda e, kb=kb: e.matmul(psT[:, kb * 128:(kb + 1) * 128], lhsT=pm[:, kb * 128:(kb + 1) * 128], rhs=self.identb[:],
                                                                 start=True, stop=True), reads=[r_pm, self.r_const], writes=[r_psT])
                        P.op("dve", lambda e: e.tensor_copy(out=pmt[:], in_=psT[:, 0:256]), reads=[r_psT], writes=[r_pmt])

                    def stB2(bi, dil=dil, blks=blks):
                        b_ = blks[bi]
                        pmt, r_pmt = PmT[bi % 2]
                        psO, r_psO = self.ps[4 + bi % 2]
                        o_, r_o = ob[bi % 2]
                        smt, r_sm = sm[bi % 4]
                        ti = b_["ti"]
                        for kb in range(2):
                            P.op("pe", lambda e, kb=kb: e.matmul(psO[:, 0:128], lhsT=pmt[:, kb * 128:(kb + 1) * 128], rhs=VT[:, ti + kb, :],
                                                                 start=(kb == 0), stop=(kb == 1)), reads=[r_pmt, r_VT], writes=[r_psO])
                        P.op("dve", lambda e: e.reciprocal(out=smt[:, 3:4], in_=smt[:, 2:3]), reads=[r_sm], writes=[r_sm])
                        P.op("act", lambda e: e.activation(out=o_[:], in_=psO[:, 0:128], func=AF.Copy, scale=smt[:, 3:4]), reads=[r_psO, r_sm], writes=[r_o])
                        P.op("act", lambda e: e.activation(out=LSEc[:, bi:bi + 1], in_=smt[:, 2:3], func=AF.Ln), reads=[r_sm], writes=[r_LSE])
                        P.op("dve", lambda e: e.tensor_tensor(out=LSEc[:, bi:bi + 1], in0=LSEc[:, bi:bi + 1], in1=smt[:, 0:1], op=ALU.add),
                             reads=[r_sm, r_LSE], writes=[r_LSE])

                    def stC(bi, dil=dil, blks=blks):
                        b_ = blks[bi]
                        o_, r_o = ob[bi % 2]
                        psX, r_psX = self.ps[6 + bi % 2]
                        q0 = b_["q0"]
                        P.op("pe", lambda e: e.matmul(psX[:, 0:128], lhsT=o_[:], rhs=self.ident[:], start=True, stop=True),
                             reads=[r_o, self.r_const], writes=[r_psX])
                        P.op("act", lambda e: e.activation(out=OT[:, q0:q0 + 127 * dil + 1:dil], in_=psX[:, 0:128], func=AF.Copy), reads=[r_psX], writes=[r_OT])

                    NBk = len(blks)
                    for it in range(NBk + 3):
                        if it < NBk:
                            stA(it)
                        if 0 <= it - 1 < NBk:
                            stB1(it - 1)
                        if 0 <= it - 2 < NBk:
                            stB2(it - 2)
                        if 0 <= it - 3 < NBk:
                            stC(it - 3)
                    P.dma("sp", lambda e, gh=gh: e.dma_start(out=self.OTd[gh], in_=OT[:]), reads=[r_OT])
                    lse_v = self.LSEd[gh].rearrange("(n i r) -> i r n", i=128, r=dil)
                    for r in range(dil):
                        for n0 in range(0, nblk, 16):
                            n1 = min(nblk, n0 + 16)
                            P.dma("sp", lambda e, lse_v=lse_v, r=r, n0=n0, n1=n1, nblk=nblk: e.dma_start(
                                out=lse_v[:, r, n0:n1], in_=LSEc[:, r * nblk + n0:r * nblk + n1], allow_slow_non_contiguous=True), reads=[r_LSE])
            P.flush()
        with contextlib.ExitStack() as st:
            o3 = [[(self.sb(st, "mo%d%d" % (i, j), [128, 512], F32), Res()) for j in range(3)] for i in range(2)]
            l3 = [[(self.sb(st, "ml%d%d" % (i, j), [128, 512], F32), Res()) for j in range(3)] for i in range(2)]
            mx = [(self.sb(st, "mm%d" % i, [128, 512], F32), Res()) for i in range(2)]
            den = [(self.sb(st, "md%d" % i, [128, 512], F32), Res()) for i in range(2)]
            acc = [(self.sb(st, "ma%d" % i, [128, 512], F32), Res()) for i in range(2)]
            k = 0
            for hh in range(4):
                for tb in range(L // 512):
                    i = k % 2
                    k += 1
                    c0 = tb * 512
                    for gi in range(3):
                        gh = gi * 4 + hh
                        P.dma("sp", lambda e, i=i, gi=gi, gh=gh, c0=c0: e.dma_start(out=o3[i][gi][0][:], in_=self.OTd[gh, :, c0:c0 + 512]), writes=[o3[i][gi][1]])
                        P.dma("act", lambda e, i=i, gi=gi, gh=gh, c0=c0: e.dma_start(
                            out=l3[i][gi][0][:], in_=self.LSEd[gh:gh + 1, c0:c0 + 512].broadcast_to([128, 512])), writes=[l3[i][gi][1]])
                    (m_, r_m), (d_, r_d), (a_, r_a) = mx[i], den[i], acc[i]
                    lt = [l3[i][g][0] for g in range(3)]
                    r_l = [l3[i][g][1] for g in range(3)]
                    ot = [o3[i][g][0] for g in range(3)]
                    r_o3 = [o3[i][g][1] for g in range(3)]
                    P.op("dve", lambda e, m_=m_, lt=lt: e.tensor_tensor(out=m_[:], in0=lt[0][:], in1=lt[1][:], op=ALU.max), reads=[r_l[0], r_l[1]], writes=[r_m])
                    P.op("dve", lambda e, m_=m_, lt=lt: e.tensor_tensor(out=m_[:], in0=m_[:], in1=lt[2][:], op=ALU.max), reads=[r_l[2], r_m], writes=[r_m])
                    for g in range(3):
                        P.op("dve", lambda e, m_=m_, lt=lt, g=g: e.tensor_tensor(out=lt[g][:], in0=lt[g][:], in1=m_[:], op=ALU.subtract), reads=[r_m, r_l[g]], writes=[r_l[g]])
                        P.op("act", lambda e, lt=lt, g=g: e.activation(out=lt[g][:], in_=lt[g][:], func=AF.Exp), reads=[r_l[g]], writes=[r_l[g]])
                        P.op("dve", lambda e, lt=lt, ot=ot, g=g: e.tensor_tensor(out=ot[g][:], in0=ot[g][:], in1=lt[g][:], op=ALU.mult), reads=[r_l[g], r_o3[g]], writes=[r_o3[g]])
                    P.op("dve", lambda e, d_=d_, lt=lt: e.tensor_tensor(out=d_[:], in0=lt[0][:], in1=lt[1][:], op=ALU.add), reads=[r_l[0], r_l[1]], writes=[r_d])
                    P.op("dve", lambda e, d_=d_, lt=lt: e.tensor_tensor(out=d_[:], in0=d_[:], in1=lt[2][:], op=ALU.add), reads=[r_l[2], r_d], writes=[r_d])
                    P.op("dve", lambda e, a_=a_, ot=ot: e.tensor_tensor(out=a_[:], in0=ot[0][:], in1=ot[1][:], op=ALU.add), reads=[r_o3[0], r_o3[1]], writes=[r_a])
                    P.op("dve", lambda e, a_=a_, ot=ot: e.tensor_tensor(out=a_[:], in0=a_[:], in1=ot[2][:], op=ALU.add), reads=[r_o3[2], r_a], writes=[r_a])
                    P.op("dve", lambda e, d_=d_: e.reciprocal(out=d_[:], in_=d_[:]), reads=[r_d], writes=[r_d])
                    P.op("dve", lambda e, a_=a_, d_=d_: e.tensor_tensor(out=a_[:], in0=a_[:], in1=d_[:], op=ALU.mult), reads=[r_d, r_a], writes=[r_a])
                    r0 = 3584 + hh * 128
                    P.dma("sp", lambda e, a_=a_, r0=r0, c0=c0: e.dma_start(out=YT[r0:r0 + 128, c0:c0 + 512], in_=a_[:]), reads=[r_a])
            P.flush()


    def emit_ssd(self, l, PT, YT):
        nc, P, L = self.nc, self.P, self.L
        NC = L // 128
        if l == 0:
            self.ssdv = self.din("ssdv", [self.depth, 4, 16, 2])
            self.convw = self.din("convw", [self.depth, 128, 24, 6])
            self.dskip = self.din("dskip", [self.depth, 4, 128, 8])
            self.ssdn = self.din("ssdn", [self.depth, 128, 16])
            self.ssdc_in = self.din("ssdc", [128, 4, 128])
            self.XCd = self.nc.dram_tensor("XCd", [6, 128, L], F32).ap()
            self.PRVd = self.nc.dram_tensor("PRVd", [NC, 128, 512], BF16).ap()

        def bc3(ap2, n):
            return ap2.rearrange("p (h o) -> p h o", o=1).broadcast_to([128, ap2.shape[1], n])

        for g in range(4):
            with contextlib.ExitStack() as st:
                TS = min(L, 4096)
                U = [(self.sb(st, "sU%d" % i, [128, TS + 4], F32), Res()) for i in range(2)]
                AC = [(self.sb(st, "sA%d" % i, [128, TS], F32), Res()) for i in range(2)]
                cw = self.sb(st, "scw", [128, 24, 6], F32)
                r_cw = Res()
                P.dma("sp", lambda e: e.dma_start(out=cw[:], in_=self.convw[l]), writes=[r_cw])
                k = 0
                for ci, (row0, wc) in enumerate([(XBC0 + g * 512 + j * 128, 4 * g + j) for j in range(4)]
                                                + [(XBC0 + 2048 + g * 128, 16 + g), (XBC0 + 2560 + g * 128, 20 + g)]):
                    for t0 in range(0, L, TS):
                        (u, r_u), (a, r_a) = U[k % 2], AC[k % 2]
                        k += 1
                        lo, hi = max(0, t0 - 2), min(L, t0 + TS + 2)
                        P.op("pool", lambda e, u=u: e.memset(u[:], 0.0), writes=[r_u])
                        P.dma("sp", lambda e, u=u, row0=row0, lo=lo, hi=hi, t0=t0: e.dma_start(
                            out=u[:, lo - (t0 - 2):hi - (t0 - 2)], in_=PT[row0:row0 + 128, lo:hi]), writes=[r_u])
                        P.op("dve", lambda e, u=u, a=a, wc=wc: e.tensor_scalar(out=a[:], in0=u[:, 0:TS], scalar1=cw[:, wc, 0:1], scalar2=cw[:, wc, 5:6],
                                                                             op0=ALU.mult, op1=ALU.add), reads=[r_u, r_cw], writes=[r_a])
                        for tap in range(1, 5):
                            P.op("dve", lambda e, u=u, a=a, wc=wc, tap=tap: e.scalar_tensor_tensor(
                                out=a[:], in0=u[:, tap:tap + TS], scalar=cw[:, wc, tap:tap + 1], in1=a[:], op0=ALU.mult, op1=ALU.add),
                                reads=[r_u, r_cw, r_a], writes=[r_a])
                        P.op("act", lambda e, a=a: e.activation(out=a[:], in_=a[:], func=AF.Silu), reads=[r_a], writes=[r_a])
                        P.dma("sp", lambda e, a=a, ci=ci, t0=t0: e.dma_start(out=self.XCd[ci, :, t0:t0 + TS], in_=a[:]), reads=[r_a])
                P.flush()
            with contextlib.ExitStack() as stg_:
                DT = self.sb(stg_, "sDT", [128, NC, 16], F32)
                ACS = self.sb(stg_, "sACS", [128, NC, 16], F32)
                EIN = self.sb(stg_, "sEIN", [128, NC, 16], F32)
                EOW = self.sb(stg_, "sEOW", [128, NC, 16], F32)
                DEC = self.sb(stg_, "sDEC", [128, NC, 16], F32)
                SC = self.sb(stg_, "sSC", [128, 4, 128], F32)
                NMB = self.sb(stg_, "sNMB", [128, 2, 4, 128], BF16)
                DSK = self.sb(stg_, "sDSK", [128, 8], F32)
                NG = self.sb(stg_, "sNG", [128, 16], F32)
                r_tab = Res()
                r_sc = Res()
                with contextlib.ExitStack() as st:
                    dtT = self.sb(st, "sdtT", [16, L], F32)
                    dtE = self.sb(st, "sdtE", [16, L], F32)
                    dtA = self.sb(st, "sdtA", [16, L], F32)
                    DTA = self.sb(st, "sDTA", [128, NC, 16], F32)
                    TOT = self.sb(st, "sTOT", [128, NC, 16], F32)
                    sv = self.sb(st, "ssv", [16, 4], F32)
                    r_dtT, r_dtE, r_dtA, r_DTA, r_TOT, r_sv = [Res() for _ in range(6)]
                    P.dma("sp", lambda e: e.dma_start(out=SC[:], in_=self.ssdc_in), writes=[r_sc])
                    P.dma("sp", lambda e: e.dma_start(out=DSK[:], in_=self.dskip[l, g]), writes=[r_sc])
                    P.dma("sp", lambda e: e.dma_start(out=NG[:], in_=self.ssdn[l]), writes=[r_sc])
                    for d in range(2):
                        for q in range(4):
                            P.op("dve", lambda e, d=d, q=q: e.tensor_copy(out=NMB[:, d, q, :], in_=SC[:, 2 + d, :]), reads=[r_sc], writes=[r_sc])
                    P.dma("sp", lambda e: e.dma_start(out=sv[:, 0:2], in_=self.ssdv[l, g]), writes=[r_sv])
                    for d in range(2):
                        r0 = DT0 + d * 32 + g * 8
                        P.dma("sp", lambda e, d=d, r0=r0: e.dma_start(out=dtT[d * 8:(d + 1) * 8, :], in_=PT[r0:r0 + 8, :]), writes=[r_dtT])
                    P.op("act", lambda e: e.activation(out=sv[:, 2:3], in_=sv[:, 1:2], func=AF.Exp), reads=[r_sv], writes=[r_sv])
                    P.op("dve", lambda e: e.tensor_scalar(out=sv[:, 3:4], in0=sv[:, 2:3], scalar1=-1.0, scalar2=None, op0=ALU.mult), reads=[r_sv], writes=[r_sv])
                    P.op("act", lambda e: e.activation(out=dtE[:], in_=dtT[:], func=AF.Exp, bias=sv[:, 0:1], scale=1.0), reads=[r_dtT, r_sv], writes=[r_dtE])
                    P.op("act", lambda e: e.activation(out=dtE[:], in_=dtE[:], func=AF.Ln, bias=1.0, scale=1.0), reads=[r_dtE], writes=[r_dtE])
                    P.op("dve", lambda e: e.tensor_scalar(out=dtA[:], in0=dtE[:], scalar1=sv[:, 3:4], scalar2=None, op0=ALU.mult), reads=[r_dtE, r_sv], writes=[r_dtA])
                    for src, r_src, dst in ((dtE, r_dtE, DT), (dtA, r_dtA, DTA)):
                        for c0 in range(0, NC, 32):
                            nn = min(32, NC - c0)
                            pst, r_ps = self.ps[(c0 // 32) % 2]
                            for c in range(nn):
                                P.op("pe", lambda e, pst=pst, src=src, c=c, c0=c0: e.matmul(
                                    pst[:, c * 16:(c + 1) * 16], lhsT=src[0:16, (c0 + c) * 128:(c0 + c + 1) * 128], rhs=self.ident[0:16, 0:16],
                                    start=True, stop=True), reads=[r_src, self.r_const], writes=[r_ps])
                            P.op("act", lambda e, pst=pst, dst=dst, c0=c0, nn=nn: e.activation(
                                out=dst[:, c0:c0 + nn, :], in_=pst[:, 0:nn * 16].rearrange("p (c h) -> p c h", h=16), func=AF.Copy),
                                reads=[r_ps], writes=[r_tab if dst is DT else r_DTA])
                    for dst, r_dst, lh in ((ACS, r_tab, None), (TOT, r_TOT, self.ones)):
                        for d in range(2):
                            for c0 in range(0, NC, 64):
                                nn = min(64, NC - c0)
                                pst, r_ps = self.ps[2 + d]
                                lhs = lh[:] if lh is not None else SC[:, d, :]
                                P.op("pe", lambda e, pst=pst, lhs=lhs, c0=c0, nn=nn, d=d: e.matmul(
                                    pst[:, 0:nn * 8], lhsT=lhs, rhs=DTA[:, c0:c0 + nn, d * 8:(d + 1) * 8], start=True, stop=True),
                                    reads=[r_DTA, r_sc, self.r_const], writes=[r_ps])
                                P.op("act", lambda e, pst=pst, dst=dst, c0=c0, nn=nn, d=d: e.activation(
                                    out=dst[:, c0:c0 + nn, d * 8:(d + 1) * 8], in_=pst[:, 0:nn * 8].rearrange("p (c h) -> p c h", h=8), func=AF.Copy),
                                    reads=[r_ps], writes=[r_dst])
                    P.op("act", lambda e: e.activation(out=EIN[:], in_=ACS[:], func=AF.Exp), reads=[r_tab], writes=[r_tab])
                    P.op("act", lambda e: e.activation(out=DEC[:], in_=TOT[:], func=AF.Exp), reads=[r_TOT], writes=[r_tab])
                    P.op("dve", lambda e: e.tensor_tensor(out=EOW[:], in0=TOT[:], in1=ACS[:], op=ALU.subtract), reads=[r_TOT, r_tab], writes=[r_tab])
                    P.op("act", lambda e: e.activation(out=EOW[:], in_=EOW[:], func=AF.Exp), reads=[r_tab], writes=[r_tab])
                    P.op("dve", lambda e: e.tensor_tensor(out=EOW[:], in0=EOW[:], in1=DT[:], op=ALU.mult), reads=[r_tab], writes=[r_tab])
                    P.flush()

                SB_ = 512

                def load_x(st_, tiles, t0, names):
                    for nm, ci in names:
                        t, r_t = tiles[nm]
                        P.dma("sp", lambda e, t=t, ci=ci, t0=t0: e.dma_start(out=t[:], in_=self.XCd[ci, :, t0:t0 + SB_]), writes=[r_t])

                def xtm(tiles, cc, Xs, r_Xs, Bt, r_Bt):
                    ps0, r_ps0 = self.ps[0]
                    ps1, r_ps1 = self.ps[1]
                    for j in range(4):
                        t, r_t = tiles["x%d" % j]
                        P.op("pe", lambda e, t=t, j=j: e.matmul(ps0[:, j * 128:(j + 1) * 128], lhsT=t[:, cc * 128:(cc + 1) * 128], rhs=self.ident[:],
                                                               start=True, stop=True), reads=[r_t, self.r_const], writes=[r_ps0])
                    P.op("act", lambda e: e.activation(out=Xs[:], in_=ps0[:], func=AF.Copy), reads=[r_ps0], writes=[r_Xs])
                    t, r_t = tiles["B"]
                    P.op("pe", lambda e, t=t: e.matmul(ps1[:, 0:128], lhsT=t[:, cc * 128:(cc + 1) * 128], rhs=self.ident[:], start=True, stop=True),
                         reads=[r_t, self.r_const], writes=[r_ps1])
                    P.op("act", lambda e: e.activation(out=Bt[:], in_=ps1[:, 0:128], func=AF.Copy), reads=[r_ps1], writes=[r_Bt])

                with contextlib.ExitStack() as st:
                    tl = [{nm: (self.sb(st, "s1%s%d" % (nm, i), [128, SB_], F32), Res()) for nm in ("x0", "x1", "x2", "x3", "B")} for i in range(2)]
                    Xs2 = [(self.sb(st, "s1X%d" % i, [128, 512], F32), Res()) for i in range(2)]
                    Bt2 = [(self.sb(st, "s1Bt%d" % i, [128, 128], BF16), Res()) for i in range(2)]
                    xw2 = [(self.sb(st, "s1w%d" % i, [128, 512], BF16), Res()) for i in range(2)]
                    Sb = self.sb(st, "s1S", [128, 512], F32)
                    pv = [(self.sb(st, "s1pv%d" % i, [128, 512], BF16), Res()) for i in range(2)]
                    tmp = self.sb(st, "s1T", [128, 512], F32)
                    r_Sb, r_tmp = Res(), Res()
                    P.op("dve", lambda e: e.memset(Sb[:], 0.0), writes=[r_Sb])
                    k = 0
                    for sbi in reversed(range(L // SB_)):
                        tiles = tl[sbi % 2]
                        load_x(st, tiles, sbi * SB_, [("x0", 0), ("x1", 1), ("x2", 2), ("x3", 3), ("B", 4)])
                        for cc in reversed(range(4)):
                            c = sbi * 4 + cc
                            (Xs, r_Xs), (Bt, r_Bt), (xw, r_xw) = Xs2[k % 2], Bt2[k % 2], xw2[k % 2]
                            k += 1
                            xtm(tiles, cc, Xs, r_Xs, Bt, r_Bt)
                            P.op("dve", lambda e, Xs=Xs, xw=xw, c=c: e.tensor_tensor(
                                out=xw[:].rearrange("p (h q) -> p h q", q=64), in0=Xs[:].rearrange("p (h q) -> p h q", q=64),
                                in1=bc3(EOW[:, c, 8:16], 64), op=ALU.mult), reads=[r_Xs, r_tab], writes=[r_xw])
                            ps2, r_ps2 = self.ps[2 + k % 2]
                            P.op("pe", lambda e, ps2=ps2, Bt=Bt, xw=xw: e.matmul(ps2[:], lhsT=Bt[:], rhs=xw[:], start=True, stop=True),
                                 reads=[r_Bt, r_xw], writes=[r_ps2])
                            pvt, r_pv = pv[k % 2]
                            P.op("act", lambda e, pvt=pvt: e.activation(out=pvt[:], in_=Sb[:], func=AF.Copy), reads=[r_Sb], writes=[r_pv])
                            P.dma("sp", lambda e, pvt=pvt, c=c: e.dma_start(out=self.PRVd[c], in_=pvt[:]), reads=[r_pv])
                            P.op("dve", lambda e, c=c: e.tensor_tensor(
                                out=tmp[:].rearrange("p (h q) -> p h q", q=64), in0=Sb[:].rearrange("p (h q) -> p h q", q=64),
                                in1=bc3(DEC[:, c, 8:16], 64), op=ALU.mult), reads=[r_Sb, r_tab], writes=[r_tmp])
                            P.op("dve", lambda e, ps2=ps2: e.tensor_tensor(out=Sb[:], in0=tmp[:], in1=ps2[:], op=ALU.add),
                                 reads=[r_tmp, r_ps2], writes=[r_Sb])
                    P.flush()

                with contextlib.ExitStack() as st:
                    names = ("x0", "x1", "x2", "x3", "B", "C", "z0", "z1", "z2", "z3")
                    tl = [{nm: (self.sb(st, "s2%s%d" % (nm, i), [128, SB_], F32), Res()) for nm in names} for i in range(2)]
                    Bb = [(self.sb(st, "s2Bb%d" % i, [128, SB_], BF16), Res()) for i in range(2)]
                    Cb = [(self.sb(st, "s2Cb%d" % i, [128, SB_], BF16), Res()) for i in range(2)]
                    Xs2 = [(self.sb(st, "s2X%d" % i, [128, 512], F32), Res()) for i in range(2)]
                    Bt2 = [(self.sb(st, "s2Bt%d" % i, [128, 128], BF16), Res()) for i in range(2)]
                    xd2 = [[(self.sb(st, "s2d%d%d" % (i, d), [128, 512], BF16), Res()) for d in range(2)] for i in range(2)]
                    xw2 = [(self.sb(st, "s2w%d" % i, [128, 512], BF16), Res()) for i in range(2)]
                    CBT = [(self.sb(st, "s2CB%d" % i, [128, 128], F32), Res()) for i in range(2)]
                    RD = [(self.sb(st, "s2RD%d" % i, [128, 16, 128], F32), Res()) for i in range(1)]
                    ARG = [(self.sb(st, "s2AR%d" % i, [128, 512], F32), Res()) for i in range(2)]
                    EX = [(self.sb(st, "s2EX%d" % i, [128, 512], F32), Res()) for i in range(2)]
                    MT = [(self.sb(st, "s2MT%d" % i, [128, 16, 128], BF16), Res()) for i in range(2)]
                    T1 = self.sb(st, "s2T1", [128, 512], F32)
                    T2 = self.sb(st, "s2T2", [128, 512], F32)
                    T3 = self.sb(st, "s2T3", [128, 512], F32)
                    YTM = [(self.sb(st, "s2Y%d" % i, [128, 512], F32), Res()) for i in range(2)]
                    YF = [(self.sb(st, "s2YF%d" % i, [128, 4, 512], F32), Res()) for i in range(2)]
                    Sf = self.sb(st, "s2S", [128, 512], F32)
                    pvl = [(self.sb(st, "s2pv%d" % i, [128, 512], BF16), Res()) for i in range(2)]
                    Sfb = self.sb(st, "s2Sb", [128, 512], BF16)
                    tmp = self.sb(st, "s2T", [128, 512], F32)
                    GZ = [(self.sb(st, "s2GZ%d" % i, [128, 512], F32), Res()) for i in range(2)]
                    YG = [(self.sb(st, "s2YG%d" % i, [128, 512], F32), Res()) for i in range(4)]
                    SQ = [(self.sb(st, "s2SQ%d" % i, [128, 512], F32), Res()) for i in range(2)]
                    RS = self.sb(st, "s2RS", [128, 512], F32)
                    OUTS = [(self.sb(st, "s2O%d" % i, [128, 512], F32), Res()) for i in range(2)]
                    r_T1, r_T2, r_T3, r_Sf, r_Sfb, r_tmp, r_RS = [Res() for _ in range(7)]
                    P.op("dve", lambda e: e.memset(Sf[:], 0.0), writes=[r_Sf])
                    P.op("dve", lambda e: e.memset(Sfb[:], 0.0), writes=[r_Sfb])
                    v3 = lambda ap: ap.rearrange("p (h q) -> p h q", q=64)
                    sbc = {}

                    def sb_load(sbi, g=g):
                        tiles = tl[sbi % 2]
                        load_x(st, tiles, sbi * SB_, [("x0", 0), ("x1", 1), ("x2", 2), ("x3", 3), ("B", 4), ("C", 5)])
                        for j in range(4):
                            t, r_t = tiles["z%d" % j]
                            r0 = Z0 + g * 512 + j * 128
                            P.dma("sp", lambda e, t=t, r0=r0: e.dma_start(out=t[:], in_=PT[r0:r0 + 128, sbi * SB_:(sbi + 1) * SB_]), writes=[r_t])
                        (bb, r_bb), (cb, r_cb) = Bb[sbi % 2], Cb[sbi % 2]
                        P.op("act", lambda e: e.activation(out=bb[:], in_=tiles["B"][0][:], func=AF.Copy), reads=[tiles["B"][1]], writes=[r_bb])
                        P.op("act", lambda e: e.activation(out=cb[:], in_=tiles["C"][0][:], func=AF.Copy), reads=[tiles["C"][1]], writes=[r_cb])
                        sbc[sbi] = dict(tiles=tiles, bb=bb, r_bb=r_bb, cb=cb, r_cb=r_cb, yf=YF[sbi % 2])

                    def front(c):
                        sbi, cc = c // 4, c % 4
                        if cc == 0:
                            sb_load(sbi)
                        S_ = sbc[sbi]
                        tiles, bb, r_bb, cb, r_cb = S_["tiles"], S_["bb"], S_["r_bb"], S_["cb"], S_["r_cb"]
                        i2 = c % 2
                        (Xs, r_Xs), (Bt, r_Bt), (xw, r_xw) = Xs2[i2], Bt2[i2], xw2[i2]
                        (xdf, r_xdf), (xdb, r_xdb) = xd2[i2]
                        (cbt, r_cbt), (rd, r_rd), (mt, r_mt) = CBT[i2], RD[0], MT[i2]
                        pvt, r_pv = pvl[i2]
                        P.dma("sp", lambda e: e.dma_start(out=pvt[:], in_=self.PRVd[c]), writes=[r_pv])
                        xtm(tiles, cc, Xs, r_Xs, Bt, r_Bt)
                        X3 = v3(Xs[:])
                        P.op("dve", lambda e: e.tensor_tensor(
                            out=rd[:], in0=self.ident[:].rearrange("p (o t) -> p o t", o=1).broadcast_to([128, 16, 128]),
                            in1=bc3(ACS[:, c, :], 128), op=ALU.mult), reads=[self.r_const, r_tab], writes=[r_rd])
                        ps1, r_ps1 = self.ps[1]
                        P.op("pe", lambda e: e.matmul(ps1[:, 128:256], lhsT=bb[:, cc * 128:(cc + 1) * 128], rhs=cb[:, cc * 128:(cc + 1) * 128],
                                                     start=True, stop=True), reads=[r_bb, r_cb], writes=[r_ps1])
                        P.op("act", lambda e: e.activation(out=cbt[:], in_=ps1[:, 128:256], func=AF.Copy), reads=[r_ps1], writes=[r_cbt])
                        for dst, r_dst, tabap in ((xdf, r_xdf, DT[:, c, 0:8]), (xdb, r_xdb, DT[:, c, 8:16]), (xw, r_xw, EOW[:, c, 0:8])):
                            P.op("pool", lambda e, dst=dst, tabap=tabap: e.tensor_tensor(out=v3(dst[:]), in0=X3, in1=bc3(tabap, 64), op=ALU.mult),
                                 reads=[r_Xs, r_tab], writes=[r_dst])
                        for q in range(4):
                            d = q // 2
                            psq, r_psq = self.ps[2 + q % 2]
                            (arg, r_arg), (ex, r_ex) = ARG[q % 2], EX[q % 2]
                            P.op("pe", lambda e, psq=psq, q=q: e.matmul(psq[:], lhsT=self.ones[:], rhs=rd[:, 4 * q:4 * q + 4, :], start=True, stop=False),
                                 reads=[r_rd, self.r_const], writes=[r_psq])
                            P.op("pe", lambda e, psq=psq, d=d: e.matmul(psq[:], lhsT=self.identb[:], rhs=NMB[:, d, :, :], start=False, stop=True),
                                 reads=[r_sc, self.r_const], writes=[r_psq])
                            P.op("dve", lambda e, psq=psq, arg=arg, q=q: e.tensor_tensor(
                                out=arg[:].rearrange("p (h t) -> p h t", t=128), in0=psq[:].rearrange("p (h t) -> p h t", t=128),
                                in1=bc3(ACS[:, c, 4 * q:4 * q + 4], 128), op=ALU.subtract), reads=[r_psq, r_tab], writes=[r_arg])
                            P.op("act", lambda e, arg=arg, ex=ex: e.activation(out=ex[:], in_=arg[:], func=AF.Exp), reads=[r_arg], writes=[r_ex])
                            P.op("dve", lambda e, ex=ex, q=q: e.tensor_tensor(
                                out=mt[:, 4 * q:4 * q + 4, :], in0=ex[:].rearrange("p (h t) -> p h t", t=128),
                                in1=cbt[:].rearrange("p (o t) -> p o t", o=1).broadcast_to([128, 4, 128]), op=ALU.mult),
                                reads=[r_ex, r_cbt], writes=[r_mt])

                    def back(c):
                        sbi, cc = c // 4, c % 4
                        S_ = sbc[sbi]
                        cb, r_cb = S_["cb"], S_["r_cb"]
                        i2 = c % 2
                        (Xs, r_Xs), (Bt, r_Bt), (xw, r_xw) = Xs2[i2], Bt2[i2], xw2[i2]
                        (xdf, r_xdf), (xdb, r_xdb) = xd2[i2]
                        (mt, r_mt), (ytm, r_ytm) = MT[i2], YTM[i2]
                        pvt, r_pv = pvl[i2]
                        X3 = v3(Xs[:])
                        psY = [self.ps[4], self.ps[5]]
                        for hd in range(16):
                            d, h = hd // 8, hd % 8
                            xd, r_xd = (xdf, r_xdf) if d == 0 else (xdb, r_xdb)
                            P.op("pe", lambda e, d=d, h=h, hd=hd, xd=xd: e.matmul(
                                psY[d][0][:, h * 64:(h + 1) * 64], lhsT=mt[:, hd, :], rhs=xd[:, h * 64:(h + 1) * 64], start=True, stop=True),
                                reads=[r_mt, r_xd], writes=[psY[d][1]])
                        psF, r_psF = self.ps[6]
                        psB, r_psB = self.ps[7]
                        P.op("pe", lambda e: e.matmul(psF[:], lhsT=cb[:, cc * 128:(cc + 1) * 128], rhs=Sfb[:], start=True, stop=True),
                             reads=[r_cb, r_Sfb], writes=[r_psF])
                        P.op("pe", lambda e: e.matmul(psB[:], lhsT=cb[:, cc * 128:(cc + 1) * 128], rhs=pvt[:], start=True, stop=True),
                             reads=[r_cb, r_pv], writes=[r_psB])
                        P.op("dve", lambda e: e.tensor_tensor(out=v3(T1[:]), in0=v3(psF[:]), in1=bc3(EIN[:, c, 0:8], 64), op=ALU.mult),
                             reads=[r_psF, r_tab], writes=[r_T1])
                        P.op("pe", lambda e: e.matmul(psF[:], lhsT=Bt[:], rhs=xw[:], start=True, stop=True), reads=[r_Bt, r_xw], writes=[r_psF])
                        P.op("dve", lambda e: e.tensor_tensor(out=v3(tmp[:]), in0=v3(Sf[:]), in1=bc3(DEC[:, c, 0:8], 64), op=ALU.mult),
                             reads=[r_Sf, r_tab], writes=[r_tmp])
                        P.op("dve", lambda e: e.tensor_tensor(out=Sf[:], in0=tmp[:], in1=psF[:], op=ALU.add), reads=[r_tmp, r_psF], writes=[r_Sf])
                        P.op("act", lambda e: e.activation(out=Sfb[:], in_=Sf[:], func=AF.Copy), reads=[r_Sf], writes=[r_Sfb])
                        P.op("dve", lambda e: e.tensor_tensor(out=T1[:], in0=T1[:], in1=psY[0][0][:], op=ALU.add), reads=[psY[0][1], r_T1], writes=[r_T1])
                        P.op("dve", lambda e: e.tensor_tensor(out=v3(T2[:]), in0=v3(psB[:]), in1=bc3(EIN[:, c, 8:16], 64), op=ALU.mult),
                             reads=[r_psB, r_tab], writes=[r_T2])
                        P.op("dve", lambda e: e.tensor_tensor(out=T2[:], in0=T2[:], in1=psY[1][0][:], op=ALU.add), reads=[psY[1][1], r_T2], writes=[r_T2])
                        P.op("pool", lambda e: e.tensor_tensor(out=v3(T3[:]), in0=X3, in1=bc3(DSK[:, 0:8], 64), op=ALU.mult),
                             reads=[r_Xs, r_sc], writes=[r_T3])
                        P.op("pool", lambda e: e.tensor_tensor(out=T3[:], in0=T3[:], in1=T1[:], op=ALU.add), reads=[r_T1, r_T3], writes=[r_T3])
                        P.op("pool", lambda e: e.tensor_tensor(out=ytm[:], in0=T3[:], in1=T2[:], op=ALU.add), reads=[r_T2, r_T3], writes=[r_ytm])

                    def back2(c, g=g):
                        sbi, cc = c // 4, c % 4
                        S_ = sbc[sbi]
                        yf, r_yf = S_["yf"]
                        ytm, r_ytm = YTM[c % 2]
                        ps7, r_ps7 = self.ps[7]
                        for j in range(4):
                            P.op("pe", lambda e, j=j: e.matmul(ps7[:, j * 128:(j + 1) * 128], lhsT=ytm[:, j * 128:(j + 1) * 128], rhs=self.ident[:],
                                                               start=True, stop=True), reads=[r_ytm, self.r_const], writes=[r_ps7])
                        P.op("act", lambda e: e.activation(out=yf[:, :, cc * 128:(cc + 1) * 128], in_=ps7[:].rearrange("p (j t) -> p j t", t=128), func=AF.Copy),
                             reads=[r_ps7], writes=[r_yf])
                        if cc != 3:
                            return
                        tiles = S_["tiles"]
                        for j in range(4):
                            (gz, r_gz), (yg, r_yg), (sq, r_sq) = GZ[j % 2], YG[j], SQ[j % 2]
                            zt, r_zt = tiles["z%d" % j]
                            P.op("act", lambda e, gz=gz, zt=zt: e.activation(out=gz[:], in_=zt[:], func=AF.Silu), reads=[r_zt], writes=[r_gz])
                            P.op("pool", lambda e, yg=yg, gz=gz, j=j: e.tensor_tensor(out=yg[:], in0=yf[:, j, :], in1=gz[:], op=ALU.mult),
                                 reads=[r_yf, r_gz], writes=[r_yg])
                            P.op("act", lambda e, sq=sq, yg=yg: e.activation(out=sq[:], in_=yg[:], func=AF.Square), reads=[r_yg], writes=[r_sq])
                            P.op("pe", lambda e, sq=sq, j=j: e.matmul(ps7[:], lhsT=self.ones[:], rhs=sq[:], start=(j == 0), stop=(j == 3)),
                                 reads=[r_sq, self.r_const], writes=[r_ps7])
                        P.op("act", lambda e: e.activation(out=RS[:], in_=ps7[:], func=AF.Sqrt, bias=EPS, scale=1.0 / 512), reads=[r_ps7], writes=[r_RS])
                        P.op("dve", lambda e: e.reciprocal(out=RS[:], in_=RS[:]), reads=[r_RS], writes=[r_RS])
                        for j in range(4):
                            (o_, r_o), (yg, r_yg) = OUTS[j % 2], YG[j]
                            col = g * 4 + j
                            P.op("dve", lambda e, o_=o_, yg=yg, col=col: e.scalar_tensor_tensor(
                                out=o_[:], in0=yg[:], scalar=NG[:, col:col + 1], in1=RS[:], op0=ALU.mult, op1=ALU.mult),
                                reads=[r_yg, r_RS, r_sc], writes=[r_o])
                            r0 = 1536 + g * 512 + j * 128
                            P.dma("sp", lambda e, o_=o_, r0=r0: e.dma_start(out=YT[r0:r0 + 128, sbi * SB_:(sbi + 1) * SB_], in_=o_[:]), reads=[r_o])

                    for it in range(NC + 2):
                        if 0 <= it - 2 < NC:
                            back2(it - 2)
                        if it < NC:
                            front(it)
                        if 0 <= it - 1 < NC:
                            back(it - 1)
                    P.flush()


    def emit_merge(self, l, PT, YT, MT):
        nc, P, L = self.nc, self.P, self.L
        if l == 0:
            self.mergew = self.din("mergew", [self.depth, 32, 128, 44 * 128])
            self.gateb = self.din("gateb", [self.depth, 128, 96])
        KB = (12, 16, 4)
        YOFF = (0, 12, 28)
        WOFF = (12, 24, 40)
        for tt in range(L // TT):
            t0 = tt * TT
            with contextlib.ExitStack() as st:
                yT = self.sb(st, "myT", [128, 32, TT], BF16)
                gT = self.sb(st, "mgT", [128, 4, TT], BF16)
                gb = self.sb(st, "mgb", [128, 96], F32)
                r_y = [Res() for _ in range(32)]
                r_g = [Res() for _ in range(4)]
                r_gb = Res()
                P.dma("sp", lambda e: e.dma_start(out=gb[:], in_=self.gateb[l]), writes=[r_gb])
                for c in range(32):
                    P.dma("pool", lambda e, c=c: e.dma_start(out=yT[:, c, :], in_=YT[c * 128:(c + 1) * 128, t0:t0 + TT]), writes=[r_y[c]])
                for c in range(4):
                    P.dma("pool", lambda e, c=c: e.dma_start(out=gT[:, c, :], in_=PT[GL0 + c * 128:GL0 + (c + 1) * 128, t0:t0 + TT]), writes=[r_g[c]])
                ring = [(self.sb(st, "mw%d" % i, [128, 44 * 128], BF16), Res()) for i in range(2)]
                sig = [(self.sb(st, "msg%d" % i, [128, 512], F32), Res()) for i in range(2)]
                acc = [(self.sb(st, "mac%d" % i, [128, 512], F32), Res()) for i in range(2)]
                mo = [(self.sb(st, "mo%d" % i, [128, 512], BF16), Res()) for i in range(2)]
                k = 0
                kk = 0
                for n in range(32):
                    wt, r_w = ring[n % 2]
                    P.dma("pool", lambda e, wt=wt, n=n: e.dma_start(out=wt[:], in_=self.mergew[l, n]), writes=[r_w])
                    for tb in range(TT // 512):
                        (a_, r_a), (o_, r_o) = acc[kk % 2], mo[kk % 2]
                        kk += 1
                        for br in range(3):
                            psG, r_psG = self.ps[(k % 4) * 2]
                            psY, r_psY = self.ps[(k % 4) * 2 + 1]
                            sg, r_sg = sig[k % 2]
                            k += 1
                            for kc in range(4):
                                P.op("pe", lambda e, psG=psG, wt=wt, br=br, kc=kc, tb=tb: e.matmul(
                                    psG[:], lhsT=wt[:, (br * 4 + kc) * 128:(br * 4 + kc + 1) * 128], rhs=gT[:, kc, tb * 512:(tb + 1) * 512],
                                    start=(kc == 0), stop=(kc == 3)), reads=[r_w, r_g[kc]], writes=[r_psG])
                            for kc in range(KB[br]):
                                P.op("pe", lambda e, psY=psY, wt=wt, br=br, kc=kc, tb=tb: e.matmul(
                                    psY[:], lhsT=wt[:, (WOFF[br] + kc) * 128:(WOFF[br] + kc + 1) * 128], rhs=yT[:, YOFF[br] + kc, tb * 512:(tb + 1) * 512],
                                    start=(kc == 0), stop=(kc == KB[br] - 1)), reads=[r_w, r_y[YOFF[br] + kc]], writes=[r_psY])
                            col = br * 32 + n
                            P.op("act", lambda e, sg=sg, psG=psG, col=col: e.activation(out=sg[:], in_=psG[:], func=AF.Sigmoid, bias=gb[:, col:col + 1], scale=1.0),
                                 reads=[r_psG, r_gb], writes=[r_sg])
                            if br == 0:
                                P.op("dve", lambda e, a_=a_, sg=sg, psY=psY: e.tensor_tensor(out=a_[:], in0=psY[:], in1=sg[:], op=ALU.mult),
                                     reads=[r_psY, r_sg], writes=[r_a])
                            else:
                                P.op("dve", lambda e, sg=sg, psY=psY: e.tensor_tensor(out=sg[:], in0=psY[:], in1=sg[:], op=ALU.mult),
                                     reads=[r_psY, r_sg], writes=[r_sg])
                                dst, r_dst = (a_, r_a) if br == 1 else (o_, r_o)
                                P.op("pool", lambda e, a_=a_, sg=sg, dst=dst: e.tensor_tensor(out=dst[:], in0=a_[:], in1=sg[:], op=ALU.add),
                                     reads=[r_sg, r_a], writes=[r_dst])
                        P.dma("sp", lambda e, o_=o_, n=n, c0=t0 + tb * 512: e.dma_start(out=MT[n * 128:(n + 1) * 128, c0:c0 + 512], in_=o_[:]), reads=[r_o])
                P.flush()

    def emit_res_linear(self, l, HT, KC, T, wsl, xT, gcol0, cast):
        nc, P, L = self.nc, self.P, self.L
        for tt in range(L // T):
            t0 = tt * T
            with contextlib.ExitStack() as st:
                hT = self.sb(st, "rhT", [128, KC, T], BF16)
                r_h = [Res() for _ in range(KC)]
                for c in range(KC):
                    P.dma("pool" if cast else "sp", lambda e, c=c: e.dma_start(out=hT[:, c, :], in_=HT[c * 128:(c + 1) * 128, t0:t0 + T]), writes=[r_h[c]])
                xo = [(self.sb(st, "rxo%d" % i, [128, 512], F32), Res()) for i in range(4)]
                xn = [(self.sb(st, "rxn%d" % i, [128, 512], F32), Res()) for i in range(4)]
                k = [0]

                def epi(n, tb, pst, r_ps, width):
                    (o_, r_o), (n_, r_n) = xo[k[0] % 4], xn[k[0] % 4]
                    k[0] += 1
                    c0 = t0 + tb * 512
                    P.dma("sp", lambda e: e.dma_start(out=o_[:], in_=xT[n * 128:(n + 1) * 128, c0:c0 + 512]), writes=[r_o])
                    P.op("dve", lambda e: e.scalar_tensor_tensor(out=n_[:], in0=pst[:], scalar=self.modL[:, gcol0 + n:gcol0 + n + 1], in1=o_[:],
                                                                op0=ALU.mult, op1=ALU.add), reads=[r_ps, r_o, self.r_modL], writes=[r_n])
                    P.dma("sp", lambda e: e.dma_start(out=xT[n * 128:(n + 1) * 128, c0:c0 + 512], in_=n_[:]), reads=[r_n])
                self.emit_linear(st, hT, r_h, KC, T, wsl, [128] * 32, epi, nring=(3 if KC <= 32 else 2), tag="rw")
                P.flush()

    def emit_ffn(self, l, xT, UT, GT, parts=("up", "conv", "down")):
        nc, P, L = self.nc, self.P, self.L
        if l == 0:
            self.mlp_up = self.din("mlp_up", [self.depth, 128, 128, D])
            self.mlp_down = self.din("mlp_down", [self.depth, 32, 128, MLP_H])
            self.mconv = self.din("mconv", [self.depth, 128, 128, 4])
        for tt in (range(L // TT) if "up" in parts else ()):
            t0 = tt * TT
            with contextlib.ExitStack() as st:
                hT = self.sb(st, "fhT", [128, KCD, TT], BF16)
                r_h = [Res() for _ in range(KCD)]
                with contextlib.ExitStack() as st2:
                    self.emit_norm(st2, xT, None, t0, TT, hT, r_h, 32, 96)
                    P.flush()
                stg = [(self.sb(st, "fstg%d" % i, [128, 512], F32), Res()) for i in range(4)]
                k = [0]

                def epi(n, tb, pst, r_ps, width):
                    s_, r_s = stg[k[0] % 4]
                    k[0] += 1
                    P.op("act", lambda e: e.activation(out=s_[:], in_=pst[:], func=AF.Copy), reads=[r_ps], writes=[r_s])
                    P.dma("sp", lambda e: e.dma_start(out=UT[n * 128:(n + 1) * 128, t0 + tb * 512:t0 + (tb + 1) * 512], in_=s_[:]), reads=[r_s])
                self.emit_linear(st, hT, r_h, KCD, TT, self.mlp_up[l], [128] * 128, epi, tag="fw")
                P.flush()
        TS = min(L, 4096)
        with contextlib.ExitStack() as st:
            cw = self.sb(st, "fcw", [128, 128, 4], F32)
            r_cw = Res()
            P.dma("sp", lambda e: e.dma_start(out=cw[:], in_=self.mconv[l]), writes=[r_cw])
            Uu = [(self.sb(st, "fU%d" % i, [128, TS + 2], F32), Res()) for i in range(2)]
            Uv = [(self.sb(st, "fV%d" % i, [128, TS + 2], F32), Res()) for i in range(2)]
            Au = [(self.sb(st, "fAu%d" % i, [128, TS], F32), Res()) for i in range(2)]
            Av = [(self.sb(st, "fAv%d" % i, [128, TS], F32), Res()) for i in range(2)]
            Go = [(self.sb(st, "fG%d" % i, [128, TS], BF16), Res()) for i in range(2)]
            k = 0
            for j in (range(64) if "conv" in parts else ()):
                for t0 in range(0, L, TS):
                    i2 = k % 2
                    k += 1
                    lo, hi = max(0, t0 - 1), min(L, t0 + TS + 1)
                    for (u, r_u), (a, r_a), ch in ((Uu[i2], Au[i2], j), (Uv[i2], Av[i2], 64 + j)):
                        if lo > t0 - 1:
                            P.op("pool", lambda e, u=u: e.memset(u[:, 0:1], 0.0), writes=[r_u])
                        if hi < t0 + TS + 1:
                            P.op("pool", lambda e, u=u: e.memset(u[:, TS + 1:TS + 2], 0.0), writes=[r_u])
                        P.dma("sp", lambda e, u=u, ch=ch, lo=lo, hi=hi, t0=t0: e.dma_start(
                            out=u[:, lo - (t0 - 1):hi - (t0 - 1)], in_=UT[ch * 128:(ch + 1) * 128, lo:hi]), writes=[r_u])
                        P.op("act", lambda e, u=u, a=a, ch=ch: e.activation(out=a[:], in_=u[:, 0:TS], func=AF.Identity,
                                                                           bias=cw[:, ch, 3:4], scale=cw[:, ch, 0:1]), reads=[r_u, r_cw], writes=[r_a])
                        for tap in (1, 2):
                            P.op("dve", lambda e, u=u, a=a, ch=ch, tap=tap: e.scalar_tensor_tensor(
                                out=a[:], in0=u[:, tap:tap + TS], scalar=cw[:, ch, tap:tap + 1], in1=a[:], op0=ALU.mult, op1=ALU.add),
                                reads=[r_u, r_cw, r_a], writes=[r_a])
                    (au, r_au), (av, r_av), (go, r_go) = Au[i2], Av[i2], Go[i2]
                    P.op("act", lambda e, au=au: e.activation(out=au[:], in_=au[:], func=AF.Silu), reads=[r_au], writes=[r_au])
                    P.op("pool", lambda e, au=au, av=av, go=go: e.tensor_tensor(out=go[:], in0=au[:], in1=av[:], op=ALU.mult), reads=[r_au, r_av], writes=[r_go])
                    P.dma("sp", lambda e, go=go, j=j, t0=t0: e.dma_start(out=GT[j * 128:(j + 1) * 128, t0:t0 + TS], in_=go[:]), reads=[r_go])
            P.flush()
        if "down" in parts:
            self.emit_res_linear(l, GT, 64, 1024, self.mlp_down[l], xT, 160, cast=False)

    def emit_model(self, xin, xT):
        nc, P, L = self.nc, self.P, self.L
        for c in range(KCD):
            P.dma("sp", lambda e, c=c: e.dma_start(out=xT[c * 128:(c + 1) * 128, :], in_=xin[c * 128:(c + 1) * 128, :]))
        self.emit_adaln()
        PT = RowSplit(self.nc, "PT", [0, XBC0, QKV0, N_IN], L, F32)
        YT = self.nc.dram_tensor("YT", [D, L], F32).ap()
        MT = self.nc.dram_tensor("MT", [D, L], BF16).ap()
        UT = RowSplit(self.nc, "UT", [0, 4096, 8192, 12288, 16384], L, F32)
        GT = self.nc.dram_tensor("GT", [MLP_H, L], BF16).ap()
        for l in range(self.depth):
            self.emit_layer_mod(l)
            self.emit_inproj(l, xT, None, PT, None)
            self.emit_pool(l, PT, YT)
            self.emit_ssd(l, PT, YT)
            self.emit_attn(l, PT, YT)
            self.emit_merge(l, PT, YT, MT)
            if l == 0:
                self.w_out = self.din("w_out", [self.depth, 32, 128, D])
            self.emit_res_linear(l, MT, KCD, TT, self.w_out[l], xT, 64, cast=False)
            self.emit_ffn(l, xT, UT, GT)
        P.flush(final=True)


def host_pool(inp, L, depth):
    pos = np.arange(L)
    ic = []
    for win in POOL_WINDOWS:
        lo = np.clip(pos - win // 2, 0, L)
        hi = np.clip(pos + win - win // 2, 0, L)
        ic.append(np.broadcast_to((1.0 / (hi - lo).astype(np.float32))[None, :], (128, L)))
    return {
        "pool_w": np.ascontiguousarray(inp["pool_w"][:depth]),
        "poolsc": np.stack([cols(inp["pool_scale"][l]) for l in range(depth)]),
        "invcnt": np.ascontiguousarray(np.stack(ic)).astype(np.float32),
    }


def _t5_buckets(rel):
    half, max_exact = 16, 8
    n = np.abs(rel)
    large = max_exact + (np.log(np.maximum(n, max_exact) / max_exact) / np.log(1024 / max_exact) * (half - max_exact)).astype(np.int32)
    large = np.minimum(large, half - 1)
    return (rel > 0).astype(np.int32) * half + np.where(n < max_exact, n, large).astype(np.int32)


def host_attn(inp, L, depth):
    q = np.arange(128)[:, None]
    j = np.arange(256)[None, :]
    delta = j - 64 - q
    inwin = np.abs(delta) <= 64
    out = np.empty((12, 128, 4, 256), np.float32)
    tab = inp["t5_table"]
    for gi, dil in enumerate((1, 4, 16)):
        bk = _t5_buckets(delta * dil)
        for hh in range(4):
            vals = tab[bk, gi * 4 + hh].astype(np.float32)
            for var in range(4):
                ok = inwin.copy()
                if var & 1:
                    ok &= (j >= 64)
                if var & 2:
                    ok &= (j < 192)
                out[gi * 4 + hh, :, var, :] = np.where(ok, vals, np.float32(-1e30))
    return {
        "bmask": out,
        "qkn": np.stack([np.stack([inp["q_norm"][l], inp["k_norm"][l]], axis=1) for l in range(depth)]).astype(np.float32),
    }


def host_ssd(inp, L, depth):
    k = np.arange(128)[:, None]
    t = np.arange(128)[None, :]
    sc = np.stack([(k <= t).astype(np.float32), (k >= t).astype(np.float32),
                   np.where(k <= t, 0.0, -30000.0).astype(np.float32), np.where(k >= t, 0.0, -30000.0).astype(np.float32)], axis=1)
    ssdv = np.empty((depth, 4, 16, 2), np.float32)
    convw = np.empty((depth, 128, 24, 6), np.float32)
    dskip = np.empty((depth, 4, 128, 8), np.float32)
    for l in range(depth):
        for g in range(4):
            for d in range(2):
                ssdv[l, g, d * 8:(d + 1) * 8, 0] = inp["ssd_dt_bias"][l, d, g * 8:(g + 1) * 8]
                ssdv[l, g, d * 8:(d + 1) * 8, 1] = inp["ssd_a_log"][l, d, g * 8:(g + 1) * 8]
            dskip[l, g] = np.broadcast_to(inp["ssd_d"][l, g * 8:(g + 1) * 8][None, :], (128, 8))
        for tap in range(5):
            convw[l, :, :, tap] = cols(inp["ssd_conv_w"][l, tap])
        convw[l, :, :, 5] = cols(inp["ssd_conv_b"][l])
    return {"ssdc": np.ascontiguousarray(sc), "ssdv": ssdv, "convw": convw, "dskip": dskip,
            "ssdn": np.stack([cols(inp["ssd_norm"][l]) for l in range(depth)])}


def host_dense(inp, L, depth):
    out = {}
    w_in = inp["w_in"][:depth]
    wpad = np.concatenate([w_in[:, :, :DT0 + 64], np.zeros((depth, D, 64), np.float32), w_in[:, :, DT0 + 64:]], axis=2)
    out["w_in"] = np.stack([slabs(wpad[l], KCD) for l in range(depth)])
    mw = np.empty((depth, 32, 128, 44 * 128), np.float32)
    for l in range(depth):
        gu = inp["gate_up"][l]
        parts = [slabs(gu[:, br * D:(br + 1) * D], 4) for br in range(3)]
        parts += [slabs(inp["proj_a"][l], 12), slabs(inp["proj_b"][l], 16), slabs(inp["proj_c"][l], 4)]
        mw[l] = np.concatenate(parts, axis=2)
    out["mergew"] = mw
    out["gateb"] = np.stack([cols(inp["gate_b"][l]) for l in range(depth)])
    out["w_out"] = np.stack([slabs(inp["w_out"][l], KCD) for l in range(depth)])
    out["mlp_up"] = np.stack([slabs(inp["mlp_up"][l], KCD) for l in range(depth)])
    out["mlp_down"] = np.stack([slabs(inp["mlp_down"][l], 64) for l in range(depth)])
    mc = np.empty((depth, 128, 128, 4), np.float32)
    for l in range(depth):
        for tap in range(3):
            mc[l, :, :, tap] = cols(inp["mlp_conv_w"][l, tap])
        mc[l, :, :, 3] = cols(inp["mlp_conv_b"][l])
    out["mconv"] = mc
    return out


def host_common(inp, b, depth):
    return {
        "ident_in": np.eye(128, dtype=np.float32),
        "xin": np.ascontiguousarray(inp["x"][b].T),
        "ccol": cols(inp["c"][b]),
        "adaw": slabs(inp["ada_w"], KCD),
        "adab": cols(inp["ada_b"]),
        "adal": np.stack([cols(inp["ada_layer"][l]) for l in range(DEPTH)]),
        "gnorm": np.stack([np.concatenate([cols(inp["norm_mix"][l]), cols(inp["norm_mlp"][l])], axis=1) for l in range(DEPTH)]),
    }


_CACHE = {}


def build(L, depth):
    key = (L, depth)
    if key not in _CACHE:
        b = Builder(L, depth)
        b.setup()
        xin = b.din("xin", [D, L])
        xT = b.nc.dram_tensor("xT", [D, L], F32, kind="ExternalOutput").ap()
        b.emit_model(xin, xT)
        _CACHE[key] = b
    return _CACHE[key]


def run_model(inp, L, depth, trace=False):
    B = inp["x"].shape[0]
    b = build(L, depth)
    shared = {}
    shared.update(host_dense(inp, L, depth))
    shared.update(host_pool(inp, L, depth))
    shared.update(host_ssd(inp, L, depth))
    shared.update(host_attn(inp, L, depth))
    in_maps = []
    for bi in range(B):
        m = dict(shared)
        m.update(host_common(inp, bi, depth))
        in_maps.append(m)
    res = run_bass_kernel_spmd(b.nc, in_maps, core_ids=list(range(B)), trace=trace)
    out = np.stack([np.ascontiguousarray(res.results[bi]["xT"].T) for bi in range(B)])
    return out, res


def kernel(**inputs):
    inp = {k: np.asarray(v) for k, v in inputs.items()}
    out, _ = run_model(inp, SEQ, DEPTH)
    return out.astype(np.float32)
```
